# Optimizing a Trainium2 kernel written in Bass

```python
import math
import jax, jax.numpy as jnp
from jax import lax
import numpy as np

D_MODEL = 1024
BATCH = 8
SEQ = 2048
DEPTH = 4
DEC_BATCH = 16
DEC_SEQ = 4096
PAST_LEN = 128

D_FF = 2816
CONV_K = 4
CONV_PAD = (2, 1)
CHUNK = 128
LRU_W = 512
LRU_BLOCKS = 8
LRU_BLOCK_W = LRU_W // LRU_BLOCKS
LRU_C = 8.0
SSD_HEADS = 8
SSD_HEAD_DIM = 64
SSD_W = SSD_HEADS * SSD_HEAD_DIM
SSD_GROUPS = 2
SSD_STATE = 64
SSD_CONV_W = SSD_W + 2 * SSD_GROUPS * SSD_STATE
RET_HEADS = 4
RET_QK = 64
RET_V = 128
RET_W = RET_HEADS * RET_V
ROPE_BASE = 10000.0
MLSTM_HEADS = 4
MLSTM_QK = 64
MLSTM_V = 128
MLSTM_W = MLSTM_HEADS * MLSTM_V
N_BRANCH = 4
BRANCH_W = 512
DN_ALPHA = (2.0 * DEPTH) ** 0.25
DN_BETA = (8.0 * DEPTH) ** -0.25
NORM_EPS = 1e-5
IN_SPLITS = (LRU_W, LRU_W,
             SSD_W, SSD_CONV_W, 2 * SSD_HEADS,
             RET_HEADS * RET_QK, RET_HEADS * RET_QK, RET_W, RET_W,
             2 * MLSTM_HEADS * MLSTM_QK, MLSTM_W, MLSTM_W, 4 * MLSTM_HEADS,
             N_BRANCH * D_MODEL)
IN_COLS = sum(IN_SPLITS)

kernel_name = 'hybrid_bidir_rglru_ssd_retention_mlstm_trunk'


def _split_points():
    pts, acc = [], 0
    for s in IN_SPLITS[:-1]:
        acc += s
        pts.append(acc)
    return pts


def _flip(a):
    return jnp.flip(a, axis=1)


def _layer_norm(x, g, b):
    xf = x.astype(jnp.float32)
    mu = jnp.mean(xf, axis=-1, keepdims=True)
    var = jnp.mean(jnp.square(xf - mu), axis=-1, keepdims=True)
    return ((xf - mu) * lax.rsqrt(var + NORM_EPS) * g + b).astype(x.dtype)


def _head_norm(x, g):
    mu = jnp.mean(x, axis=-1, keepdims=True)
    var = jnp.mean(jnp.square(x - mu), axis=-1, keepdims=True)
    y = (x - mu) * lax.rsqrt(var + NORM_EPS)
    return y.reshape(x.shape[0], x.shape[1], -1) * g


def _rms_norm(x, g):
    return x * lax.rsqrt(jnp.mean(jnp.square(x), axis=-1, keepdims=True) + NORM_EPS) * g


def _dwconv(x, w, b):
    y = lax.conv_general_dilated(x, w[:, None, :], window_strides=(1,), padding=[CONV_PAD],
                                 dimension_numbers=('NWC', 'WIO', 'NWC'),
                                 feature_group_count=x.shape[-1])
    return y + b


def _swiglu(x, w_up, w_down):
    gate, up = jnp.split(x @ w_up, 2, axis=-1)
    return (jax.nn.silu(gate) * up) @ w_down


def _rope(x):
    seq, d = x.shape[1], x.shape[-1]
    inv = ROPE_BASE ** (-jnp.arange(0, d, 2, dtype=jnp.float32) / d)
    ang = jnp.arange(seq, dtype=jnp.float32)[:, None] * inv[None, :]
    cos = jnp.cos(ang)[None, :, None, :]
    sin = jnp.sin(ang)[None, :, None, :]
    x1, x2 = x[..., : d // 2], x[..., d // 2:]
    return jnp.concatenate([x1 * cos - x2 * sin, x1 * sin + x2 * cos], axis=-1)


def _lin_combine(left, right):
    a1, b1 = left
    a2, b2 = right
    return a1 * a2, a2 * b1 + b2


def _rglru_direction(xc, gate_w, gate_b, lam, reverse):
    bsz, seq, _ = xc.shape
    xb = xc.reshape(bsz, seq, LRU_BLOCKS, LRU_BLOCK_W)
    pre = jnp.einsum('blhi,ghij->gblhj', xb, gate_w).reshape(2, bsz, seq, LRU_W) + gate_b[:, None, None, :]
    r = jax.nn.sigmoid(pre[0])
    i = jax.nn.sigmoid(pre[1])
    log_a = -LRU_C * r * jax.nn.softplus(-lam)
    a = jnp.exp(log_a)
    b = jnp.sqrt(-jnp.expm1(2.0 * log_a)) * (i * xc)
    _, h = lax.associative_scan(_lin_combine, (a, b), reverse=reverse, axis=1)
    return h


def _chunked_decay_scan(q, k, v, log_a, include_diag):
    bsz, seq, nh, dk = q.shape
    dv = v.shape[-1]
    n = seq // CHUNK
    qc = q.reshape(bsz, n, CHUNK, nh, dk)
    kc = k.reshape(bsz, n, CHUNK, nh, dk)
    vc = v.reshape(bsz, n, CHUNK, nh, dv)
    cum = jnp.cumsum(log_a.reshape(bsz, n, CHUNK, nh), axis=2)
    mask = jnp.tril(jnp.ones((CHUNK, CHUNK), dtype=bool), k=0 if include_diag else -1)
    diff = cum[:, :, :, None, :] - cum[:, :, None, :, :]
    decay = jnp.exp(jnp.where(mask[None, None, :, :, None], diff, -jnp.inf))
    scores = jnp.einsum('bnihd,bnjhd->bnijh', qc, kc) * decay
    y_intra = jnp.einsum('bnijh,bnjhv->bnihv', scores, vc)
    total = cum[:, :, -1, :]
    w_state = jnp.exp(total[:, :, None, :] - cum)
    s_chunk = jnp.einsum('bnjh,bnjhd,bnjhv->bnhdv', w_state, kc, vc)

    def step(h, inp):
        s_n, tot_n = inp
        return jnp.exp(tot_n)[..., None, None] * h + s_n, h

    h0 = jnp.zeros((bsz, nh, dk, dv), jnp.float32)
    _, h_prev = lax.scan(step, h0, (jnp.moveaxis(s_chunk, 1, 0), jnp.moveaxis(total, 1, 0)))
    h_prev = jnp.moveaxis(h_prev, 0, 1)
    y_inter = jnp.einsum('bnihd,bnhdv->bnihv', qc * jnp.exp(cum)[..., None], h_prev)
    return (y_intra + y_inter).reshape(bsz, seq, nh, dv)


def _mlstm_chunked(q, k, v, log_i, log_f):
    bsz, seq, nh, dk = q.shape
    dv = v.shape[-1]
    n = seq // CHUNK
    qc = q.reshape(bsz, n, CHUNK, nh, dk)
    kc = k.reshape(bsz, n, CHUNK, nh, dk)
    vc = v.reshape(bsz, n, CHUNK, nh, dv)
    li = log_i.reshape(bsz, n, CHUNK, nh)
    b = jnp.cumsum(log_f.reshape(bsz, n, CHUNK, nh), axis=2)
    b_last = b[:, :, -1, :]
    w_state = b_last[:, :, None, :] - b + li
    m_loc = jnp.max(w_state, axis=2)
    e_state = jnp.exp(w_state - m_loc[:, :, None, :])
    s_chunk = jnp.einsum('bnjh,bnjhd,bnjhv->bnhdv', e_state, kc, vc)
    n_chunk = jnp.einsum('bnjh,bnjhd->bnhd', e_state, kc)

    def step(carry, inp):
        c_mat, n_vec, m = carry
        s_n, ns_n, bl_n, ml_n = inp
        m_new = jnp.maximum(bl_n + m, ml_n)
        dec = jnp.exp(bl_n + m - m_new)
        inj = jnp.exp(ml_n - m_new)
        c_new = dec[..., None, None] * c_mat + inj[..., None, None] * s_n
        n_new = dec[..., None] * n_vec + inj[..., None] * ns_n
        return (c_new, n_new, m_new), (c_mat, n_vec, m)

    init = (jnp.zeros((bsz, nh, dk, dv), jnp.float32), jnp.zeros((bsz, nh, dk), jnp.float32),
            jnp.zeros((bsz, nh), jnp.float32))
    xs = (jnp.moveaxis(s_chunk, 1, 0), jnp.moveaxis(n_chunk, 1, 0), jnp.moveaxis(b_last, 1, 0), jnp.moveaxis(m_loc, 1, 0))
    _, (c_prev, n_prev, m_prev) = lax.scan(step, init, xs)
    c_prev = jnp.moveaxis(c_prev, 0, 1)
    n_prev = jnp.moveaxis(n_prev, 0, 1)
    m_prev = jnp.moveaxis(m_prev, 0, 1)
    mask = jnp.tril(jnp.ones((CHUNK, CHUNK), dtype=bool))
    intra = jnp.where(mask[None, None, :, :, None],
                      b[:, :, :, None, :] - b[:, :, None, :, :] + li[:, :, None, :, :], -jnp.inf)
    inter = b + m_prev[:, :, None, :]
    m_row = jnp.maximum(inter, jnp.max(intra, axis=3))
    smat = jnp.einsum('bnihd,bnjhd->bnijh', qc, kc) * jnp.exp(intra - m_row[:, :, :, None, :])
    e_inter = jnp.exp(inter - m_row)
    num = jnp.einsum('bnijh,bnjhv->bnihv', smat, vc) + e_inter[..., None] * jnp.einsum('bnihd,bnhdv->bnihv', qc, c_prev)
    den = jnp.sum(smat, axis=3) + e_inter * jnp.einsum('bnihd,bnhd->bnih', qc, n_prev)
    h = num / jnp.maximum(jnp.abs(den), jnp.exp(-m_row))[..., None]
    return h.reshape(bsz, seq, nh, dv)


def _token_mixers(x, w_in, lru_conv_w, lru_conv_b, lru_gate_w, lru_gate_b, lru_lambda,
                  ssd_conv_w, ssd_conv_b, ssd_dt_bias, ssd_a_log, ssd_d, ssd_norm_w,
                  ret_norm_w, mlstm_conv_w, mlstm_conv_b, mlstm_gate_b, mlstm_norm_w,
                  w_branch, w_out):
    f32 = jnp.float32
    bsz, seq, _ = x.shape
    (xa, ga, z, xbc, dt_raw, rq, rk, rv, rg, mqk, mv, mo, mgate, mix_g) = jnp.split(x @ w_in, _split_points(), axis=-1)

    xa = _dwconv(xa, lru_conv_w, lru_conv_b).astype(f32)
    h_lru = (_rglru_direction(xa, lru_gate_w[0], lru_gate_b[0], lru_lambda[0], False)
             + _rglru_direction(xa, lru_gate_w[1], lru_gate_b[1], lru_lambda[1], True))
    y_a = jax.nn.gelu(ga.astype(f32)) * h_lru

    xbc = jax.nn.silu(_dwconv(xbc, ssd_conv_w, ssd_conv_b).astype(f32))
    xs, bm, cm = jnp.split(xbc, [SSD_W, SSD_W + SSD_GROUPS * SSD_STATE], axis=-1)
    xs = xs.reshape(bsz, seq, SSD_HEADS, SSD_HEAD_DIM)
    heads_per_group = SSD_HEADS // SSD_GROUPS
    bm = jnp.repeat(bm.reshape(bsz, seq, SSD_GROUPS, SSD_STATE), heads_per_group, axis=2)
    cm = jnp.repeat(cm.reshape(bsz, seq, SSD_GROUPS, SSD_STATE), heads_per_group, axis=2)
    dt = jax.nn.softplus(dt_raw.astype(f32).reshape(bsz, seq, 2, SSD_HEADS) + ssd_dt_bias)
    log_a = dt * -jnp.exp(ssd_a_log.astype(f32))
    y_fwd = _chunked_decay_scan(cm, bm * dt[:, :, 0, :, None], xs, log_a[:, :, 0], True)
    y_bwd = _flip(_chunked_decay_scan(_flip(cm), _flip(bm * dt[:, :, 1, :, None]), _flip(xs),
                                      _flip(log_a[:, :, 1]), True))
    y_b = (y_fwd + y_bwd + ssd_d[:, None] * xs).reshape(bsz, seq, SSD_W)
    y_b = _rms_norm(y_b * jax.nn.silu(z.astype(f32)), ssd_norm_w)

    rq = _rope(rq.astype(f32).reshape(bsz, seq, RET_HEADS, RET_QK)) * RET_QK ** -0.5
    rk = _rope(rk.astype(f32).reshape(bsz, seq, RET_HEADS, RET_QK))
    rv = rv.astype(f32).reshape(bsz, seq, RET_HEADS, RET_V)
    log_gamma = jnp.broadcast_to(jnp.log1p(-jnp.exp2(-5.0 - jnp.arange(RET_HEADS, dtype=f32))), (bsz, seq, RET_HEADS))
    o_ret = (_chunked_decay_scan(rq, rk, rv, log_gamma, True)
             + _flip(_chunked_decay_scan(_flip(rq), _flip(rk), _flip(rv), log_gamma, False)))
    y_c = jax.nn.silu(rg.astype(f32)) * _head_norm(o_ret, ret_norm_w)

    mqk = jax.nn.silu(_dwconv(mqk, mlstm_conv_w, mlstm_conv_b).astype(f32))
    mq, mk = jnp.split(mqk, 2, axis=-1)
    mq = mq.reshape(bsz, seq, MLSTM_HEADS, MLSTM_QK)
    mk = mk.reshape(bsz, seq, MLSTM_HEADS, MLSTM_QK) * MLSTM_QK ** -0.5
    mv = mv.astype(f32).reshape(bsz, seq, MLSTM_HEADS, MLSTM_V)
    gt = mgate.astype(f32).reshape(bsz, seq, 2, 2, MLSTM_HEADS) + mlstm_gate_b
    h_fwd = _mlstm_chunked(mq, mk, mv, gt[:, :, 0, 0], jax.nn.log_sigmoid(gt[:, :, 0, 1]))
    h_bwd = _flip(_mlstm_chunked(_flip(mq), _flip(mk), _flip(mv), _flip(gt[:, :, 1, 0]),
                                 _flip(jax.nn.log_sigmoid(gt[:, :, 1, 1]))))
    y_d = jax.nn.sigmoid(mo.astype(f32)) * _head_norm(h_fwd + h_bwd, mlstm_norm_w)

    gates = jax.nn.sigmoid(mix_g.astype(f32)).reshape(bsz, seq, N_BRANCH, D_MODEL)
    dt_x = x.dtype
    merged = (gates[:, :, 0] * (y_a.astype(dt_x) @ w_branch[0])
              + gates[:, :, 1] * (y_b.astype(dt_x) @ w_branch[1])
              + gates[:, :, 2] * (y_c.astype(dt_x) @ w_branch[2])
              + gates[:, :, 3] * (y_d.astype(dt_x) @ w_branch[3]))
    return merged.astype(dt_x) @ w_out


def _trunk(x, w_in, lru_conv_w, lru_conv_b, lru_gate_w, lru_gate_b, lru_lambda,
           ssd_conv_w, ssd_conv_b, ssd_dt_bias, ssd_a_log, ssd_d, ssd_norm_w,
           ret_norm_w, mlstm_conv_w, mlstm_conv_b, mlstm_gate_b, mlstm_norm_w,
           w_branch, w_out, w_ffn_in, w_ffn_out, ln_g, ln_b):
    for l in range(DEPTH):
        x = _layer_norm(DN_ALPHA * x + 0.5 * _swiglu(x, w_ffn_in[l, 0], w_ffn_out[l, 0]), ln_g[l, 0], ln_b[l, 0])
        mix = _token_mixers(x, w_in[l], lru_conv_w[l], lru_conv_b[l], lru_gate_w[l], lru_gate_b[l], lru_lambda[l],
                            ssd_conv_w[l], ssd_conv_b[l], ssd_dt_bias[l], ssd_a_log[l], ssd_d[l], ssd_norm_w[l],
                            ret_norm_w[l], mlstm_conv_w[l], mlstm_conv_b[l], mlstm_gate_b[l], mlstm_norm_w[l],
                            w_branch[l], w_out[l])
        x = _layer_norm(DN_ALPHA * x + mix, ln_g[l, 1], ln_b[l, 1])
        x = _layer_norm(DN_ALPHA * x + 0.5 * _swiglu(x, w_ffn_in[l, 1], w_ffn_out[l, 1]), ln_g[l, 2], ln_b[l, 2])
    return x


def setup_inputs(seed: int = 0) -> dict:
    key = jax.random.key(seed)
    ks = jax.random.split(key, 26)
    f32 = jnp.float32

    def nrm(k, shape, scale):
        return scale * jax.random.normal(k, shape, f32)

    D = D_MODEL
    u = jax.random.uniform(ks[6], (DEPTH, 2, LRU_W), f32, 0.9, 0.999)
    p = u ** (1.0 / LRU_C)
    dt0 = jnp.exp(jax.random.uniform(ks[10], (DEPTH, 2, SSD_HEADS), f32, math.log(1e-3), math.log(1e-1)))
    gate_i = nrm(ks[17], (DEPTH, 2, 1, MLSTM_HEADS), 0.1)
    gate_f = jnp.linspace(3.0, 6.0, MLSTM_HEADS, dtype=f32) + nrm(ks[18], (DEPTH, 2, 1, MLSTM_HEADS), 0.1)
    return {
        'x_prompt': nrm(ks[0], (BATCH, SEQ, D), 1.0),
        'x_sample': nrm(ks[1], (DEC_BATCH, DEC_SEQ, D), 1.0),
        'w_in': nrm(ks[2], (DEPTH, D, IN_COLS), D ** -0.5),
        'lru_conv_w': nrm(ks[3], (DEPTH, CONV_K, LRU_W), CONV_K ** -0.5),
        'lru_conv_b': nrm(ks[4], (DEPTH, LRU_W), 0.02),
        'lru_gate_w': nrm(ks[5], (DEPTH, 2, 2, LRU_BLOCKS, LRU_BLOCK_W, LRU_BLOCK_W), LRU_BLOCK_W ** -0.5),
        'lru_gate_b': nrm(ks[7], (DEPTH, 2, 2, LRU_W), 0.02),
        'lru_lambda': jnp.log(p) - jnp.log1p(-p),
        'ssd_conv_w': nrm(ks[8], (DEPTH, CONV_K, SSD_CONV_W), CONV_K ** -0.5),
        'ssd_conv_b': nrm(ks[9], (DEPTH, SSD_CONV_W), 0.02),
        'ssd_dt_bias': dt0 + jnp.log(-jnp.expm1(-dt0)),
        'ssd_a_log': jnp.log(jax.random.uniform(ks[11], (DEPTH, 2, SSD_HEADS), f32, 1.0, 16.0)),
        'ssd_d': 1.0 + nrm(ks[12], (DEPTH, SSD_HEADS), 0.02),
        'ssd_norm_w': 1.0 + nrm(ks[13], (DEPTH, SSD_W), 0.02),
        'ret_norm_w': 1.0 + nrm(ks[14], (DEPTH, RET_W), 0.02),
        'mlstm_conv_w': nrm(ks[15], (DEPTH, CONV_K, 2 * MLSTM_HEADS * MLSTM_QK), CONV_K ** -0.5),
        'mlstm_conv_b': nrm(ks[16], (DEPTH, 2 * MLSTM_HEADS * MLSTM_QK), 0.02),
        'mlstm_gate_b': jnp.concatenate([gate_i, gate_f], axis=2),
        'mlstm_norm_w': 1.0 + nrm(ks[19], (DEPTH, MLSTM_W), 0.02),
        'w_branch': nrm(ks[20], (DEPTH, N_BRANCH, BRANCH_W, D), DN_BETA * BRANCH_W ** -0.5),
        'w_out': nrm(ks[21], (DEPTH, D, D), DN_BETA * D ** -0.5),
        'w_ffn_in': nrm(ks[22], (DEPTH, 2, D, 2 * D_FF), D ** -0.5),
        'w_ffn_out': nrm(ks[23], (DEPTH, 2, D_FF, D), DN_BETA * D_FF ** -0.5),
        'ln_g': 1.0 + nrm(ks[24], (DEPTH, 3, D), 0.02),
        'ln_b': nrm(ks[25], (DEPTH, 3, D), 0.02),
    }


def reference(x_prompt, x_sample, w_in, lru_conv_w, lru_conv_b, lru_gate_w, lru_gate_b, lru_lambda,
              ssd_conv_w, ssd_conv_b, ssd_dt_bias, ssd_a_log, ssd_d, ssd_norm_w,
              ret_norm_w, mlstm_conv_w, mlstm_conv_b, mlstm_gate_b, mlstm_norm_w,
              w_branch, w_out, w_ffn_in, w_ffn_out, ln_g, ln_b):
    y_prompt = _trunk(x_prompt, w_in, lru_conv_w, lru_conv_b, lru_gate_w, lru_gate_b, lru_lambda,
                      ssd_conv_w, ssd_conv_b, ssd_dt_bias, ssd_a_log, ssd_d, ssd_norm_w,
                      ret_norm_w, mlstm_conv_w, mlstm_conv_b, mlstm_gate_b, mlstm_norm_w,
                      w_branch, w_out, w_ffn_in, w_ffn_out, ln_g, ln_b)
    y_sample = _trunk(x_sample, w_in, lru_conv_w, lru_conv_b, lru_gate_w, lru_gate_b, lru_lambda,
                      ssd_conv_w, ssd_conv_b, ssd_dt_bias, ssd_a_log, ssd_d, ssd_norm_w,
                      ret_norm_w, mlstm_conv_w, mlstm_conv_b, mlstm_gate_b, mlstm_norm_w,
                      w_branch, w_out, w_ffn_in, w_ffn_out, ln_g, ln_b)
    return (y_prompt, y_sample)
```

```python
import contextlib
import math
import numpy as np
import concourse.bass as bass
import concourse.mybir as mybir
from concourse.bass_utils import run_bass_kernel_spmd

F32 = mybir.dt.float32
BF16 = mybir.dt.bfloat16
AF = mybir.ActivationFunctionType
ALU = mybir.AluOpType
AX = mybir.AxisListType

D = 1024
DFF = 2816
DEPTH = 4
NKC = 8
TT = 512
CH = 128
DN_ALPHA = (2.0 * DEPTH) ** 0.25
EPS = 1e-5
NCORES = 8

GE = 4096
OFF = {}
_o = 0
for f in range(2):
    OFF[("up", f)] = _o; _o += 11 * GE
    OFF[("dn", f)] = _o; _o += 8 * 2816
OFF["inF"] = _o; _o += 15 * GE
OFF["inT"] = _o; _o += 5 * GE
OFF["br"] = _o; _o += 4 * GE
OFF["out"] = _o; _o += 2 * GE
OFF["lru"] = _o; _o += 2048
WPP = _o
assert WPP % 2048 == 0

_sp = [512, 512, 512, 768, 16, 256, 256, 512, 512, 512, 512, 512, 16, 4096]
_names = ["xa", "ga", "z", "xbc", "dt", "rq", "rk", "rv", "rg", "mqk", "mv", "mo", "mg", "mix"]
COL = {}
_a = 0
for n_, s_ in zip(_names, _sp):
    COL[n_] = _a; _a += s_
assert _a == 9504


def _rot_cols(base):
    idx = []
    for h in range(4):
        for d in range(64):
            idx.append(base + h * 64 + (d + 32) % 64)
    return idx


def _inF_colmap():
    groups = []
    groups.append(list(range(COL["xa"], COL["xa"] + 512)))
    groups.append(list(range(COL["ga"], COL["ga"] + 512)))
    groups.append(list(range(COL["xbc"], COL["xbc"] + 512)))
    g3 = list(range(COL["xbc"] + 512, COL["xbc"] + 768))
    ga = [-1] * 128
    gb = [-1] * 128
    for h in range(8):
        ga[h] = COL["dt"] + h
        ga[32 + h] = COL["dt"] + 8 + h
    for h in range(4):
        ga[64 + h] = COL["mg"] + 0 + h
        ga[96 + h] = COL["mg"] + 8 + h
        gb[64 + h] = COL["mg"] + 4 + h
        gb[96 + h] = COL["mg"] + 12 + h
    groups.append(g3 + ga + gb)
    groups.append(list(range(COL["mqk"], COL["mqk"] + 512)))
    groups.append(list(range(COL["rq"], COL["rq"] + 256)) + _rot_cols(COL["rq"]))
    groups.append(list(range(COL["rk"], COL["rk"] + 256)) + _rot_cols(COL["rk"]))
    for g in range(8):
        groups.append(list(range(COL["mix"] + g * 512, COL["mix"] + (g + 1) * 512)))
    return groups


def _inT_colmap():
    return [list(range(COL[n], COL[n] + 512)) for n in ("z", "rv", "rg", "mv", "mo")]


def _grp_kc(w, cols):
    cols = np.asarray(cols)
    safe = np.where(cols < 0, 0, cols)
    g = w[:, safe]
    if (cols < 0).any():
        g = g.copy()
        g[:, cols < 0] = 0.0
    k = w.shape[0] // 128
    return g.reshape(k, 128, len(cols)).transpose(1, 0, 2).reshape(128, k * len(cols))


def pack_layer_weights(inp, l):
    out = np.zeros((128, WPP), np.float32)
    for f in range(2):
        wu = inp["w_ffn_in"][l, f]
        o = OFF[("up", f)]
        for g in range(11):
            cols = list(range(256 * g, 256 * g + 256)) + list(range(DFF + 256 * g, DFF + 256 * g + 256))
            out[:, o + g * GE:o + (g + 1) * GE] = _grp_kc(wu, cols)
        wd = inp["w_ffn_out"][l, f]
        o = OFF[("dn", f)]
        for dc in range(8):
            out[:, o + dc * 2816:o + (dc + 1) * 2816] = _grp_kc(wd, list(range(dc * 128, dc * 128 + 128)))
    wi = inp["w_in"][l]
    for g, cols in enumerate(_inF_colmap()):
        out[:, OFF["inF"] + g * GE:OFF["inF"] + (g + 1) * GE] = _grp_kc(wi, cols)
    for g, cols in enumerate(_inT_colmap()):
        out[:, OFF["inT"] + g * GE:OFF["inT"] + (g + 1) * GE] = _grp_kc(wi, cols)
    for i in range(4):
        out[:, OFF["br"] + i * GE:OFF["br"] + (i + 1) * GE] = _grp_kc(inp["w_branch"][l, i], list(range(1024)))
    for g in range(2):
        out[:, OFF["out"] + g * GE:OFF["out"] + (g + 1) * GE] = _grp_kc(inp["w_out"][l], list(range(g * 512, g * 512 + 512)))
    gw = inp["lru_gate_w"][l]
    blk = np.zeros((128, 16, 128), np.float32)
    for d_ in range(2):
        for g_ in range(2):
            for ct in range(4):
                m = (d_ * 2 + g_) * 4 + ct
                for hl in range(2):
                    blk[hl * 64:(hl + 1) * 64, m, hl * 64:(hl + 1) * 64] = gw[d_, g_, ct * 2 + hl]
    out[:, OFF["lru"]:OFF["lru"] + 2048] = blk.reshape(128, 2048)
    return out


PV = {}
_p = 0
for nm, n in [("ln_g", 24), ("ln_b", 24), ("xa_w", 16), ("xa_b", 4), ("xbc_w", 24), ("xbc_b", 6),
              ("mqk_w", 16), ("mqk_b", 4), ("lru_gb", 16), ("lru_lam", 8), ("rbA", 1), ("rbB", 1), ("ralog", 1)]:
    PV[nm] = _p; _p += n
NPV = _p


def pack_pvec(inp, l):
    pv = np.zeros((128, NPV), np.float32)
    for i in range(3):
        pv[:, PV["ln_g"] + i * 8:PV["ln_g"] + i * 8 + 8] = inp["ln_g"][l, i].reshape(8, 128).T
        pv[:, PV["ln_b"] + i * 8:PV["ln_b"] + i * 8 + 8] = inp["ln_b"][l, i].reshape(8, 128).T
    for nm, key, nct in (("xa", "lru_conv", 4), ("xbc", "ssd_conv", 6), ("mqk", "mlstm_conv", 4)):
        w = inp[key + "_w"][l]
        b = inp[key + "_b"][l]
        for ct in range(nct):
            for k in range(4):
                pv[:, PV[nm + "_w"] + ct * 4 + k] = w[k, ct * 128:(ct + 1) * 128]
            pv[:, PV[nm + "_b"] + ct] = b[ct * 128:(ct + 1) * 128]
    for d_ in range(2):
        for g_ in range(2):
            for ct in range(4):
                pv[:, PV["lru_gb"] + (d_ * 2 + g_) * 4 + ct] = inp["lru_gate_b"][l, d_, g_, ct * 128:(ct + 1) * 128]
        for ct in range(4):
            pv[:, PV["lru_lam"] + d_ * 4 + ct] = inp["lru_lambda"][l, d_, ct * 128:(ct + 1) * 128]
    pv[0:8, PV["rbA"]] = inp["ssd_dt_bias"][l, 0]
    pv[32:40, PV["rbA"]] = inp["ssd_dt_bias"][l, 1]
    pv[64:68, PV["rbA"]] = inp["mlstm_gate_b"][l, 0, 0]
    pv[96:100, PV["rbA"]] = inp["mlstm_gate_b"][l, 1, 0]
    pv[64:68, PV["rbB"]] = inp["mlstm_gate_b"][l, 0, 1]
    pv[96:100, PV["rbB"]] = inp["mlstm_gate_b"][l, 1, 1]
    pv[0:8, PV["ralog"]] = inp["ssd_a_log"][l, 0]
    pv[32:40, PV["ralog"]] = inp["ssd_a_log"][l, 1]
    return pv


def pack_bvec(inp, l):
    bv = np.zeros((4, 512), np.float32)
    bv[0] = inp["ssd_norm_w"][l]
    bv[1] = inp["ret_norm_w"][l]
    bv[2] = inp["mlstm_norm_w"][l]
    bv[3] = np.repeat(inp["ssd_d"][l], 64)
    return bv


CST = {}
_c = 0
for nm, n in [("ident", 128), ("triU", 128), ("triL", 128), ("negF", 128), ("negB", 128), ("sel", 16 * 128),
              ("retW", 4 * 128), ("retS", 20), ("ones", 512), ("selT", 24), ("retG", 256)]:
    CST[nm] = _c; _c += n
NCST = _c
NEGV = -30000.0


def build_consts(tmax):
    c = np.zeros((128, NCST), np.float32)
    j = np.arange(128)[:, None]
    i = np.arange(128)[None, :]
    c[:, CST["ident"]:CST["ident"] + 128] = (j == i)
    c[:, CST["triU"]:CST["triU"] + 128] = (j <= i)
    c[:, CST["triL"]:CST["triL"] + 128] = (j >= i)
    c[:, CST["negF"]:CST["negF"] + 128] = np.where(j <= i, 0.0, NEGV)
    c[:, CST["negB"]:CST["negB"] + 128] = np.where(j >= i, 0.0, NEGV)
    for m in range(16):
        r = (m % 8) + (32 if m >= 8 else 0)
        c[r, CST["sel"] + m * 128:CST["sel"] + (m + 1) * 128] = 1.0
    lg = np.log1p(-np.exp2(-5.0 - np.arange(4, dtype=np.float64)))
    for h in range(4):
        c[:, CST["retW"] + h * 128:CST["retW"] + (h + 1) * 128] = 0.125 * np.exp(lg[h] * np.abs(i - j))
        t = np.arange(128, dtype=np.float64)
        c[:, CST["retS"] + 0 + h] = 0.125 * np.exp(lg[h] * (t + 1))
        c[:, CST["retS"] + 4 + h] = 0.125 * np.exp(lg[h] * (128 - t))
        c[:, CST["retS"] + 8 + h] = np.exp(lg[h] * (127 - t))
        c[:, CST["retS"] + 12 + h] = np.exp(lg[h] * t)
        c[:, CST["retS"] + 16 + h] = np.exp(lg[h] * 128)
    c[:, CST["ones"]:CST["ones"] + 512] = 1.0
    for n in range(16):
        c[(n % 8) + (32 if n >= 8 else 0), CST["selT"] + n] = 1.0
    for n2 in range(8):
        c[64 + 32 * (n2 // 4) + (n2 % 4), CST["selT"] + 16 + n2] = 1.0
    for h in range(4):
        r0 = (h % 2) * 64
        c0 = CST["retG"] + (h // 2) * 128
        c[r0:r0 + 64, c0:c0 + 128] = np.exp(lg[h] * 128)
    p = np.arange(128)
    d = p % 64
    inv = 10000.0 ** (-(2.0 * (d % 32)) / 64.0)
    ang = inv[:, None].astype(np.float32) * np.arange(tmax, dtype=np.float32)[None, :]
    cos = np.cos(ang).astype(np.float32)
    sin = np.sin(ang).astype(np.float32)
    sin_signed = np.where((d < 32)[:, None], -sin, sin).astype(np.float32)
    return c, cos, sin_signed


class Ev:
    __slots__ = ("sem", "val", "eng", "small")

    def __init__(self, sem, val, eng, small=False):
        self.sem, self.val, self.eng, self.small = sem, val, eng, small


class Tl:
    def __init__(self, ap, name):
        self.ap, self.name = ap, name
        self.w = None
        self.r = {}

    def __getitem__(self, idx):
        return V(self.ap[idx], self)

    def v(self):
        return V(self.ap, self)

    @property
    def tl(self):
        return self

    def rearrange(self, pat, **kw):
        return self.v().rearrange(pat, **kw)

    def bitcast(self, dt):
        return self.v().bitcast(dt)

    def bcast(self, axis, n):
        return self.v().bcast(axis, n)


class V:
    def __init__(self, ap, tl):
        self.ap, self.tl = ap, tl

    def __getitem__(self, idx):
        return V(self.ap[idx], self.tl)

    def bitcast(self, dt):
        return V(self.ap.bitcast(dt), self.tl)

    def rearrange(self, pat, **kw):
        return V(self.ap.rearrange(pat, **kw), self.tl)

    def bcast(self, axis, n):
        ap = self.ap.unsqueeze(axis)
        shp = list(ap.shape)
        shp[axis] = n
        return V(ap.broadcast_to(shp), self.tl)


def _ap(x):
    return x.ap if isinstance(x, (V, Tl)) else x


def _tl(x):
    return x.tl if isinstance(x, V) else (x if isinstance(x, Tl) else None)


EPOCH = 30000
RELAX_SAME_ENGINE = False
NDSEM = 40


class Sched:
    def __init__(self, nc, stack):
        self.nc = nc
        self.stack = stack
        self.h = {"pe": nc.tensor, "act": nc.scalar, "dve": nc.vector, "pool": nc.gpsimd, "sp": nc.sync}
        self.cnt = {e: 0 for e in ("pe", "act", "dve", "pool")}
        self.sems = {e: [] for e in ("pe", "act", "dve", "pool")}
        self.waited = {e: {} for e in self.h}
        self.dsem = {q: [[stack.enter_context(nc.semaphore(f"dq{q}{i}")), 0] for i in range(n)]
                     for q, n in (("sp", NDSEM), ("pool", 16), ("act", 8), ("cvt", 8))}
        self.drr = {"sp": 0, "pool": 0, "act": 0, "cvt": 0}
        self.last = {}
        self.ninstr = 0

    def _sem(self, eng, n):
        k = (n - 1) // EPOCH
        while len(self.sems[eng]) <= k:
            self.sems[eng].append(self.stack.enter_context(self.nc.semaphore(f"s_{eng}{len(self.sems[eng])}")))
        return self.sems[eng][k], (n - 1) % EPOCH + 1

    def _wait(self, eng, ev):
        if ev is None:
            return
        key = id(ev.sem)
        if self.waited[eng].get(key, 0) >= ev.val:
            return
        self.h[eng].wait_ge(ev.sem, ev.val)
        self.waited[eng][key] = ev.val

    def _deps(self, eng, reads, writes):
        evs = []
        for t in reads:
            if t.w is not None:
                evs.append(t.w)
        for t in writes:
            if t.w is not None:
                evs.append(t.w)
            evs.extend(t.r.values())
        for ev in evs:
            if ev.eng == eng and eng == "pe":
                continue
            if ev.eng == eng and RELAX_SAME_ENGINE and not ev.small and ev.eng in self.cnt:
                continue
            self._wait(eng, ev)

    def _update(self, ev, reads, writes, key):
        for t in writes:
            t.w = ev
            t.r = {}
        for t in reads:
            if t in writes:
                continue
            t.r[key] = ev

    def op(self, eng, fn, reads=(), writes=(), small=False):
        reads = [t for t in reads if t is not None]
        writes = [t for t in writes if t is not None]
        self._deps(eng, reads, writes)
        ins = fn(self.h[eng])
        self.cnt[eng] += 1
        sem, val = self._sem(eng, self.cnt[eng])
        ins.then_inc(sem, 1)
        ev = Ev(sem, val, eng, small)
        self.last[eng] = ev
        self._update(ev, reads, writes, eng)
        self.ninstr += 1
        return ev

    def dma(self, q, out, in_, reads=(), writes=(), key=None):
        reads = [t for t in list(reads) + [_tl(in_)] if t is not None]
        writes = [t for t in list(writes) + [_tl(out)] if t is not None]
        key = key or q
        pool_ = self.dsem[key]
        slot = pool_[self.drr[key] % len(pool_)]
        self.drr[key] += 1
        if slot[1] > 0:
            self._wait(q, Ev(slot[0], slot[1], "dma"))
        evs = []
        for t in reads:
            if t.w is not None:
                evs.append(t.w)
        for t in writes:
            if t.w is not None:
                evs.append(t.w)
            evs.extend(t.r.values())
        for ev in evs:
            self._wait(q, ev)
        ins = self.h[q].dma_start(out=_ap(out), in_=_ap(in_))
        slot[1] += 16
        ins.then_inc(slot[0], 16)
        ev = Ev(slot[0], slot[1], "dma")
        self._update(ev, reads, writes, ("dma", q, id(slot[0])))
        self.ninstr += 1
        return ev

    def barrier(self):
        evs = [ev for ev in self.last.values()]
        for k_, pool_ in self.dsem.items():
            if k_ == "cvt":
                continue
            for s in pool_:
                if s[1] > 0:
                    evs.append(Ev(s[0], s[1], "dma"))
        for e in self.h:
            for ev in evs:
                if ev.eng == e and e == "pe":
                    continue
                self._wait(e, ev)

    def act(self, out, in_, func, bias=None, scale=None, small=False, eng="act"):
        kw = {}
        if bias is not None:
            kw["bias"] = _ap(bias)
        if scale is not None:
            kw["scale"] = _ap(scale)
        return self.op(eng, lambda h: h.activation(out=_ap(out), in_=_ap(in_), func=func, **kw),
                       reads=[_tl(in_), _tl(bias), _tl(scale)], writes=[_tl(out)], small=small)

    def tt(self, out, in0, in1, op, small=False, eng="dve"):
        return self.op(eng, lambda h: h.tensor_tensor(out=_ap(out), in0=_ap(in0), in1=_ap(in1), op=op),
                       reads=[_tl(in0), _tl(in1)], writes=[_tl(out)], small=small)

    def ts(self, out, in0, s1, op0, s2=None, op1=None, small=False, eng="dve"):
        kw = {}
        if op1 is not None:
            kw["op1"] = op1
        return self.op(eng, lambda h: h.tensor_scalar(out=_ap(out), in0=_ap(in0), scalar1=_ap(s1), scalar2=_ap(s2),
                                                      op0=op0, **kw),
                       reads=[_tl(in0), _tl(s1), _tl(s2)], writes=[_tl(out)], small=small)

    def stt(self, out, in0, scalar, in1, op0, op1, small=False):
        return self.op("dve", lambda h: h.scalar_tensor_tensor(out=_ap(out), in0=_ap(in0), scalar=_ap(scalar),
                                                               in1=_ap(in1), op0=op0, op1=op1),
                       reads=[_tl(in0), _tl(scalar), _tl(in1)], writes=[_tl(out)], small=small)

    def copy(self, out, in_, eng="dve", small=False):
        if eng == "act":
            return self.act(out, in_, AF.Copy, small=small)
        return self.op(eng, lambda h: h.tensor_copy(out=_ap(out), in_=_ap(in_)),
                       reads=[_tl(in_)], writes=[_tl(out)], small=small)

    def memset(self, out, val, eng="dve", small=False):
        return self.op(eng, lambda h: h.memset(_ap(out), val), writes=[_tl(out)], small=small)

    def recip(self, out, in_, small=False):
        return self.op("dve", lambda h: h.reciprocal(out=_ap(out), in_=_ap(in_)),
                       reads=[_tl(in_)], writes=[_tl(out)], small=small)

    def scan(self, out, d0, d1, init, op0=ALU.mult, op1=ALU.add):
        return self.op("dve", lambda h: h.tensor_tensor_scan(out=_ap(out), data0=_ap(d0), data1=_ap(d1),
                                                             initial=_ap(init), op0=op0, op1=op1),
                       reads=[_tl(d0), _tl(d1), _tl(init)], writes=[_tl(out)])

    def reduce(self, out, in_, op, axis=AX.X, small=False):
        return self.op("dve", lambda h: h.tensor_reduce(out=_ap(out), in_=_ap(in_), axis=axis, op=op),
                       reads=[_tl(in_)], writes=[_tl(out)], small=small)

    def mm(self, out, pairs, extra_reads=()):
        reads = [_tl(a) for a, b in pairs] + [_tl(b) for a, b in pairs] + list(extra_reads)
        n = len(pairs)

        def fn(h):
            ins = None
            for k, (a, b) in enumerate(pairs):
                ins = h.matmul(_ap(out), _ap(a), _ap(b), start=(k == 0), stop=(k == n - 1))
            return ins
        self.ninstr += n - 1
        return self.op("pe", fn, reads=reads, writes=[_tl(out)])

    def mm_multi(self, groups, extra_reads=()):
        reads, writes = list(extra_reads), []
        for out, pairs in groups:
            writes.append(_tl(out))
            for a, b in pairs:
                reads += [_tl(a), _tl(b)]

        def fn(h):
            ins = None
            for out, pairs in groups:
                n = len(pairs)
                for k, (a, b) in enumerate(pairs):
                    ins = h.matmul(_ap(out), _ap(a), _ap(b), start=(k == 0), stop=(k == n - 1))
            return ins
        self.ninstr += sum(len(p) for _, p in groups) - 1
        return self.op("pe", fn, reads=reads, writes=writes)

    def mm_raw(self, items):
        reads, writes = [], []
        for o, a, b, st_, sp_ in items:
            writes.append(_tl(o)); reads += [_tl(a), _tl(b)]

        def fn(h):
            ins = None
            for o, a, b, st_, sp_ in items:
                ins = h.matmul(_ap(o), _ap(a), _ap(b), start=st_, stop=sp_)
            return ins
        self.ninstr += len(items) - 1
        return self.op("pe", fn, reads=reads, writes=writes)

    def transposes(self, items):
        reads, writes = [], []
        for o, i, idn in items:
            writes.append(_tl(o)); reads += [_tl(i), _tl(idn)]

        def fn(h):
            ins = None
            for o, i, idn in items:
                ins = h.transpose(_ap(o), _ap(i), _ap(idn))
            return ins
        self.ninstr += len(items) - 1
        return self.op("pe", fn, reads=reads, writes=writes)


class Arena:
    def __init__(self, nc, stack, nbytes):
        self.t = stack.enter_context(nc.sbuf_tensor("arena", [128, nbytes // 4], F32))
        self.n32 = nbytes // 4
        self.off = 0

    def alloc(self, shape, dt, name):
        n = int(np.prod(shape))
        isz = 4 if dt == F32 else 2
        n32 = (n * isz + 3) // 4
        n32 = (n32 + 15) // 16 * 16
        assert self.off + n32 <= self.n32, f"arena overflow at {name}: {self.off + n32} > {self.n32}"
        ap = self.t[:, self.off:self.off + n32]
        self.off += n32
        if dt != F32:
            ap = ap.bitcast(dt)
        ap = ap[:, 0:n]
        if len(shape) == 2:
            ap = ap.rearrange("p (a b) -> p a b", a=shape[0])
        elif len(shape) == 3:
            ap = ap.rearrange("p (a b c) -> p a b c", a=shape[0], b=shape[1])
        return Tl(ap, name)


class Prog:
    def __init__(self, Ts, depth, debug=False, phases=("P1", "LRU", "GATE", "SWB", "SWF", "P2")):
        self.Ts = list(Ts)
        self.depth = depth
        self.debug = debug
        self.phases = phases
        self.base = [0]
        for t in self.Ts:
            self.base.append(self.base[-1] + t)
        self.Ttot = self.base[-1]
        self.Tmax = max(self.Ts)
        self.nc = bass.Bass("TRN2", target_bir_lowering=False)
        self.stack = contextlib.ExitStack()
        nc = self.nc
        Tt = self.Ttot
        self.dr = {}

        def dram(name, shape, dt, kind):
            self.dr[name] = nc.dram_tensor(name, list(shape), dt, kind=kind).ap()
            return self.dr[name]
        dram("x_in", [Tt, D], F32, "ExternalInput")
        dram("y_out", [Tt, D], F32, "ExternalOutput")
        for l in range(depth):
            dram(f"wbig{l}", [128 * WPP // 2048, 2048], F32, "ExternalInput")
            dram(f"wb16_{l}", [128 * WPP // 2048, 2048], BF16, "Internal")
        dram("pvec", [depth, 128, NPV], F32, "ExternalInput")
        dram("bvec", [depth, 4, 512], F32, "ExternalInput")
        dram("cst", [128, NCST], F32, "ExternalInput")
        dram("rcos", [128, self.Tmax], F32, "ExternalInput")
        dram("rsin", [128, self.Tmax], F32, "ExternalInput")
        sk = "ExternalOutput" if debug else "Internal"
        dram("xT", [8, 128, Tt], F32, sk)
        dram("fm_xa", [4, 128, Tt], BF16, sk)
        dram("fm_ga", [4, 128, Tt], BF16, sk)
        dram("fm_xbc", [6, 128, Tt], BF16, sk)
        dram("fm_mqk", [4, 128, Tt], BF16, sk)
        dram("fm_rq", [2, 128, Tt], BF16, sk)
        dram("fm_rk", [2, 128, Tt], BF16, sk)
        dram("fm_mix", [32, 128, Tt], BF16, sk)
        dram("fm_gr", [2, 128, Tt], F32, sk)
        for n_ in ("z", "rv", "rg", "mv", "mo"):
            dram("tm_" + n_, [Tt, 512], BF16, sk)
        dram("fm_y", [16, 128, Tt], BF16, sk)
        dram("gq", [4, 128, Tt], F32, sk)
        dram("tmg", [Tt, 256], F32, sk)
        nch = Tt // CH
        dram("st_ssd", [nch, 128, 256], BF16, sk)
        dram("st_ret", [nch, 128, 256], BF16, sk)
        dram("st_ml", [nch, 128, 264], BF16, sk)
        dram("st_sc", [nch, 128, 8], F32, sk)
        self.drt = {}

    def dt_(self, *key):
        if key not in self.drt:
            self.drt[key] = Tl(None, str(key))
        return self.drt[key]

    def wview(self, l):
        return self.dr[f"wb16_{l}"].rearrange("(p a) b -> p (a b)", p=128)

    def build(self):
        nc = self.nc
        with self.stack:
            self.S = S = Sched(nc, self.stack)
            st = self.stack
            self.cst = Tl(st.enter_context(nc.sbuf_tensor("cst_sb", [128, NCST], F32))[:], "cst")
            self.cstb = Tl(st.enter_context(nc.sbuf_tensor("cstb_sb", [128, 512], BF16))[:], "cstb")
            self.pv = Tl(st.enter_context(nc.sbuf_tensor("pv_sb", [128, NPV], F32))[:], "pv")
            self.dpv = Tl(st.enter_context(nc.sbuf_tensor("dpv_sb", [128, 64], F32))[:], "dpv")
            self.bv = Tl(st.enter_context(nc.sbuf_tensor("bv_sb", [128, 4, 512], F32))[:], "bv")
            self.SC = Tl(st.enter_context(nc.sbuf_tensor("sc_sb", [128, 32, 24], F32))[:], "SC")
            self.lnb8 = Tl(st.enter_context(nc.sbuf_tensor("lnb8_sb", [128, 1], F32))[:], "lnb8")
            psum = st.enter_context(nc.psum_tensor("ps_all", [128, 8 * 512], F32))
            self.bank = [Tl(psum[:, k * 512:(k + 1) * 512], f"bank{k}") for k in range(8)]
            self.arena = Arena(nc, st, 150 * 1024)
            rows = 128 * WPP // 2048
            npc = 7
            for l in range(self.depth):
                src = self.dr[f"wbig{l}"]
                dst = self.dr[f"wb16_{l}"]
                step = rows // npc
                for i in range(npc):
                    S.dma("pool", dst[i * step:(i + 1) * step, :], src[i * step:(i + 1) * step, :],
                          writes=[self.dt_("w", l, i)], key="cvt")
            S.dma("sp", self.cst, self.dr["cst"])
            S.copy(self.cstb[:, 0:128], self.cst[:, CST["ident"]:CST["ident"] + 128])
            S.copy(self.cstb[:, 128:256], self.cst[:, CST["ones"]:CST["ones"] + 128])
            S.copy(self.cstb[:, 256:384], self.cst[:, CST["triU"]:CST["triU"] + 128])
            S.copy(self.cstb[:, 384:512], self.cst[:, CST["triL"]:CST["triL"] + 128])
            S.memset(self.lnb8, math.log(8.0), small=True)
            self.epsb = Tl(st.enter_context(nc.sbuf_tensor("epsb_sb", [128, 1], F32))[:], "epsb")
            S.memset(self.epsb, EPS, small=True)
            self.identb = self.cstb[:, 0:128]
            self.onesb = self.cstb[:, 128:256]
            self.ident = self.cst[:, CST["ident"]:CST["ident"] + 128]
            self.wring_i = 0
            for l in range(self.depth):
                self.layer(l)
            S.barrier()
        return nc

    def pcol(self, name, i=0):
        return self.pv[:, PV[name] + i:PV[name] + i + 1]

    def layer(self, l):
        S = self.S
        S.dma("sp", self.pv, self.dr["pvec"][l])
        S.dma("sp", self.bv, self.dr["bvec"][l].partition_broadcast(128))
        self.derive_params(l)
        nseq = len(self.Ts)
        if "P1" in self.phases:
            self.setup_P1()
            for s in range(nseq):
                for t0 in range(0, self.Ts[s], TT):
                    self.P1(l, s, t0)
        for s in range(nseq):
            if "LRU" in self.phases:
                self.LRU(l, s)
            if "GATE" in self.phases:
                self.GATE(l, s)
            if "SWB" in self.phases:
                self.SWEEP(l, s, "B")
            if "SWF" in self.phases:
                self.SWEEP(l, s, "F")
        if "P2" in self.phases:
            self.setup_P2()
            for s in range(nseq):
                for t0 in range(0, self.Ts[s], TT):
                    self.P2(l, s, t0)

    def new_phase(self):
        self.S.barrier()
        self.arena.off = 0
        for b in self.bank:
            b.w = None
            b.r = {}

    def derive_params(self, l):
        S = self.S
        dp = self.dpv
        lam = self.pv[:, PV["lru_lam"]:PV["lru_lam"] + 8]
        S.act(dp[:, 32:40], lam, AF.Exp, scale=-1.0, small=True)
        S.act(dp[:, 32:40], dp[:, 32:40], AF.Ln, bias=1.0, small=True)
        S.ts(dp[:, 0:8], dp[:, 32:40], -4.0, ALU.mult, small=True)
        S.ts(dp[:, 8:16], dp[:, 32:40], -8.0, ALU.mult, small=True)
        S.ts(dp[:, 16:32], self.pv[:, PV["lru_gb"]:PV["lru_gb"] + 16], 0.5, ALU.mult, small=True)
        S.act(dp[:, 40:41], self.pcol("ralog"), AF.Exp, small=True)
        S.ts(dp[:, 40:41], dp[:, 40:41], -1.0, ALU.mult, small=True)
        S.ts(dp[:, 41:42], self.pcol("rbB"), -1.0, ALU.mult, small=True)
        if not hasattr(self, "dpv2"):
            self.dpv2 = Tl(self.stack.enter_context(self.nc.sbuf_tensor("dpv2_sb", [128, 64], F32))[:], "dpv2")
        d2 = self.dpv2
        S.ts(d2[:, 0:24], self.pv[:, PV["xbc_w"]:PV["xbc_w"] + 24], 0.5, ALU.mult, small=True)
        S.ts(d2[:, 24:30], self.pv[:, PV["xbc_b"]:PV["xbc_b"] + 6], 0.5, ALU.mult, small=True)
        S.ts(d2[:, 30:46], self.pv[:, PV["mqk_w"]:PV["mqk_w"] + 16], 0.5, ALU.mult, small=True)
        S.ts(d2[:, 46:50], self.pv[:, PV["mqk_b"]:PV["mqk_b"] + 4], 0.5, ALU.mult, small=True)

    def wload(self, l, off, n, q="sp"):
        w = self.wring[self.wring_i % len(self.wring)]
        self.wring_i += 1
        self.S.dma(q, w[:, 0:n], self.wview(l)[:, off:off + n], reads=[self.dt_("w", l, i) for i in range(7)])
        return w

    def alloc_common(self, nw):
        A = self.arena
        self.X32 = A.alloc([8, TT], F32, "X32")
        self.Xb = A.alloc([8, TT], BF16, "Xb")
        self.SQ = A.alloc([8, TT], BF16, "SQ")
        self.G = A.alloc([22, TT], BF16, "G")
        self.wring = [A.alloc([GE], BF16, f"W{i}") for i in range(nw)]
        self.lnt = [A.alloc([TT], F32, f"lnt{i}") for i in range(3)]
        self.sgt = [A.alloc([TT], BF16, f"sg{i}") for i in range(2)]
        self.pset = 0

    def banks4(self):
        b = self.bank[4 * self.pset:4 * self.pset + 4]
        self.pset ^= 1
        return b

    def setup_P1(self):
        self.new_phase()
        A = self.arena
        self.alloc_common(3)
        self.stg = [A.alloc([4, TT], BF16, f"stg{i}") for i in range(2)]
        self.stg32 = A.alloc([2, TT], F32, "stg32")
        self.rcos = A.alloc([TT], F32, "rcos")
        self.rsin = A.alloc([TT], F32, "rsin")
        self.tmpf = [A.alloc([TT], F32, f"tmpf{i}") for i in range(3)]
        self.xin = [A.alloc([D], F32, f"xin{i}") for i in range(2)]
        self.stg_i = 0

    def setup_P2(self):
        self.new_phase()
        A = self.arena
        self.alloc_common(2)
        self.Y = A.alloc([16, TT], BF16, "Y")
        self.M32 = A.alloc([8, TT], F32, "M32")
        self.GT = [A.alloc([8, TT], BF16, f"GT{i}") for i in range(2)]
        self.tmpf = [A.alloc([TT], F32, f"tmpf{i}") for i in range(2)]
        self.outt = [A.alloc([D], F32, f"outt{i}") for i in range(2)]

    def ffn(self, l, f):
        S = self.S
        c = 0.5 / DN_ALPHA
        for g in range(11):
            W = self.wload(l, OFF[("up", f)] + g * GE, GE)
            Wv = W.v().rearrange("p (k j) -> p k j", k=8)
            bk = self.banks4()
            S.mm_multi([(bk[cc], [(Wv[:, kc, cc * 128:(cc + 1) * 128], self.Xb[:, kc, :]) for kc in range(8)])
                        for cc in range(4)])
            for p in range(2):
                sg = self.sgt[p]
                S.act(sg, bk[p], AF.Silu)
                S.tt(self.G[:, 2 * g + p, :], sg, bk[2 + p], ALU.mult)
        for half in range(2):
            bk = self.banks4()
            for q in range(4):
                dc = half * 4 + q
                W = self.wload(l, OFF[("dn", f)] + dc * 2816, 2816)
                Wv = W[:, 0:2816].rearrange("p (k j) -> p k j", k=22)
                S.mm(bk[q], [(Wv[:, fc, :], self.G[:, fc, :]) for fc in range(22)])
                S.stt(self.X32[:, dc, :], bk[q], c, self.X32[:, dc, :], ALU.mult, ALU.add)

    def layernorm(self, i):
        S = self.S
        epsp = EPS / (DN_ALPHA ** 2)
        S.act(self.Xb, self.X32, AF.Copy)
        S.act(self.SQ, self.X32, AF.Square)
        bk = self.banks4()
        S.mm(bk[0], [(self.onesb, self.Xb[:, kc, :]) for kc in range(8)])
        S.mm(bk[1], [(self.onesb, self.SQ[:, kc, :]) for kc in range(8)])
        mean, t1, rstd = self.lnt
        S.act(mean, bk[0], AF.Identity, scale=1.0 / D)
        S.act(t1, bk[0], AF.Square, scale=1.0 / D)
        S.stt(t1, bk[1], 1.0 / D, t1, ALU.mult, ALU.subtract)
        S.ts(t1, t1, epsp, ALU.add)
        S.act(t1, t1, AF.Ln)
        S.act(rstd, t1, AF.Exp, scale=-0.5)
        S.tt(self.X32, self.X32, mean.v().bcast(1, 8), ALU.subtract)
        S.tt(self.X32, self.X32, rstd.v().bcast(1, 8), ALU.mult)
        for kc in range(8):
            S.act(self.X32[:, kc, :], self.X32[:, kc, :], AF.Identity,
                  scale=self.pcol("ln_g", i * 8 + kc), bias=self.pcol("ln_b", i * 8 + kc))
        S.copy(self.Xb, self.X32, eng="dve")

    def load_x(self, l, g0):
        S = self.S
        if l == 0:
            for tc in range(TT // 128):
                xi = self.xin[tc % 2]
                S.dma("sp", xi, self.dr["x_in"][g0 + tc * 128:g0 + (tc + 1) * 128, :])
                bk = self.banks4()
                for hb in range(2):
                    S.transposes([(bk[hb][:, q * 128:(q + 1) * 128], xi[:, (hb * 4 + q) * 128:(hb * 4 + q + 1) * 128],
                                   self.ident) for q in range(4)])
                    S.copy(self.X32[:, hb * 4:hb * 4 + 4, tc * 128:(tc + 1) * 128],
                           bk[hb].v().rearrange("p (q t) -> p q t", q=4), eng=("act" if hb else "dve"))
        else:
            S.dma("sp", self.X32, self.dr["xT"][:, :, g0:g0 + TT].rearrange("k p t -> p k t"),
                  reads=[self.dt_("xT", g0)])

    def store_x(self, l, g0, final):
        S = self.S
        if final:
            for tc in range(TT // 128):
                ot = self.outt[tc % 2]
                bk = self.banks4()
                for hb in range(2):
                    S.transposes([(bk[hb][:, q * 128:(q + 1) * 128], self.X32[:, hb * 4 + q, tc * 128:(tc + 1) * 128],
                                   self.ident) for q in range(4)])
                    S.copy(ot[:, hb * 512:(hb + 1) * 512], bk[hb], eng=("act" if hb else "dve"))
                S.dma("sp", self.dr["y_out"][g0 + tc * 128:g0 + (tc + 1) * 128, :], ot)
        else:
            S.dma("sp", self.dr["xT"][:, :, g0:g0 + TT].rearrange("k p t -> p k t"), self.X32,
                  writes=[self.dt_("xT", g0)])

    def stage(self):
        s = self.stg[self.stg_i % 2]
        self.stg_i += 1
        return s

    def P1(self, l, s, t0):
        S = self.S
        g0 = self.base[s] + t0
        self.load_x(l, g0)
        S.copy(self.Xb, self.X32, eng="act")
        self.ffn(l, 0)
        self.layernorm(0)
        S.dma("sp", self.dr["xT"][:, :, g0:g0 + TT].rearrange("k p t -> p k t"), self.X32,
              writes=[self.dt_("xT", g0)])
        S.dma("pool", self.rcos, self.dr["rcos"][:, t0:t0 + TT])
        S.dma("pool", self.rsin, self.dr["rsin"][:, t0:t0 + TT])
        fmdst = {0: ("fm_xa", 0), 1: ("fm_ga", 0), 2: ("fm_xbc", 0), 4: ("fm_mqk", 0)}
        for g in range(15):
            W = self.wload(l, OFF["inF"] + g * GE, GE)
            Wv = W.v().rearrange("p (k j) -> p k j", k=8)
            bk = self.banks4()
            S.mm_multi([(bk[cc], [(Wv[:, kc, cc * 128:(cc + 1) * 128], self.Xb[:, kc, :]) for kc in range(8)])
                        for cc in range(4)])
            if g in (0, 2, 4):
                stg = self.stage()
                for cc in range(4):
                    S.copy(stg[:, cc, :], bk[cc], eng=("act" if cc % 2 else "dve"))
                nm, c0 = fmdst[g]
                S.dma("sp", self.dr[nm][c0:c0 + 4, :, g0:g0 + TT].rearrange("k p t -> p k t"), stg,
                      writes=[self.dt_(nm, s)])
            elif g == 1:
                stg = self.stage()
                for cc in range(4):
                    a, b, c_ = self.tmpf
                    S.act(a, bk[cc], AF.Square)
                    S.ts(a, a, 0.044715, ALU.mult, 1.0, ALU.add)
                    S.tt(a, a, bk[cc], ALU.mult)
                    S.act(b, a, AF.Tanh, scale=0.7978845608028654)
                    S.act(c_, bk[cc], AF.Copy, scale=0.5)
                    S.stt(stg[:, cc, :], b, 1.0, c_, ALU.add, ALU.mult)
                S.dma("sp", self.dr["fm_ga"][:, :, g0:g0 + TT].rearrange("k p t -> p k t"), stg,
                      writes=[self.dt_("fm_ga", s)])
            elif g == 3:
                stg = self.stage()
                S.copy(stg[:, 0, :], bk[0], eng="act")
                S.copy(stg[:, 1, :], bk[1], eng="dve")
                S.dma("sp", self.dr["fm_xbc"][4:6, :, g0:g0 + TT].rearrange("k p t -> p k t"), stg[:, 0:2, :],
                      writes=[self.dt_("fm_xbc", s)])
                S.copy(self.stg32[:, 0, :], bk[2], eng="act")
                S.copy(self.stg32[:, 1, :], bk[3], eng="dve")
                S.dma("sp", self.dr["fm_gr"][:, :, g0:g0 + TT].rearrange("k p t -> p k t"), self.stg32,
                      writes=[self.dt_("fm_gr", s)])
            elif g in (5, 6):
                stg = self.stage()
                for cc in range(2):
                    a, b, _ = self.tmpf
                    S.tt(a, bk[cc + 2], self.rsin, ALU.mult)
                    S.tt(b, bk[cc], self.rcos, ALU.mult)
                    S.tt(stg[:, cc, :], a, b, ALU.add)
                nm = "fm_rq" if g == 5 else "fm_rk"
                S.dma("sp", self.dr[nm][:, :, g0:g0 + TT].rearrange("k p t -> p k t"), stg[:, 0:2, :],
                      writes=[self.dt_(nm, s)])
            else:
                gi = g - 7
                stg = self.stage()
                for cc in range(4):
                    a = self.tmpf[cc % 3]
                    S.act(a, bk[cc], AF.Tanh, scale=0.5)
                    S.ts(stg[:, cc, :], a, 0.5, ALU.mult, 0.5, ALU.add)
                S.dma("sp", self.dr["fm_mix"][gi * 4:gi * 4 + 4, :, g0:g0 + TT].rearrange("k p t -> p k t"), stg,
                      writes=[self.dt_("fm_mix", s)])
        for g, nm in enumerate(("z", "rv", "rg", "mv", "mo")):
            W = self.wload(l, OFF["inT"] + g * GE, GE)
            Wv = W.v().rearrange("p (k j) -> p k j", k=8)
            bk = self.banks4()
            S.mm_multi([(bk[tc], [(self.Xb[:, kc, tc * 128:(tc + 1) * 128], Wv[:, kc, :]) for kc in range(8)])
                        for tc in range(4)])
            stg = self.stage()
            for tc in range(4):
                if nm in ("z", "rg"):
                    S.act(stg[:, tc, :], bk[tc], AF.Silu)
                elif nm == "mo":
                    a = self.tmpf[tc % 3]
                    S.act(a, bk[tc], AF.Tanh, scale=0.5)
                    S.ts(stg[:, tc, :], a, 0.5, ALU.mult, 0.5, ALU.add)
                else:
                    S.copy(stg[:, tc, :], bk[tc], eng=("act" if tc % 2 else "dve"))
            S.dma("sp", self.dr["tm_" + nm][g0:g0 + TT, :].rearrange("(c p) j -> p c j", p=128), stg,
                  writes=[self.dt_("tm_" + nm, s)])

    def P2(self, l, s, t0):
        S = self.S
        g0 = self.base[s] + t0
        final = (l == self.depth - 1)
        S.dma("sp", self.Y, self.dr["fm_y"][:, :, g0:g0 + TT].rearrange("k p t -> p k t"),
              reads=[self.dt_("fm_y", s)])
        S.dma("sp", self.X32, self.dr["xT"][:, :, g0:g0 + TT].rearrange("k p t -> p k t"),
              reads=[self.dt_("xT", g0)])
        for i in range(4):
            W = self.wload(l, OFF["br"] + i * GE, GE)
            Wv = W.v().rearrange("p (k j) -> p k j", k=4)
            GT = self.GT[i % 2]
            S.dma("pool", GT, self.dr["fm_mix"][i * 8:i * 8 + 8, :, g0:g0 + TT].rearrange("k p t -> p k t"),
                  reads=[self.dt_("fm_mix", s)])
            for half in range(2):
                bk = self.banks4()
                S.mm_multi([(bk[q], [(Wv[:, kc, (half * 4 + q) * 128:(half * 4 + q + 1) * 128],
                                      self.Y[:, i * 4 + kc, :]) for kc in range(4)]) for q in range(4)])
                for q in range(4):
                    dc = half * 4 + q
                    if i == 0:
                        S.tt(self.M32[:, dc, :], bk[q], GT[:, dc, :], ALU.mult)
                    else:
                        a = self.tmpf[q % 2]
                        S.tt(a, bk[q], GT[:, dc, :], ALU.mult)
                        S.tt(self.M32[:, dc, :], self.M32[:, dc, :], a, ALU.add)
        S.act(self.SQ, self.M32, AF.Copy)
        for g in range(2):
            W = self.wload(l, OFF["out"] + g * GE, GE)
            Wv = W.v().rearrange("p (k j) -> p k j", k=8)
            bk = self.banks4()
            S.mm_multi([(bk[cc], [(Wv[:, kc, cc * 128:(cc + 1) * 128], self.SQ[:, kc, :]) for kc in range(8)])
                        for cc in range(4)])
            for cc in range(4):
                dc = g * 4 + cc
                S.stt(self.X32[:, dc, :], bk[cc], 1.0 / DN_ALPHA, self.X32[:, dc, :], ALU.mult, ALU.add)
        self.layernorm(1)
        self.ffn(l, 1)
        self.layernorm(2)
        self.store_x(l, g0, final)


def _LRU(self, l, s):
    S = self.S
    self.new_phase()
    A = self.arena
    T = self.Ts[s]
    b0 = self.base[s]
    XP = A.alloc([T + 4], BF16, "XP")
    XC = A.alloc([T], F32, "XC")
    XCb = A.alloc([T], BF16, "XCb")
    A1 = A.alloc([T], F32, "A1")
    A2 = A.alloc([T], F32, "A2")
    B1 = A.alloc([T], F32, "B1")
    HF = A.alloc([T], F32, "HF")
    HB = A.alloc([T], F32, "HB")
    GA = A.alloc([T], BF16, "GA")
    YA = A.alloc([T], BF16, "YA")
    LW = A.alloc([2048], BF16, "LW")
    S.dma("sp", LW, self.wview(l)[:, OFF["lru"]:OFF["lru"] + 2048], reads=[self.dt_("w", l, i) for i in range(7)])
    LWv = LW.v().rearrange("p (m j) -> p m j", m=16)
    S.memset(XP[:, 0:2], 0.0, small=True)
    S.memset(XP[:, T + 2:T + 4], 0.0, small=True)
    nb = 0
    for ct in range(4):
        S.dma("sp", XP[:, 2:2 + T], self.dr["fm_xa"][ct, :, b0:b0 + T], reads=[self.dt_("fm_xa", s)])
        S.dma("pool", GA, self.dr["fm_ga"][ct, :, b0:b0 + T], reads=[self.dt_("fm_ga", s)])
        S.ts(XC, XP[:, 0:T], self.pcol("xa_w", ct * 4), ALU.mult, self.pcol("xa_b", ct), ALU.add)
        for k in range(1, 4):
            S.stt(XC, XP[:, k:k + T], self.pcol("xa_w", ct * 4 + k), XC, ALU.mult, ALU.add)
        S.act(XCb, XC, AF.Copy)
        for d_ in range(2):
            for gate, dst in ((0, A1), (1, B1)):
                m = (d_ * 2 + gate) * 4 + ct
                hb = self.dpv[:, 16 + m:17 + m]
                for c0 in range(0, T, 512):
                    bk = self.bank[nb % 8]; nb += 1
                    S.mm(bk, [(LWv[:, m, :], XCb[:, c0:c0 + 512])])
                    S.act(dst[:, c0:c0 + 512], bk, AF.Tanh, scale=0.5, bias=hb)
            ch = self.dpv[:, d_ * 4 + ct:d_ * 4 + ct + 1]
            cf = self.dpv[:, 8 + d_ * 4 + ct:8 + d_ * 4 + ct + 1]
            S.act(A2, A1, AF.Exp, scale=cf, bias=cf)
            S.act(A1, A1, AF.Exp, scale=ch, bias=ch)
            S.act(A2, A2, AF.Sqrt, scale=-0.25, bias=0.25)
            S.stt(B1, B1, 1.0, XC, ALU.add, ALU.mult)
            S.tt(B1, B1, A2, ALU.mult)
            if d_ == 0:
                S.scan(HF, A1, B1, 0.0)
            else:
                S.scan(HB[:, ::-1], A1[:, ::-1], B1[:, ::-1], 0.0)
        S.tt(HF, HF, HB, ALU.add)
        S.tt(YA, HF, GA, ALU.mult)
        S.dma("sp", self.dr["fm_y"][ct, :, b0:b0 + T], YA, writes=[self.dt_("fm_y", s)])


def _GATE(self, l, s):
    S = self.S
    self.new_phase()
    A = self.arena
    T = self.Ts[s]
    b0 = self.base[s]
    nch = T // CH
    GA = A.alloc([T], F32, "gGA")
    GB = A.alloc([T], F32, "gGB")
    U = [A.alloc([T], F32, f"gU{i}") for i in range(5)]
    SM = A.alloc([8, 32], F32, "gSM")
    R1 = A.alloc([nch * 24], F32, "gR1")
    STG = [A.alloc([512], F32, f"gST{i}") for i in range(2)]
    ones = self.cst[:, CST["ones"]:CST["ones"] + 128]
    S.dma("sp", GA, self.dr["fm_gr"][0, :, b0:b0 + T], reads=[self.dt_("fm_gr", s)])
    S.dma("sp", GB, self.dr["fm_gr"][1, :, b0:b0 + T], reads=[self.dt_("fm_gr", s)])
    DT, LNDT, CUM, TA, TB = U
    lo = slice(0, 64)
    hi = slice(64, 128)
    S.act(DT[lo], GA[lo], AF.Exp, bias=self.pv[lo, PV["rbA"]:PV["rbA"] + 1])
    S.act(DT[lo], DT[lo], AF.Ln, bias=1.0)
    S.act(LNDT[lo], DT[lo], AF.Ln)
    S.ts(DT[lo], DT[lo], self.dpv[lo, 40:41], ALU.mult)
    for c in range(nch):
        cs = slice(c * CH, (c + 1) * CH)
        S.scan(CUM[0:32, cs], ones[0:32, :], DT[0:32, cs], 0.0)
        S.scan(CUM[32:64, cs][:, ::-1], ones[32:64, :], DT[32:64, cs][:, ::-1], 0.0)
    S.tt(LNDT[lo], LNDT[lo], CUM[lo], ALU.subtract)
    S.act(TA[lo], CUM[lo], AF.Exp)
    c3 = CUM.v().rearrange("p (c t) -> p c t", t=CH)
    S.copy(SM[0:32, 0, 0:nch], c3[0:32, :, CH - 1], small=True)
    S.copy(SM[32:64, 0, 0:nch], c3[32:64, :, 0], small=True)
    S.act(SM[lo, 1, 0:nch], SM[lo, 0, 0:nch], AF.Exp, small=True)
    S.tt(TB.v().rearrange("p (c t) -> p c t", t=CH)[lo], LNDT.v().rearrange("p (c t) -> p c t", t=CH)[lo],
         SM[lo, 0, 0:nch].bcast(2, CH), ALU.add)
    S.act(TB[lo], TB[lo], AF.Exp)
    S.dma("sp", self.dr["gq"][0, 0:64, b0:b0 + T], CUM[lo], writes=[self.dt_("gq", s)])
    S.dma("sp", self.dr["gq"][1, 0:64, b0:b0 + T], LNDT[lo], writes=[self.dt_("gq", s)])
    S.act(DT[hi], GB[hi], AF.Exp, scale=-1.0, bias=self.dpv[hi, 41:42])
    S.act(DT[hi], DT[hi], AF.Ln, bias=1.0)
    NB = GB
    o1f = self.cst[64:96, CST["ones"]:CST["ones"] + 1]
    o1b = self.cst[96:128, CST["ones"]:CST["ones"] + 1]
    S.scan(NB[64:96, :], V(o1f.ap.broadcast_to([32, T]), o1f.tl), DT[64:96, :], 0.0)
    S.scan(NB[96:128, :][:, ::-1], V(o1b.ap.broadcast_to([32, T]), o1b.tl), DT[96:128, :][:, ::-1], 0.0)
    UG = LNDT
    S.ts(UG[hi], GA[hi], self.pv[hi, PV["rbA"]:PV["rbA"] + 1], ALU.add)
    S.tt(UG[hi], UG[hi], NB[hi], ALU.add)
    S.reduce(SM[hi, 2, 0:nch], UG.v().rearrange("p (c t) -> p c t", t=CH)[hi], ALU.max, small=True)
    S.scan(SM[64:96, 3, 0:nch], SM[64:96, 2, 0:nch], SM[64:96, 2, 0:nch], 0.0, op0=ALU.max, op1=ALU.max)
    S.scan(SM[96:128, 3, 0:nch][:, ::-1], SM[96:128, 2, 0:nch][:, ::-1], SM[96:128, 2, 0:nch][:, ::-1], 0.0,
           op0=ALU.max, op1=ALU.max)
    S.memset(SM[hi, 4, :], 0.0, small=True)
    if nch > 1:
        S.copy(SM[64:96, 4, 1:nch], SM[64:96, 3, 0:nch - 1], small=True)
        S.copy(SM[96:128, 4, 0:nch - 1], SM[96:128, 3, 1:nch], small=True)
    S.tt(SM[hi, 5, 0:nch], SM[hi, 4, 0:nch], SM[hi, 3, 0:nch], ALU.subtract, small=True)
    S.act(SM[hi, 5, 0:nch], SM[hi, 5, 0:nch], AF.Exp, small=True)
    mgb = SM[hi, 3, 0:nch].bcast(2, CH)
    S.tt(TA.v().rearrange("p (c t) -> p c t", t=CH)[hi], UG.v().rearrange("p (c t) -> p c t", t=CH)[hi], mgb,
         ALU.subtract)
    S.act(TA[hi], TA[hi], AF.Exp)
    S.tt(TB.v().rearrange("p (c t) -> p c t", t=CH)[hi], NB.v().rearrange("p (c t) -> p c t", t=CH)[hi], mgb,
         ALU.subtract)
    S.act(TB[hi], TB[hi], AF.Exp, bias=self.lnb8[hi])
    selT = self.cst[:, CST["selT"]:CST["selT"] + 24]
    r1 = R1.v().rearrange("p (c n) -> p c n", n=24)
    S.tt(r1[lo, :, 0:16], SM[lo, 1, 0:nch].bcast(2, 16), selT[lo, 0:16].bcast(1, nch), ALU.mult)
    S.tt(r1[hi, :, 16:24], SM[hi, 5, 0:nch].bcast(2, 8), selT[hi, 16:24].bcast(1, nch), ALU.mult)
    bk = self.bank[0]
    o128 = self.cst[:, CST["ones"]:CST["ones"] + 128]
    S.memset(r1[lo, :, 16:24], 0.0)
    S.memset(r1[hi, :, 0:16], 0.0)
    for c0 in range(0, nch, 16):
        c1 = min(nch, c0 + 16)
        bkx = self.bank[(c0 // 16) % 2 * 5]
        S.mm(bkx[:, 0:(c1 - c0) * 24], [(o128, R1[:, c0 * 24:c1 * 24])])
        S.copy(self.SC[:, c0:c1, :], bkx.v()[:, 0:(c1 - c0) * 24].rearrange("p (c n) -> p c n", n=24))
    for c in range(nch):
        bk = self.bank[1 + c % 4]
        cs = slice(c * CH, (c + 1) * CH)
        S.transposes([(bk[:, 0:128], TA[:, cs], self.ident), (bk[:, 128:256], TB[:, cs], self.ident)])
        st = STG[c % 2]
        S.copy(st[:, 0:256], bk[:, 0:256], eng=("act" if c % 2 else "dve"))
        S.dma("sp", self.dr["tmg"][b0 + c * CH:b0 + (c + 1) * CH, :], st[:, 0:256], writes=[self.dt_("tmg", s)])


Prog.LRU = _LRU
Prog.GATE = _GATE


SEG = 512


def _nbk(self):
    b = self.bank[self._bki % 8]
    self._bki += 1
    return b


def _conv_silu(self, dst, XP, XH, TH, wcol0, bcol, n):
    S = self.S
    d2 = self.dpv2
    S.ts(XH, XP[:, 0:n], d2[:, wcol0:wcol0 + 1], ALU.mult, d2[:, bcol:bcol + 1], ALU.add)
    for k in range(1, 4):
        S.stt(XH, XP[:, k:k + n], d2[:, wcol0 + k:wcol0 + k + 1], XH, ALU.mult, ALU.add)
    S.act(TH, XH, AF.Tanh)
    S.stt(dst, TH, 1.0, XH, ALU.add, ALU.mult)


def _load_halo(self, XP, name, ct, s, t0, T):
    S = self.S
    b0 = self.base[s]
    lo = max(t0 - 2, 0)
    hi = min(t0 + SEG + 1, T)
    if t0 == 0:
        S.memset(XP[:, 0:2], 0.0, small=True)
    if t0 + SEG == T:
        S.memset(XP[:, SEG + 2:SEG + 4], 0.0, small=True)
    S.dma("sp", XP[:, lo - (t0 - 2):hi - (t0 - 2)], self.dr[name][ct, :, b0 + lo:b0 + hi], reads=[self.dt_(name, s)])


def _headnorm(self, Y, STATS, MV2, TMP4, nwi, gate, OUTb):
    S = self.S
    for h in range(4):
        S.op("dve", lambda e, h=h: e.bn_stats(out=_ap(STATS[:, h, :]), in_=_ap(Y[:, h, :])),
             reads=[Y.tl], writes=[STATS.tl], small=True)
    for h in range(4):
        S.op("dve", lambda e, h=h: e.bn_aggr(out=_ap(MV2[:, h, :]), in_=_ap(STATS[:, h, :])),
             reads=[STATS.tl], writes=[MV2.tl], small=True)
    S.act(TMP4, MV2[:, :, 1], AF.Ln, bias=self.epsb, small=True)
    S.act(TMP4, TMP4, AF.Exp, scale=-0.5, small=True)
    for h in range(4):
        S.ts(Y[:, h, :], Y[:, h, :], MV2[:, h, 0:1], ALU.subtract, TMP4[:, h:h + 1], ALU.mult)
    Yf = Y.rearrange("p h v -> p (h v)")
    S.tt(Yf, Yf, self.bv[:, nwi, :], ALU.mult)
    S.tt(OUTb, Yf, gate, ALU.mult)


def _SWEEP(self, l, s, sw):
    S = self.S
    self.new_phase()
    A = self.arena
    T = self.Ts[s]
    b0 = self.base[s]
    nseg = T // SEG
    F = (sw == "F")
    dn = 0 if F else 1
    self._bki = 0
    HS = A.alloc([256], F32, "HS"); HR = A.alloc([2, 128], F32, "HR"); HM = A.alloc([2, 132], F32, "HM")
    for t_ in (HS, HR, HM):
        S.memset(t_, 0.0)
    if F:
        HSx = [A.alloc([2, 256], BF16, f"HSx{i}") for i in range(2)]
        HRx = [A.alloc([2, 2, 128], BF16, f"HRx{i}") for i in range(2)]
        HMx = [A.alloc([2, 2, 132], BF16, f"HMx{i}") for i in range(2)]
        HSxb = [A.alloc([2, 256], BF16, f"HSxb{i}") for i in range(2)]
        HRxb = [A.alloc([2, 2, 128], BF16, f"HRxb{i}") for i in range(2)]
        HMxb = [A.alloc([2, 2, 132], BF16, f"HMxb{i}") for i in range(2)]
        for t_ in HSx + HRx + HMx + HSxb + HRxb + HMxb:
            S.memset(t_, 0.0)
    else:
        HSb = [A.alloc([256], BF16, f"HSb{i}") for i in range(2)]
        HRb = [A.alloc([2, 128], BF16, f"HRb{i}") for i in range(2)]
        HMb = [A.alloc([2, 132], BF16, f"HMb{i}") for i in range(2)]
        for t_ in HSb + HRb + HMb:
            S.memset(t_, 0.0)
    XP = [A.alloc([SEG + 4], BF16, f"sXP{i}") for i in range(2)]
    XH = A.alloc([SEG], F32, "sXH"); TH = A.alloc([SEG], F32, "sTH")
    XBC = A.alloc([6, SEG], BF16, "sXBC"); MQK = A.alloc([4, SEG], BF16, "sMQK")
    RK = A.alloc([2, SEG], BF16, "sRK")
    TMG = A.alloc([4, 256], F32, "sTMG")
    RV = A.alloc([4, 512], BF16, "sRV"); MV = A.alloc([4, 512], BF16, "sMV")
    TMA = A.alloc([896], BF16, "sTMA"); TMB = A.alloc([256], BF16, "sTMB")
    WV = A.alloc([512], BF16, "sWV"); WVR = A.alloc([512], BF16, "sWVR"); EV = A.alloc([4, 132], BF16, "sEV")
    if F:
        RQ = A.alloc([2, SEG], BF16, "sRQ")
        CMX = A.alloc([2, SEG], BF16, "sCMX"); MQX = A.alloc([2, 2, SEG], BF16, "sMQX"); RQX = A.alloc([2, 2, SEG], BF16, "sRQX")
        for t_ in (CMX, MQX, RQX):
            S.memset(t_, 0.0)
        CUM = A.alloc([SEG], F32, "sCUM"); CB = A.alloc([SEG], F32, "sCB")
        S.memset(CUM, 0.0); S.memset(CB, 0.0)
        ZG = A.alloc([4, 512], BF16, "sZ"); RG = A.alloc([4, 512], BF16, "sRG"); MO = A.alloc([4, 512], BF16, "sMO")
        WE = [A.alloc([512], F32, f"sWE{i}") for i in range(2)]
        PS_ = [A.alloc([512], BF16, f"sPS{i}") for i in range(4)]
        PR = A.alloc([512], BF16, "sPR"); PM = A.alloc([8, 128], BF16, "sPM")
        Y1 = A.alloc([512], F32, "sY1"); Y2 = A.alloc([512], F32, "sY2"); Y3 = A.alloc([512], F32, "sY3")
        OB = [A.alloc([512], BF16, f"sOB{i}") for i in range(3)]
        YT = [A.alloc([4, SEG], BF16, f"sYT{i}") for i in range(3)]
        STATS = A.alloc([4, 6], F32, "sSTATS"); MV2 = A.alloc([4, 2], F32, "sMV2"); TMP4 = A.alloc([8], F32, "sTMP4")
        DEN = A.alloc([8], F32, "sDEN")
    cst = self.cst
    ident = self.ident
    identb = self.identb
    onesb = self.onesb
    lo = slice(0, 64); hi = slice(64, 128)
    segs = range(nseg) if F else range(nseg - 1, -1, -1)
    xpi = 0
    tmv = lambda nm: self.dr["tm_" + nm][g0:g0 + SEG, :].rearrange("(c p) j -> p c j", p=128)
    for sg in segs:
        t0 = sg * SEG
        g0 = b0 + t0
        for ct in (range(6) if F else range(5)):
            xp = XP[xpi % 2]; xpi += 1
            _load_halo(self, xp, "fm_xbc", ct, s, t0, T)
            _conv_silu(self, XBC[:, ct, :], xp, XH, TH, ct * 4, 24 + ct, SEG)
        for ct in (range(4) if F else (2, 3)):
            xp = XP[xpi % 2]; xpi += 1
            _load_halo(self, xp, "fm_mqk", ct, s, t0, T)
            _conv_silu(self, MQK[:, ct, :], xp, XH, TH, 30 + ct * 4, 46 + ct, SEG)
        S.dma("pool", RK, self.dr["fm_rk"][:, :, g0:g0 + SEG].rearrange("k p t -> p k t"), reads=[self.dt_("fm_rk", s)])
        S.dma("pool", TMG, self.dr["tmg"][g0:g0 + SEG, :].rearrange("(c p) j -> p c j", p=128), reads=[self.dt_("tmg", s)])
        S.dma("pool", RV, tmv("rv"), reads=[self.dt_("tm_rv", s)])
        S.dma("pool", MV, tmv("mv"), reads=[self.dt_("tm_mv", s)])
        if F:
            S.dma("pool", RQ, self.dr["fm_rq"][:, :, g0:g0 + SEG].rearrange("k p t -> p k t"), reads=[self.dt_("fm_rq", s)])
            S.dma("sp", RQX[lo, :, 0, :], self.dr["fm_rq"][:, 0:64, g0:g0 + SEG].rearrange("k p t -> p k t"), reads=[self.dt_("fm_rq", s)])
            S.dma("sp", RQX[hi, :, 1, :], self.dr["fm_rq"][:, 64:128, g0:g0 + SEG].rearrange("k p t -> p k t"), reads=[self.dt_("fm_rq", s)])
            S.copy(CMX[lo, 0, :], XBC[lo, 5, :], eng="act")
            S.copy(CMX[hi, 1, :], XBC[hi, 5, :], eng="act")
            S.copy(MQX[lo, :, 0, :], MQK[lo, 0:2, :], eng="dve")
            S.copy(MQX[hi, :, 1, :], MQK[hi, 0:2, :], eng="dve")
            S.dma("sp", CUM[lo], self.dr["gq"][0, 0:64, g0:g0 + SEG], reads=[self.dt_("gq", s)])
            S.dma("sp", CB[lo], self.dr["gq"][1, 0:64, g0:g0 + SEG], reads=[self.dt_("gq", s)])
            for tl_, nm in ((ZG, "z"), (RG, "rg"), (MO, "mo")):
                S.dma("pool", tl_, tmv(nm), reads=[self.dt_("tm_" + nm, s)])
        cls = range(SEG // CH) if F else range(SEG // CH - 1, -1, -1)
        for cl in cls:
            cs = slice(cl * CH, (cl + 1) * CH)
            lc = t0 // CH + cl
            gc = g0 // CH + cl
            par = lc % 2
            tmg = TMG[:, cl, :]
            bT = _nbk(self); bT2 = _nbk(self)
            bTb = bT.v().bitcast(BF16); bT2b = bT2.v().bitcast(BF16)
            items = [(bTb[:, ct * 128:(ct + 1) * 128], XBC[:, ct, cs], identb) for ct in range(5)]
            items += [(bTb[:, 640:768], MQK[:, 2, cs], identb), (bTb[:, 768:896], MQK[:, 3, cs], identb)]
            S.transposes(items)
            S.transposes([(bT2b[:, 0:128], RK[:, 0, cs], identb), (bT2b[:, 128:256], RK[:, 1, cs], identb)])
            S.copy(TMA, bTb[:, 0:896], eng="act")
            S.copy(TMB, bT2b[:, 0:256], eng="dve")
            Vs = TMA[:, 0:512]; Bm = TMA[:, 512:640]; MKt = TMA[:, 640:896]; RKt = TMB
            S.tt(WV.v().rearrange("p (h v) -> p h v", h=8), Vs.rearrange("p (h v) -> p h v", h=8),
                 tmg[:, 128 + 32 * dn:128 + 32 * dn + 8].bcast(2, 64), ALU.mult)
            S.tt(WVR.v().rearrange("p (h v) -> p h v", h=4), RV[:, cl, :].rearrange("p (h v) -> p h v", h=4),
                 cst[:, CST["retS"] + 8 + 4 * dn:CST["retS"] + 12 + 4 * dn].bcast(2, 128), ALU.mult)
            S.tt(EV[:, :, 0:128], MV[:, cl, :].rearrange("p (h v) -> p h v", h=4),
                 tmg[:, 64 + 32 * dn:64 + 32 * dn + 4].bcast(2, 128), ALU.mult)
            S.copy(EV[:, :, 128], tmg[:, 64 + 32 * dn:64 + 32 * dn + 4], small=True)
            for h in range(4):
                r = slice((h % 2) * 64, (h % 2) * 64 + 64)
                S.ts(HM[r, h // 2, :], HM[r, h // 2, :], self.SC[r, lc, 16 + dn * 4 + h:16 + dn * 4 + h + 1], ALU.mult)
            if not F:
                S.copy(HMb[par], HM, eng="act")
                S.dma("sp", self.dr["st_ssd"][gc], HSb[1 - par], writes=[self.dt_("st_ssd", s)])
                S.dma("sp", self.dr["st_ret"][gc], HRb[1 - par].v().rearrange("p a b -> p (a b)"), writes=[self.dt_("st_ret", s)])
                S.dma("sp", self.dr["st_ml"][gc], HMb[par].v().rearrange("p a b -> p (a b)"), writes=[self.dt_("st_ml", s)])
            else:
                hsf = HSx[par]; hrf = HRx[par]; hmf = HMx[par]
                hsb = HSxb[par]; hrb = HRxb[par]; hmb = HMxb[par]
                S.copy(hsf[lo, 0, :], HS[lo], eng="act"); S.copy(hsf[hi, 1, :], HS[hi], eng="act")
                S.copy(hrf[lo, :, 0, :], HR[lo], eng="dve"); S.copy(hrf[hi, :, 1, :], HR[hi], eng="dve")
                S.copy(hmf[lo, :, 0, :], HM[lo], eng="act"); S.copy(hmf[hi, :, 1, :], HM[hi], eng="act")
                S.dma("pool", hsb[lo, 0, :], self.dr["st_ssd"][gc, 0:64, :], reads=[self.dt_("st_ssd", s)])
                S.dma("pool", hsb[hi, 1, :], self.dr["st_ssd"][gc, 64:128, :], reads=[self.dt_("st_ssd", s)])
                S.dma("pool", hrb[lo, :, 0, :], self.dr["st_ret"][gc, 0:64, :].rearrange("p (c v) -> p c v", c=2), reads=[self.dt_("st_ret", s)])
                S.dma("pool", hrb[hi, :, 1, :], self.dr["st_ret"][gc, 64:128, :].rearrange("p (c v) -> p c v", c=2), reads=[self.dt_("st_ret", s)])
                S.dma("pool", hmb[lo, :, 0, :], self.dr["st_ml"][gc, 0:64, :].rearrange("p (c v) -> p c v", c=2), reads=[self.dt_("st_ml", s)])
                S.dma("pool", hmb[hi, :, 1, :], self.dr["st_ml"][gc, 64:128, :].rearrange("p (c v) -> p c v", c=2), reads=[self.dt_("st_ml", s)])
                bG = _nbk(self)
                S.mm(bG[:, 0:256], [(XBC[:, 4, cs], CMX[:, :, cs])])
                pidx = 0
                Pm = {}
                for d_ in range(2):
                    neg = cst[:, CST["negF"]:CST["negF"] + 128] if d_ == 0 else cst[:, CST["negB"]:CST["negB"] + 128]
                    for g in range(2):
                        bE = _nbk(self)
                        grp = []
                        for q in range(4):
                            m = d_ * 8 + g * 4 + q
                            sel = cst[:, CST["sel"] + m * 128:CST["sel"] + (m + 1) * 128]
                            grp.append((bE[:, q * 128:(q + 1) * 128],
                                        [(sel, CUM[:, cs]), (CB[:, cs], sel), (ident, neg)]))
                        S.mm_multi(grp)
                        we = WE[pidx % 2]
                        S.act(we, bE, AF.Exp)
                        pt = PS_[pidx % 4]; pidx += 1
                        S.tt(pt.v().rearrange("p (q i) -> p q i", q=4), we.v().rearrange("p (q i) -> p q i", q=4),
                             bG[:, g * 128:(g + 1) * 128].bcast(1, 4), ALU.mult)
                        Pm[(d_, g)] = pt
                bY = _nbk(self)
                S.mm_multi([(bY[:, h * 64:(h + 1) * 64],
                             [(Pm[(0, h // 4)][:, (h % 4) * 128:(h % 4 + 1) * 128], Vs[:, h * 64:(h + 1) * 64]),
                              (Pm[(1, h // 4)][:, (h % 4) * 128:(h % 4 + 1) * 128], Vs[:, h * 64:(h + 1) * 64])])
                            for h in range(8)])
                bIf = _nbk(self); bIb = _nbk(self)
                S.mm(bIf, [(XBC[:, 5, cs], hsf.v().rearrange("p a b -> p (a b)"))])
                S.mm(bIb, [(XBC[:, 5, cs], hsb.v().rearrange("p a b -> p (a b)"))])
                y3 = lambda t_: t_.v().rearrange("p (h v) -> p h v", h=8)
                S.tt(y3(Y1), bIf.v().rearrange("p (h v) -> p h v", h=8), tmg[:, 0:8].bcast(2, 64), ALU.mult)
                S.tt(y3(Y2), bIb.v().rearrange("p (h v) -> p h v", h=8), tmg[:, 32:40].bcast(2, 64), ALU.mult)
                S.tt(Y1, Y1, Y2, ALU.add)
                S.tt(Y1, Y1, bY, ALU.add)
                S.tt(Y2, Vs, self.bv[:, 3, :], ALU.mult)
                S.tt(Y1, Y1, Y2, ALU.add)
                S.tt(Y1, Y1, ZG[:, cl, :], ALU.mult)
                S.op("dve", lambda e: e.bn_stats(out=_ap(STATS[:, 0, :]), in_=_ap(Y1)), reads=[Y1], writes=[STATS], small=True)
                S.op("dve", lambda e: e.bn_aggr(out=_ap(MV2[:, 0, :]), in_=_ap(STATS[:, 0, :])), reads=[STATS], writes=[MV2], small=True)
                S.tt(TMP4[:, 0:1], MV2[:, 0, 0:1], MV2[:, 0, 0:1], ALU.mult, small=True)
                S.tt(TMP4[:, 0:1], TMP4[:, 0:1], MV2[:, 0, 1:2], ALU.add, small=True)
                S.act(TMP4[:, 0:1], TMP4[:, 0:1], AF.Ln, bias=self.epsb, small=True)
                S.act(TMP4[:, 0:1], TMP4[:, 0:1], AF.Exp, scale=-0.5, small=True)
                S.stt(OB[0], Y1, TMP4[:, 0:1], self.bv[:, 0, :], ALU.mult, ALU.mult)
                bG2 = _nbk(self)
                S.mm_multi([(bG2[:, ct * 256:(ct + 1) * 256], [(RK[:, ct, cs], RQX[:, ct, :, cs])]) for ct in range(2)])
                S.tt(PR, bG2, cst[:, CST["retW"]:CST["retW"] + 512], ALU.mult)
                bY2 = _nbk(self)
                S.mm_multi([(bY2[:, h * 128:(h + 1) * 128], [(PR[:, h * 128:(h + 1) * 128], RV[:, cl, h * 128:(h + 1) * 128])])
                            for h in range(4)])
                bI2f = _nbk(self); bI2b = _nbk(self)
                for bI, hh in ((bI2f, hrf), (bI2b, hrb)):
                    S.mm_multi([(bI[:, ct * 256:(ct + 1) * 256], [(RQ[:, ct, cs], hh[:, ct, :, :])]) for ct in range(2)])
                y4 = lambda t_: t_.v().rearrange("p (h v) -> p h v", h=4)
                S.tt(y4(Y2), bI2f.v().rearrange("p (h v) -> p h v", h=4), cst[:, CST["retS"]:CST["retS"] + 4].bcast(2, 128), ALU.mult)
                S.tt(y4(Y3), bI2b.v().rearrange("p (h v) -> p h v", h=4), cst[:, CST["retS"] + 4:CST["retS"] + 8].bcast(2, 128), ALU.mult)
                S.tt(Y2, Y2, Y3, ALU.add)
                S.tt(Y2, Y2, bY2, ALU.add)
                _headnorm(self, y4(Y2), STATS, MV2, TMP4[:, 0:4], 1, RG[:, cl, :], OB[1])
                bG3 = _nbk(self)
                S.mm_multi([(bG3[:, ct * 256:(ct + 1) * 256], [(MQK[:, 2 + ct, cs], MQX[:, ct, :, cs])]) for ct in range(2)])
                for d_ in range(2):
                    msk = cst[:, CST["triU"]:CST["triU"] + 128] if d_ == 0 else cst[:, CST["triL"]:CST["triL"] + 128]
                    for h in range(4):
                        S.stt(PM[:, d_ * 4 + h, :], bG3[:, h * 128:(h + 1) * 128], tmg[:, 64 + 32 * d_ + h:64 + 32 * d_ + h + 1],
                              msk, ALU.mult, ALU.mult)
                bN = [_nbk(self), _nbk(self)]
                bD = _nbk(self)
                for d_, hm in ((0, hmf), (1, hmb)):
                    raw = []
                    for ct in range(2):
                        raw.append((bN[d_][:, ct * 256:(ct + 1) * 256], MQK[:, ct, cs], hm[:, ct, :, 0:128], True, False))
                        for hl in range(2):
                            h = ct * 2 + hl
                            raw.append((bN[d_][:, h * 128:(h + 1) * 128], PM[:, d_ * 4 + h, :], MV[:, cl, h * 128:(h + 1) * 128],
                                        False, hl == 1))
                    S.mm_raw(raw)
                raw = []
                for d_, hm in ((0, hmf), (1, hmb)):
                    for ct in range(2):
                        raw.append((bD[:, d_ * 4 + ct * 2:d_ * 4 + ct * 2 + 2], MQK[:, ct, cs], hm[:, ct, :, 128], True, False))
                        for hl in range(2):
                            h = ct * 2 + hl
                            raw.append((bD[:, d_ * 4 + h:d_ * 4 + h + 1], PM[:, d_ * 4 + h, :], onesb[:, 0:1], False, hl == 1))
                S.mm_raw(raw)
                S.act(DEN, bD[:, 0:8], AF.Abs, small=True)
                clampv = tmg[:, 192:256].rearrange("p (d x) -> p d x", d=2)[:, :, 0:4]
                S.tt(DEN.v().rearrange("p (d x) -> p d x", d=2), DEN.v().rearrange("p (d x) -> p d x", d=2), clampv, ALU.max, small=True)
                S.recip(DEN, DEN, small=True)
                S.tt(y4(Y3), bN[0].v().rearrange("p (h v) -> p h v", h=4), DEN[:, 0:4].bcast(2, 128), ALU.mult)
                S.tt(y4(Y1), bN[1].v().rearrange("p (h v) -> p h v", h=4), DEN[:, 4:8].bcast(2, 128), ALU.mult)
                S.tt(Y3, Y3, Y1, ALU.add)
                _headnorm(self, y4(Y3), STATS, MV2, TMP4[:, 0:4], 2, MO[:, cl, :], OB[2])
                for bi in range(3):
                    bO = _nbk(self)
                    bOb = bO.v().bitcast(BF16)
                    S.transposes([(bOb[:, k * 128:(k + 1) * 128], OB[bi][:, k * 128:(k + 1) * 128], identb) for k in range(4)])
                    S.copy(YT[bi][:, :, cs], bOb[:, 0:512].rearrange("p (k t) -> p k t", k=4), eng=("act" if bi % 2 else "dve"))
            bS = _nbk(self)
            S.mm(bS, [(Bm, WV)])
            for g in range(2):
                r = slice(g * 64, (g + 1) * 64)
                S.tt(HS[r].rearrange("p (h v) -> p h v", h=4), HS[r].rearrange("p (h v) -> p h v", h=4),
                     self.SC[r, lc, dn * 8 + g * 4:dn * 8 + g * 4 + 4].bcast(2, 64), ALU.mult)
                S.tt(HS[r], HS[r], bS[r, g * 256:(g + 1) * 256], ALU.add)
            bS2 = _nbk(self)
            S.mm_multi([(bS2[:, ct * 256:(ct + 1) * 256], [(RKt[:, ct * 128:(ct + 1) * 128], WVR[:, ct * 256:(ct + 1) * 256])])
                        for ct in range(2)])
            S.tt(HR, HR, cst[:, CST["retG"]:CST["retG"] + 256].rearrange("p (c v) -> p c v", c=2), ALU.mult)
            b2v = bS2.v().rearrange("p (c a v) -> p c a v", c=2, a=2)
            S.tt(HR[lo], HR[lo], b2v[lo, :, 0, :], ALU.add)
            S.tt(HR[hi], HR[hi], b2v[hi, :, 1, :], ALU.add)
            for ct in range(2):
                bS3 = _nbk(self)
                S.mm(bS3[:, 0:264], [(MKt[:, ct * 128:(ct + 1) * 128], EV[:, 2 * ct:2 * ct + 2, :])])
                S.tt(HM[lo, ct, 0:129], HM[lo, ct, 0:129], bS3[lo, 0:129], ALU.add)
                S.tt(HM[hi, ct, 0:129], HM[hi, ct, 0:129], bS3[hi, 132:261], ALU.add)
            if not F:
                S.copy(HSb[par], HS, eng="act")
                S.copy(HRb[par], HR, eng="act")
        if F:
            for bi in range(3):
                S.dma("sp", self.dr["fm_y"][4 + bi * 4:8 + bi * 4, :, g0:g0 + SEG].rearrange("k p t -> p k t"), YT[bi],
                      writes=[self.dt_("fm_y", s)])


Prog.SWEEP = _SWEEP


_SEQS = [2048, 4096, 4096]


def kernel(**inputs):
    inp = {k: np.asarray(v) for k, v in inputs.items()}
    prog = Prog(_SEQS, DEPTH)
    nc = prog.build()
    wbig = [pack_layer_weights(inp, l).reshape(-1, 2048) for l in range(DEPTH)]
    pvec = np.stack([pack_pvec(inp, l) for l in range(DEPTH)])
    bvec = np.stack([pack_bvec(inp, l) for l in range(DEPTH)])
    cst, rc, rs = build_consts(max(_SEQS))
    xp, xs = inp["x_prompt"], inp["x_sample"]
    in_maps = []
    for c in range(NCORES):
        x_in = np.concatenate([xp[c], xs[2 * c], xs[2 * c + 1]], axis=0)
        m = {"x_in": np.ascontiguousarray(x_in), "pvec": pvec, "bvec": bvec, "cst": cst, "rcos": rc, "rsin": rs}
        for l in range(DEPTH):
            m[f"wbig{l}"] = wbig[l]
        in_maps.append(m)
    res = run_bass_kernel_spmd(nc, in_maps, core_ids=list(range(NCORES)))
    yp = np.empty_like(xp)
    ys = np.empty_like(xs)
    for c in range(NCORES):
        y = np.asarray(res.results[c]["y_out"])
        yp[c] = y[0:2048]
        ys[2 * c] = y[2048:6144]
        ys[2 * c + 1] = y[6144:10240]
    return yp, ys
```

```python
import contextlib
import math
import numpy as np
import concourse.bass as bass
import concourse.mybir as mybir
from concourse.bass_utils import run_bass_kernel_spmd

F32 = mybir.dt.float32
BF16 = mybir.dt.bfloat16
AF = mybir.ActivationFunctionType
ALU = mybir.AluOpType
AX = mybir.AxisListType

D = 1024
DFF = 2816
DEPTH = 4
NKC = 8
TT = 512
CH = 128
DN_ALPHA = (2.0 * DEPTH) ** 0.25
EPS = 1e-5
NCORES = 8

GE = 4096
OFF = {}
_o = 0
for f in range(2):
    OFF[("up", f)] = _o; _o += 11 * GE
    OFF[("dn", f)] = _o; _o += 8 * 2816
OFF["inF"] = _o; _o += 15 * GE
OFF["inT"] = _o; _o += 5 * GE
OFF["br"] = _o; _o += 4 * GE
OFF["out"] = _o; _o += 2 * GE
OFF["lru"] = _o; _o += 2048
WPP = _o
assert WPP % 2048 == 0

_sp = [512, 512, 512, 768, 16, 256, 256, 512, 512, 512, 512, 512, 16, 4096]
_names = ["xa", "ga", "z", "xbc", "dt", "rq", "rk", "rv", "rg", "mqk", "mv", "mo", "mg", "mix"]
COL = {}
_a = 0
for n_, s_ in zip(_names, _sp):
    COL[n_] = _a; _a += s_
assert _a == 9504


def _rot_cols(base):
    idx = []
    for h in range(4):
        for d in range(64):
            idx.append(base + h * 64 + (d + 32) % 64)
    return idx


def _inF_colmap():
    groups = []
    groups.append(list(range(COL["xa"], COL["xa"] + 512)))
    groups.append(list(range(COL["ga"], COL["ga"] + 512)))
    groups.append(list(range(COL["xbc"], COL["xbc"] + 512)))
    g3 = list(range(COL["xbc"] + 512, COL["xbc"] + 768))
    ga = [-1] * 128
    gb = [-1] * 128
    for h in range(8):
        ga[h] = COL["dt"] + h
        ga[32 + h] = COL["dt"] + 8 + h
    for h in range(4):
        ga[64 + h] = COL["mg"] + 0 + h
        ga[96 + h] = COL["mg"] + 8 + h
        gb[64 + h] = COL["mg"] + 4 + h
        gb[96 + h] = COL["mg"] + 12 + h
    groups.append(g3 + ga + gb)
    groups.append(list(range(COL["mqk"], COL["mqk"] + 512)))
    groups.append(list(range(COL["rq"], COL["rq"] + 256)) + _rot_cols(COL["rq"]))
    groups.append(list(range(COL["rk"], COL["rk"] + 256)) + _rot_cols(COL["rk"]))
    for g in range(8):
        groups.append(list(range(COL["mix"] + g * 512, COL["mix"] + (g + 1) * 512)))
    return groups


def _inT_colmap():
    return [list(range(COL[n], COL[n] + 512)) for n in ("z", "rv", "rg", "mv", "mo")]


def _grp_kc(w, cols):
    cols = np.asarray(cols)
    safe = np.where(cols < 0, 0, cols)
    g = w[:, safe]
    if (cols < 0).any():
        g = g.copy()
        g[:, cols < 0] = 0.0
    k = w.shape[0] // 128
    return g.reshape(k, 128, len(cols)).transpose(1, 0, 2).reshape(128, k * len(cols))


def pack_layer_weights(inp, l):
    out = np.zeros((128, WPP), np.float32)
    for f in range(2):
        wu = inp["w_ffn_in"][l, f]
        o = OFF[("up", f)]
        for g in range(11):
            cols = list(range(256 * g, 256 * g + 256)) + list(range(DFF + 256 * g, DFF + 256 * g + 256))
            out[:, o + g * GE:o + (g + 1) * GE] = _grp_kc(wu, cols)
        wd = inp["w_ffn_out"][l, f]
        o = OFF[("dn", f)]
        for dc in range(8):
            out[:, o + dc * 2816:o + (dc + 1) * 2816] = _grp_kc(wd, list(range(dc * 128, dc * 128 + 128)))
    wi = inp["w_in"][l]
    for g, cols in enumerate(_inF_colmap()):
        out[:, OFF["inF"] + g * GE:OFF["inF"] + (g + 1) * GE] = _grp_kc(wi, cols)
    for g, cols in enumerate(_inT_colmap()):
        out[:, OFF["inT"] + g * GE:OFF["inT"] + (g + 1) * GE] = _grp_kc(wi, cols)
    for i in range(4):
        out[:, OFF["br"] + i * GE:OFF["br"] + (i + 1) * GE] = _grp_kc(inp["w_branch"][l, i], list(range(1024)))
    for g in range(2):
        out[:, OFF["out"] + g * GE:OFF["out"] + (g + 1) * GE] = _grp_kc(inp["w_out"][l], list(range(g * 512, g * 512 + 512)))
    gw = inp["lru_gate_w"][l]
    blk = np.zeros((128, 16, 128), np.float32)
    for d_ in range(2):
        for g_ in range(2):
            for ct in range(4):
                m = (d_ * 2 + g_) * 4 + ct
                for hl in range(2):
                    blk[hl * 64:(hl + 1) * 64, m, hl * 64:(hl + 1) * 64] = gw[d_, g_, ct * 2 + hl]
    out[:, OFF["lru"]:OFF["lru"] + 2048] = blk.reshape(128, 2048)
    return out


PV = {}
_p = 0
for nm, n in [("ln_g", 24), ("ln_b", 24), ("xa_w", 16), ("xa_b", 4), ("xbc_w", 24), ("xbc_b", 6),
              ("mqk_w", 16), ("mqk_b", 4), ("lru_gb", 16), ("lru_lam", 8), ("rbA", 1), ("rbB", 1), ("ralog", 1)]:
    PV[nm] = _p; _p += n
NPV = _p


def pack_pvec(inp, l):
    pv = np.zeros((128, NPV), np.float32)
    for i in range(3):
        pv[:, PV["ln_g"] + i * 8:PV["ln_g"] + i * 8 + 8] = inp["ln_g"][l, i].reshape(8, 128).T
        pv[:, PV["ln_b"] + i * 8:PV["ln_b"] + i * 8 + 8] = inp["ln_b"][l, i].reshape(8, 128).T
    for nm, key, nct in (("xa", "lru_conv", 4), ("xbc", "ssd_conv", 6), ("mqk", "mlstm_conv", 4)):
        w = inp[key + "_w"][l]
        b = inp[key + "_b"][l]
        for ct in range(nct):
            for k in range(4):
                pv[:, PV[nm + "_w"] + ct * 4 + k] = w[k, ct * 128:(ct + 1) * 128]
            pv[:, PV[nm + "_b"] + ct] = b[ct * 128:(ct + 1) * 128]
    for d_ in range(2):
        for g_ in range(2):
            for ct in range(4):
                pv[:, PV["lru_gb"] + (d_ * 2 + g_) * 4 + ct] = inp["lru_gate_b"][l, d_, g_, ct * 128:(ct + 1) * 128]
        for ct in range(4):
            pv[:, PV["lru_lam"] + d_ * 4 + ct] = inp["lru_lambda"][l, d_, ct * 128:(ct + 1) * 128]
    pv[0:8, PV["rbA"]] = inp["ssd_dt_bias"][l, 0]
    pv[32:40, PV["rbA"]] = inp["ssd_dt_bias"][l, 1]
    pv[64:68, PV["rbA"]] = inp["mlstm_gate_b"][l, 0, 0]
    pv[96:100, PV["rbA"]] = inp["mlstm_gate_b"][l, 1, 0]
    pv[64:68, PV["rbB"]] = inp["mlstm_gate_b"][l, 0, 1]
    pv[96:100, PV["rbB"]] = inp["mlstm_gate_b"][l, 1, 1]
    pv[0:8, PV["ralog"]] = inp["ssd_a_log"][l, 0]
    pv[32:40, PV["ralog"]] = inp["ssd_a_log"][l, 1]
    return pv


def pack_bvec(inp, l):
    bv = np.zeros((4, 512), np.float32)
    bv[0] = inp["ssd_norm_w"][l]
    bv[1] = inp["ret_norm_w"][l]
    bv[2] = inp["mlstm_norm_w"][l]
    bv[3] = np.repeat(inp["ssd_d"][l], 64)
    return bv


CST = {}
_c = 0
for nm, n in [("ident", 128), ("triU", 128), ("triL", 128), ("negF", 128), ("negB", 128), ("sel", 16 * 128),
              ("retW", 4 * 128), ("retS", 20), ("ones", 512), ("selT", 24), ("retG", 256)]:
    CST[nm] = _c; _c += n
NCST = _c
NEGV = -30000.0


def build_consts(tmax):
    c = np.zeros((128, NCST), np.float32)
    j = np.arange(128)[:, None]
    i = np.arange(128)[None, :]
    c[:, CST["ident"]:CST["ident"] + 128] = (j == i)
    c[:, CST["triU"]:CST["triU"] + 128] = (j <= i)
    c[:, CST["triL"]:CST["triL"] + 128] = (j >= i)
    c[:, CST["negF"]:CST["negF"] + 128] = np.where(j <= i, 0.0, NEGV)
    c[:, CST["negB"]:CST["negB"] + 128] = np.where(j >= i, 0.0, NEGV)
    for m in range(16):
        r = (m % 8) + (32 if m >= 8 else 0)
        c[r, CST["sel"] + m * 128:CST["sel"] + (m + 1) * 128] = 1.0
    lg = np.log1p(-np.exp2(-5.0 - np.arange(4, dtype=np.float64)))
    for h in range(4):
        c[:, CST["retW"] + h * 128:CST["retW"] + (h + 1) * 128] = 0.125 * np.exp(lg[h] * np.abs(i - j))
        t = np.arange(128, dtype=np.float64)
        c[:, CST["retS"] + 0 + h] = 0.125 * np.exp(lg[h] * (t + 1))
        c[:, CST["retS"] + 4 + h] = 0.125 * np.exp(lg[h] * (128 - t))
        c[:, CST["retS"] + 8 + h] = np.exp(lg[h] * (127 - t))
        c[:, CST["retS"] + 12 + h] = np.exp(lg[h] * t)
        c[:, CST["retS"] + 16 + h] = np.exp(lg[h] * 128)
    c[:, CST["ones"]:CST["ones"] + 512] = 1.0
    for n in range(16):
        c[(n % 8) + (32 if n >= 8 else 0), CST["selT"] + n] = 1.0
    for n2 in range(8):
        c[64 + 32 * (n2 // 4) + (n2 % 4), CST["selT"] + 16 + n2] = 1.0
    for h in range(4):
        r0 = (h % 2) * 64
        c0 = CST["retG"] + (h // 2) * 128
        c[r0:r0 + 64, c0:c0 + 128] = np.exp(lg[h] * 128)
    p = np.arange(128)
    d = p % 64
    inv = 10000.0 ** (-(2.0 * (d % 32)) / 64.0)
    ang = inv[:, None].astype(np.float32) * np.arange(tmax, dtype=np.float32)[None, :]
    cos = np.cos(ang).astype(np.float32)
    sin = np.sin(ang).astype(np.float32)
    sin_signed = np.where((d < 32)[:, None], -sin, sin).astype(np.float32)
    return c, cos, sin_signed


class Ev:
    __slots__ = ("sem", "val", "eng", "small")

    def __init__(self, sem, val, eng, small=False):
        self.sem, self.val, self.eng, self.small = sem, val, eng, small


class Tl:
    def __init__(self, ap, name):
        self.ap, self.name = ap, name
        self.w = None
        self.r = {}

    def __getitem__(self, idx):
        return V(self.ap[idx], self)

    def v(self):
        return V(self.ap, self)

    @property
    def tl(self):
        return self

    def rearrange(self, pat, **kw):
        return self.v().rearrange(pat, **kw)

    def bitcast(self, dt):
        return self.v().bitcast(dt)

    def bcast(self, axis, n):
        return self.v().bcast(axis, n)


class V:
    def __init__(self, ap, tl):
        self.ap, self.tl = ap, tl

    def __getitem__(self, idx):
        return V(self.ap[idx], self.tl)

    def bitcast(self, dt):
        return V(self.ap.bitcast(dt), self.tl)

    def rearrange(self, pat, **kw):
        return V(self.ap.rearrange(pat, **kw), self.tl)

    def bcast(self, axis, n):
        ap = self.ap.unsqueeze(axis)
        shp = list(ap.shape)
        shp[axis] = n
        return V(ap.broadcast_to(shp), self.tl)


def _ap(x):
    return x.ap if isinstance(x, (V, Tl)) else x


def _tl(x):
    return x.tl if isinstance(x, V) else (x if isinstance(x, Tl) else None)


EPOCH = 30000
RELAX_SAME_ENGINE = False
NDSEM = 40


class Sched:
    def __init__(self, nc, stack):
        self.nc = nc
        self.stack = stack
        self.h = {"pe": nc.tensor, "act": nc.scalar, "dve": nc.vector, "pool": nc.gpsimd, "sp": nc.sync}
        self.cnt = {e: 0 for e in ("pe", "act", "dve", "pool")}
        self.sems = {e: [] for e in ("pe", "act", "dve", "pool")}
        self.waited = {e: {} for e in self.h}
        self.dsem = {q: [[stack.enter_context(nc.semaphore(f"dq{q}{i}")), 0] for i in range(n)]
                     for q, n in (("sp", NDSEM), ("pool", 16), ("act", 8), ("cvt", 8))}
        self.drr = {"sp": 0, "pool": 0, "act": 0, "cvt": 0}
        self.last = {}
        self.ninstr = 0

    def _sem(self, eng, n):
        k = (n - 1) // EPOCH
        while len(self.sems[eng]) <= k:
            self.sems[eng].append(self.stack.enter_context(self.nc.semaphore(f"s_{eng}{len(self.sems[eng])}")))
        return self.sems[eng][k], (n - 1) % EPOCH + 1

    def _wait(self, eng, ev):
        if ev is None:
            return
        key = id(ev.sem)
        if self.waited[eng].get(key, 0) >= ev.val:
            return
        self.h[eng].wait_ge(ev.sem, ev.val)
        self.waited[eng][key] = ev.val

    def _deps(self, eng, reads, writes):
        evs = []
        for t in reads:
            if t.w is not None:
                evs.append(t.w)
        for t in writes:
            if t.w is not None:
                evs.append(t.w)
            evs.extend(t.r.values())
        for ev in evs:
            if ev.eng == eng and eng == "pe":
                continue
            if ev.eng == eng and RELAX_SAME_ENGINE and not ev.small and ev.eng in self.cnt:
                continue
            self._wait(eng, ev)

    def _update(self, ev, reads, writes, key):
        for t in writes:
            t.w = ev
            t.r = {}
        for t in reads:
            if t in writes:
                continue
            t.r[key] = ev

    def op(self, eng, fn, reads=(), writes=(), small=False):
        reads = [t for t in reads if t is not None]
        writes = [t for t in writes if t is not None]
        self._deps(eng, reads, writes)
        ins = fn(self.h[eng])
        self.cnt[eng] += 1
        sem, val = self._sem(eng, self.cnt[eng])
        ins.then_inc(sem, 1)
        ev = Ev(sem, val, eng, small)
        self.last[eng] = ev
        self._update(ev, reads, writes, eng)
        self.ninstr += 1
        return ev

    def dma(self, q, out, in_, reads=(), writes=(), key=None):
        reads = [t for t in list(reads) + [_tl(in_)] if t is not None]
        writes = [t for t in list(writes) + [_tl(out)] if t is not None]
        key = key or q
        pool_ = self.dsem[key]
        slot = pool_[self.drr[key] % len(pool_)]
        self.drr[key] += 1
        if slot[1] > 0:
            self._wait(q, Ev(slot[0], slot[1], "dma"))
        evs = []
        for t in reads:
            if t.w is not None:
                evs.append(t.w)
        for t in writes:
            if t.w is not None:
                evs.append(t.w)
            evs.extend(t.r.values())
        for ev in evs:
            self._wait(q, ev)
        ins = self.h[q].dma_start(out=_ap(out), in_=_ap(in_))
        slot[1] += 16
        ins.then_inc(slot[0], 16)
        ev = Ev(slot[0], slot[1], "dma")
        self._update(ev, reads, writes, ("dma", q, id(slot[0])))
        self.ninstr += 1
        return ev

    def barrier(self):
        evs = [ev for ev in self.last.values()]
        for k_, pool_ in self.dsem.items():
            if k_ == "cvt":
                continue
            for s in pool_:
                if s[1] > 0:
                    evs.append(Ev(s[0], s[1], "dma"))
        for e in self.h:
            for ev in evs:
                if ev.eng == e and e == "pe":
                    continue
                self._wait(e, ev)

    def act(self, out, in_, func, bias=None, scale=None, small=False, eng="act"):
        kw = {}
        if bias is not None:
            kw["bias"] = _ap(bias)
        if scale is not None:
            kw["scale"] = _ap(scale)
        return self.op(eng, lambda h: h.activation(out=_ap(out), in_=_ap(in_), func=func, **kw),
                       reads=[_tl(in_), _tl(bias), _tl(scale)], writes=[_tl(out)], small=small)

    def tt(self, out, in0, in1, op, small=False, eng="dve"):
        return self.op(eng, lambda h: h.tensor_tensor(out=_ap(out), in0=_ap(in0), in1=_ap(in1), op=op),
                       reads=[_tl(in0), _tl(in1)], writes=[_tl(out)], small=small)

    def ts(self, out, in0, s1, op0, s2=None, op1=None, small=False, eng="dve"):
        kw = {}
        if op1 is not None:
            kw["op1"] = op1
        return self.op(eng, lambda h: h.tensor_scalar(out=_ap(out), in0=_ap(in0), scalar1=_ap(s1), scalar2=_ap(s2),
                                                      op0=op0, **kw),
                       reads=[_tl(in0), _tl(s1), _tl(s2)], writes=[_tl(out)], small=small)

    def stt(self, out, in0, scalar, in1, op0, op1, small=False):
        return self.op("dve", lambda h: h.scalar_tensor_tensor(out=_ap(out), in0=_ap(in0), scalar=_ap(scalar),
                                                               in1=_ap(in1), op0=op0, op1=op1),
                       reads=[_tl(in0), _tl(scalar), _tl(in1)], writes=[_tl(out)], small=small)

    def copy(self, out, in_, eng="dve", small=False):
        if eng == "act":
            return self.act(out, in_, AF.Copy, small=small)
        return self.op(eng, lambda h: h.tensor_copy(out=_ap(out), in_=_ap(in_)),
                       reads=[_tl(in_)], writes=[_tl(out)], small=small)

    def memset(self, out, val, eng="dve", small=False):
        return self.op(eng, lambda h: h.memset(_ap(out), val), writes=[_tl(out)], small=small)

    def recip(self, out, in_, small=False):
        return self.op("dve", lambda h: h.reciprocal(out=_ap(out), in_=_ap(in_)),
                       reads=[_tl(in_)], writes=[_tl(out)], small=small)

    def scan(self, out, d0, d1, init, op0=ALU.mult, op1=ALU.add):
        return self.op("dve", lambda h: h.tensor_tensor_scan(out=_ap(out), data0=_ap(d0), data1=_ap(d1),
                                                             initial=_ap(init), op0=op0, op1=op1),
                       reads=[_tl(d0), _tl(d1), _tl(init)], writes=[_tl(out)])

    def reduce(self, out, in_, op, axis=AX.X, small=False):
        return self.op("dve", lambda h: h.tensor_reduce(out=_ap(out), in_=_ap(in_), axis=axis, op=op),
                       reads=[_tl(in_)], writes=[_tl(out)], small=small)

    def mm(self, out, pairs, extra_reads=()):
        reads = [_tl(a) for a, b in pairs] + [_tl(b) for a, b in pairs] + list(extra_reads)
        n = len(pairs)

        def fn(h):
            ins = None
            for k, (a, b) in enumerate(pairs):
                ins = h.matmul(_ap(out), _ap(a), _ap(b), start=(k == 0), stop=(k == n - 1))
            return ins
        self.ninstr += n - 1
        return self.op("pe", fn, reads=reads, writes=[_tl(out)])

    def mm_multi(self, groups, extra_reads=()):
        reads, writes = list(extra_reads), []
        for out, pairs in groups:
            writes.append(_tl(out))
            for a, b in pairs:
                reads += [_tl(a), _tl(b)]

        def fn(h):
            ins = None
            for out, pairs in groups:
                n = len(pairs)
                for k, (a, b) in enumerate(pairs):
                    ins = h.matmul(_ap(out), _ap(a), _ap(b), start=(k == 0), stop=(k == n - 1))
            return ins
        self.ninstr += sum(len(p) for _, p in groups) - 1
        return self.op("pe", fn, reads=reads, writes=writes)

    def mm_raw(self, items):
        reads, writes = [], []
        for o, a, b, st_, sp_ in items:
            writes.append(_tl(o)); reads += [_tl(a), _tl(b)]

        def fn(h):
            ins = None
            for o, a, b, st_, sp_ in items:
                ins = h.matmul(_ap(o), _ap(a), _ap(b), start=st_, stop=sp_)
            return ins
        self.ninstr += len(items) - 1
        return self.op("pe", fn, reads=reads, writes=writes)

    def transposes(self, items):
        reads, writes = [], []
        for o, i, idn in items:
            writes.append(_tl(o)); reads += [_tl(i), _tl(idn)]

        def fn(h):
            ins = None
            for o, i, idn in items:
                ins = h.transpose(_ap(o), _ap(i), _ap(idn))
            return ins
        self.ninstr += len(items) - 1
        return self.op("pe", fn, reads=reads, writes=writes)


class Arena:
    def __init__(self, nc, stack, nbytes):
        self.t = stack.enter_context(nc.sbuf_tensor("arena", [128, nbytes // 4], F32))
        self.n32 = nbytes // 4
        self.off = 0

    def alloc(self, shape, dt, name):
        n = int(np.prod(shape))
        isz = 4 if dt == F32 else 2
        n32 = (n * isz + 3) // 4
        n32 = (n32 + 15) // 16 * 16
        assert self.off + n32 <= self.n32, f"arena overflow at {name}: {self.off + n32} > {self.n32}"
        ap = self.t[:, self.off:self.off + n32]
        self.off += n32
        if dt != F32:
            ap = ap.bitcast(dt)
        ap = ap[:, 0:n]
        if len(shape) == 2:
            ap = ap.rearrange("p (a b) -> p a b", a=shape[0])
        elif len(shape) == 3:
            ap = ap.rearrange("p (a b c) -> p a b c", a=shape[0], b=shape[1])
        return Tl(ap, name)


class Prog:
    def __init__(self, Ts, depth, debug=False, phases=("P1", "LRU", "GATE", "SWB", "SWF", "P2")):
        self.Ts = list(Ts)
        self.depth = depth
        self.debug = debug
        self.phases = phases
        self.base = [0]
        for t in self.Ts:
            self.base.append(self.base[-1] + t)
        self.Ttot = self.base[-1]
        self.Tmax = max(self.Ts)
        self.nc = bass.Bass("TRN2", target_bir_lowering=False)
        self.stack = contextlib.ExitStack()
        nc = self.nc
        Tt = self.Ttot
        self.dr = {}

        def dram(name, shape, dt, kind):
            self.dr[name] = nc.dram_tensor(name, list(shape), dt, kind=kind).ap()
            return self.dr[name]
        dram("x_in", [Tt, D], F32, "ExternalInput")
        dram("y_out", [Tt, D], F32, "ExternalOutput")
        for l in range(depth):
            dram(f"wbig{l}", [128 * WPP // 2048, 2048], F32, "ExternalInput")
            dram(f"wb16_{l}", [128 * WPP // 2048, 2048], BF16, "Internal")
        dram("pvec", [depth, 128, NPV], F32, "ExternalInput")
        dram("bvec", [depth, 4, 512], F32, "ExternalInput")
        dram("cst", [128, NCST], F32, "ExternalInput")
        dram("rcos", [128, self.Tmax], F32, "ExternalInput")
        dram("rsin", [128, self.Tmax], F32, "ExternalInput")
        sk = "ExternalOutput" if debug else "Internal"
        dram("xT", [8, 128, Tt], F32, sk)
        dram("fm_xa", [4, 128, Tt], BF16, sk)
        dram("fm_ga", [4, 128, Tt], BF16, sk)
        dram("fm_xbc", [6, 128, Tt], BF16, sk)
        dram("fm_mqk", [4, 128, Tt], BF16, sk)
        dram("fm_rq", [2, 128, Tt], BF16, sk)
        dram("fm_rk", [2, 128, Tt], BF16, sk)
        dram("fm_mix", [32, 128, Tt], BF16, sk)
        dram("fm_gr", [2, 128, Tt], F32, sk)
        for n_ in ("z", "rv", "rg", "mv", "mo"):
            dram("tm_" + n_, [Tt, 512], BF16, sk)
        dram("fm_y", [16, 128, Tt], BF16, sk)
        dram("gq", [4, 128, Tt], F32, sk)
        dram("tmg", [Tt, 256], F32, sk)
        nch = Tt // CH
        dram("st_ssd", [nch, 128, 256], BF16, sk)
        dram("st_ret", [nch, 128, 256], BF16, sk)
        dram("st_ml", [nch, 128, 264], BF16, sk)
        dram("st_sc", [nch, 128, 8], F32, sk)
        self.drt = {}

    def dt_(self, *key):
        if key not in self.drt:
            self.drt[key] = Tl(None, str(key))
        return self.drt[key]

    def wview(self, l):
        return self.dr[f"wb16_{l}"].rearrange("(p a) b -> p (a b)", p=128)

    def build(self):
        nc = self.nc
        with self.stack:
            self.S = S = Sched(nc, self.stack)
            st = self.stack
            self.cst = Tl(st.enter_context(nc.sbuf_tensor("cst_sb", [128, NCST], F32))[:], "cst")
            self.cstb = Tl(st.enter_context(nc.sbuf_tensor("cstb_sb", [128, 512], BF16))[:], "cstb")
            self.pv = Tl(st.enter_context(nc.sbuf_tensor("pv_sb", [128, NPV], F32))[:], "pv")
            self.dpv = Tl(st.enter_context(nc.sbuf_tensor("dpv_sb", [128, 64], F32))[:], "dpv")
            self.bv = Tl(st.enter_context(nc.sbuf_tensor("bv_sb", [128, 4, 512], F32))[:], "bv")
            self.SC = Tl(st.enter_context(nc.sbuf_tensor("sc_sb", [128, 32, 24], F32))[:], "SC")
            self.lnb8 = Tl(st.enter_context(nc.sbuf_tensor("lnb8_sb", [128, 1], F32))[:], "lnb8")
            psum = st.enter_context(nc.psum_tensor("ps_all", [128, 8 * 512], F32))
            self.bank = [Tl(psum[:, k * 512:(k + 1) * 512], f"bank{k}") for k in range(8)]
            self.arena = Arena(nc, st, 150 * 1024)
            rows = 128 * WPP // 2048
            npc = 7
            self.cvt_rows = rows
            for i in range(npc):
                self.cvt_piece(0, i)
            S.dma("sp", self.cst, self.dr["cst"])
            S.copy(self.cstb[:, 0:128], self.cst[:, CST["ident"]:CST["ident"] + 128])
            S.copy(self.cstb[:, 128:256], self.cst[:, CST["ones"]:CST["ones"] + 128])
            S.copy(self.cstb[:, 256:384], self.cst[:, CST["triU"]:CST["triU"] + 128])
            S.copy(self.cstb[:, 384:512], self.cst[:, CST["triL"]:CST["triL"] + 128])
            S.memset(self.lnb8, math.log(8.0), small=True)
            self.epsb = Tl(st.enter_context(nc.sbuf_tensor("epsb_sb", [128, 1], F32))[:], "epsb")
            S.memset(self.epsb, EPS, small=True)
            self.identb = self.cstb[:, 0:128]
            self.onesb = self.cstb[:, 128:256]
            self.ident = self.cst[:, CST["ident"]:CST["ident"] + 128]
            self.wring_i = 0
            for l in range(self.depth):
                self.layer(l)
            S.barrier()
        return nc

    def cvt_piece(self, l, i):
        rows, npc = self.cvt_rows, 7
        step = rows // npc
        src = self.dr[f"wbig{l}"]
        dst = self.dr[f"wb16_{l}"]
        self.S.dma("pool", dst[i * step:(i + 1) * step, :], src[i * step:(i + 1) * step, :],
                   writes=[self.dt_("w", l, i)], key="cvt")

    def pcol(self, name, i=0):
        return self.pv[:, PV[name] + i:PV[name] + i + 1]

    def layer(self, l):
        S = self.S
        S.dma("sp", self.pv, self.dr["pvec"][l])
        S.dma("sp", self.bv, self.dr["bvec"][l].partition_broadcast(128))
        self.derive_params(l)
        nseq = len(self.Ts)
        if "P1" in self.phases:
            self.setup_P1()
            for s in range(nseq):
                for t0 in range(0, self.Ts[s], TT):
                    self.P1(l, s, t0)
        for s in range(nseq):
            if "LRU" in self.phases:
                self.LRU(l, s)
            if "GATE" in self.phases:
                self.GATE(l, s)
            if "SWB" in self.phases:
                self.SWEEP(l, s, "B")
            if "SWF" in self.phases:
                self.SWEEP(l, s, "F")
        if "P2" in self.phases:
            self.setup_P2()
            ti = 0
            for s in range(nseq):
                for t0 in range(0, self.Ts[s], TT):
                    if l + 1 < self.depth and ti < 7:
                        self.cvt_piece(l + 1, ti)
                    ti += 1
                    self.P2(l, s, t0)
            if l + 1 < self.depth:
                for i in range(min(ti, 7), 7):
                    self.cvt_piece(l + 1, i)

    def new_phase(self):
        self.S.barrier()
        self.arena.off = 0
        for b in self.bank:
            b.w = None
            b.r = {}

    def derive_params(self, l):
        S = self.S
        dp = self.dpv
        lam = self.pv[:, PV["lru_lam"]:PV["lru_lam"] + 8]
        S.act(dp[:, 32:40], lam, AF.Exp, scale=-1.0, small=True)
        S.act(dp[:, 32:40], dp[:, 32:40], AF.Ln, bias=1.0, small=True)
        S.ts(dp[:, 0:8], dp[:, 32:40], -4.0, ALU.mult, small=True)
        S.ts(dp[:, 8:16], dp[:, 32:40], -8.0, ALU.mult, small=True)
        S.ts(dp[:, 16:32], self.pv[:, PV["lru_gb"]:PV["lru_gb"] + 16], 0.5, ALU.mult, small=True)
        S.act(dp[:, 40:41], self.pcol("ralog"), AF.Exp, small=True)
        S.ts(dp[:, 40:41], dp[:, 40:41], -1.0, ALU.mult, small=True)
        S.ts(dp[:, 41:42], self.pcol("rbB"), -1.0, ALU.mult, small=True)
        if not hasattr(self, "dpv2"):
            self.dpv2 = Tl(self.stack.enter_context(self.nc.sbuf_tensor("dpv2_sb", [128, 64], F32))[:], "dpv2")
        d2 = self.dpv2
        S.ts(d2[:, 0:24], self.pv[:, PV["xbc_w"]:PV["xbc_w"] + 24], 0.5, ALU.mult, small=True)
        S.ts(d2[:, 24:30], self.pv[:, PV["xbc_b"]:PV["xbc_b"] + 6], 0.5, ALU.mult, small=True)
        S.ts(d2[:, 30:46], self.pv[:, PV["mqk_w"]:PV["mqk_w"] + 16], 0.5, ALU.mult, small=True)
        S.ts(d2[:, 46:50], self.pv[:, PV["mqk_b"]:PV["mqk_b"] + 4], 0.5, ALU.mult, small=True)

    def wload(self, l, off, n, q="sp"):
        w = self.wring[self.wring_i % len(self.wring)]
        self.wring_i += 1
        self.S.dma(q, w[:, 0:n], self.wview(l)[:, off:off + n], reads=[self.dt_("w", l, i) for i in range(7)])
        return w

    def alloc_common(self, nw):
        A = self.arena
        self.X32 = A.alloc([8, TT], F32, "X32")
        self.Xb = A.alloc([8, TT], BF16, "Xb")
        self.SQ = A.alloc([8, TT], BF16, "SQ")
        self.G = A.alloc([22, TT], BF16, "G")
        self.wring = [A.alloc([GE], BF16, f"W{i}") for i in range(nw)]
        self.lnt = [A.alloc([TT], F32, f"lnt{i}") for i in range(3)]
        self.sgt = [A.alloc([TT], BF16, f"sg{i}") for i in range(2)]
        self.pset = 0

    def banks4(self):
        b = self.bank[4 * self.pset:4 * self.pset + 4]
        self.pset ^= 1
        return b

    def setup_P1(self):
        self.new_phase()
        A = self.arena
        self.alloc_common(3)
        self.stg = [A.alloc([4, TT], BF16, f"stg{i}") for i in range(2)]
        self.stg32 = A.alloc([2, TT], F32, "stg32")
        self.rcos = A.alloc([TT], F32, "rcos")
        self.rsin = A.alloc([TT], F32, "rsin")
        self.tmpf = [A.alloc([TT], F32, f"tmpf{i}") for i in range(3)]
        self.xin = [A.alloc([D], F32, f"xin{i}") for i in range(2)]
        self.stg_i = 0

    def setup_P2(self):
        self.new_phase()
        A = self.arena
        self.alloc_common(2)
        self.Y = A.alloc([16, TT], BF16, "Y")
        self.M32 = A.alloc([8, TT], F32, "M32")
        self.GT = [A.alloc([8, TT], BF16, f"GT{i}") for i in range(2)]
        self.tmpf = [A.alloc([TT], F32, f"tmpf{i}") for i in range(2)]
        self.outt = [A.alloc([D], F32, f"outt{i}") for i in range(2)]

    def ffn(self, l, f):
        S = self.S
        c = 0.5 / DN_ALPHA
        for g in range(11):
            W = self.wload(l, OFF[("up", f)] + g * GE, GE)
            Wv = W.v().rearrange("p (k j) -> p k j", k=8)
            bk = self.banks4()
            S.mm_multi([(bk[cc], [(Wv[:, kc, cc * 128:(cc + 1) * 128], self.Xb[:, kc, :]) for kc in range(8)])
                        for cc in range(4)])
            for p in range(2):
                sg = self.sgt[p]
                S.act(sg, bk[p], AF.Silu)
                S.tt(self.G[:, 2 * g + p, :], sg, bk[2 + p], ALU.mult)
        for half in range(2):
            bk = self.banks4()
            for q in range(4):
                dc = half * 4 + q
                W = self.wload(l, OFF[("dn", f)] + dc * 2816, 2816)
                Wv = W[:, 0:2816].rearrange("p (k j) -> p k j", k=22)
                S.mm(bk[q], [(Wv[:, fc, :], self.G[:, fc, :]) for fc in range(22)])
                S.stt(self.X32[:, dc, :], bk[q], c, self.X32[:, dc, :], ALU.mult, ALU.add)

    def layernorm(self, i):
        S = self.S
        epsp = EPS / (DN_ALPHA ** 2)
        S.act(self.Xb, self.X32, AF.Copy)
        S.act(self.SQ, self.X32, AF.Square)
        bk = self.banks4()
        S.mm(bk[0], [(self.onesb, self.Xb[:, kc, :]) for kc in range(8)])
        S.mm(bk[1], [(self.onesb, self.SQ[:, kc, :]) for kc in range(8)])
        mean, t1, rstd = self.lnt
        S.act(mean, bk[0], AF.Identity, scale=1.0 / D)
        S.act(t1, bk[0], AF.Square, scale=1.0 / D)
        S.stt(t1, bk[1], 1.0 / D, t1, ALU.mult, ALU.subtract)
        S.ts(t1, t1, epsp, ALU.add)
        S.act(t1, t1, AF.Ln)
        S.act(rstd, t1, AF.Exp, scale=-0.5)
        S.tt(self.X32, self.X32, mean.v().bcast(1, 8), ALU.subtract)
        S.tt(self.X32, self.X32, rstd.v().bcast(1, 8), ALU.mult)
        for kc in range(8):
            S.act(self.X32[:, kc, :], self.X32[:, kc, :], AF.Identity,
                  scale=self.pcol("ln_g", i * 8 + kc), bias=self.pcol("ln_b", i * 8 + kc))
        S.copy(self.Xb, self.X32, eng="dve")

    def load_x(self, l, g0):
        S = self.S
        if l == 0:
            for tc in range(TT // 128):
                xi = self.xin[tc % 2]
                S.dma("sp", xi, self.dr["x_in"][g0 + tc * 128:g0 + (tc + 1) * 128, :])
                bk = self.banks4()
                for hb in range(2):
                    S.transposes([(bk[hb][:, q * 128:(q + 1) * 128], xi[:, (hb * 4 + q) * 128:(hb * 4 + q + 1) * 128],
                                   self.ident) for q in range(4)])
                    S.copy(self.X32[:, hb * 4:hb * 4 + 4, tc * 128:(tc + 1) * 128],
                           bk[hb].v().rearrange("p (q t) -> p q t", q=4), eng=("act" if hb else "dve"))
        else:
            S.dma("sp", self.X32, self.dr["xT"][:, :, g0:g0 + TT].rearrange("k p t -> p k t"),
                  reads=[self.dt_("xT", g0)])

    def store_x(self, l, g0, final):
        S = self.S
        if final:
            for tc in range(TT // 128):
                ot = self.outt[tc % 2]
                bk = self.banks4()
                for hb in range(2):
                    S.transposes([(bk[hb][:, q * 128:(q + 1) * 128], self.X32[:, hb * 4 + q, tc * 128:(tc + 1) * 128],
                                   self.ident) for q in range(4)])
                    S.copy(ot[:, hb * 512:(hb + 1) * 512], bk[hb], eng=("act" if hb else "dve"))
                S.dma("pool", self.dr["y_out"][g0 + tc * 128:g0 + (tc + 1) * 128, :], ot)
        else:
            S.dma("pool", self.dr["xT"][:, :, g0:g0 + TT].rearrange("k p t -> p k t"), self.X32,
                  writes=[self.dt_("xT", g0)])

    def stage(self):
        s = self.stg[self.stg_i % 2]
        self.stg_i += 1
        return s

    def P1(self, l, s, t0):
        S = self.S
        g0 = self.base[s] + t0
        self.load_x(l, g0)
        S.copy(self.Xb, self.X32, eng="act")
        self.ffn(l, 0)
        self.layernorm(0)
        S.dma("pool", self.dr["xT"][:, :, g0:g0 + TT].rearrange("k p t -> p k t"), self.X32,
              writes=[self.dt_("xT", g0)])
        S.dma("pool", self.rcos, self.dr["rcos"][:, t0:t0 + TT])
        S.dma("pool", self.rsin, self.dr["rsin"][:, t0:t0 + TT])
        fmdst = {0: ("fm_xa", 0), 1: ("fm_ga", 0), 2: ("fm_xbc", 0), 4: ("fm_mqk", 0)}
        for g in range(15):
            W = self.wload(l, OFF["inF"] + g * GE, GE)
            Wv = W.v().rearrange("p (k j) -> p k j", k=8)
            bk = self.banks4()
            S.mm_multi([(bk[cc], [(Wv[:, kc, cc * 128:(cc + 1) * 128], self.Xb[:, kc, :]) for kc in range(8)])
                        for cc in range(4)])
            if g in (0, 2, 4):
                stg = self.stage()
                for cc in range(4):
                    S.copy(stg[:, cc, :], bk[cc], eng=("act" if cc % 2 else "dve"))
                nm, c0 = fmdst[g]
                S.dma("pool", self.dr[nm][c0:c0 + 4, :, g0:g0 + TT].rearrange("k p t -> p k t"), stg,
                      writes=[self.dt_(nm, s)])
            elif g == 1:
                stg = self.stage()
                for cc in range(4):
                    a, b, c_ = self.tmpf
                    S.act(a, bk[cc], AF.Square)
                    S.ts(a, a, 0.044715, ALU.mult, 1.0, ALU.add)
                    S.tt(a, a, bk[cc], ALU.mult)
                    S.act(b, a, AF.Tanh, scale=0.7978845608028654)
                    S.act(c_, bk[cc], AF.Copy, scale=0.5)
                    S.stt(stg[:, cc, :], b, 1.0, c_, ALU.add, ALU.mult)
                S.dma("pool", self.dr["fm_ga"][:, :, g0:g0 + TT].rearrange("k p t -> p k t"), stg,
                      writes=[self.dt_("fm_ga", s)])
            elif g == 3:
                stg = self.stage()
                S.copy(stg[:, 0, :], bk[0], eng="act")
                S.copy(stg[:, 1, :], bk[1], eng="dve")
                S.dma("pool", self.dr["fm_xbc"][4:6, :, g0:g0 + TT].rearrange("k p t -> p k t"), stg[:, 0:2, :],
                      writes=[self.dt_("fm_xbc", s)])
                S.copy(self.stg32[:, 0, :], bk[2], eng="act")
                S.copy(self.stg32[:, 1, :], bk[3], eng="dve")
                S.dma("pool", self.dr["fm_gr"][:, :, g0:g0 + TT].rearrange("k p t -> p k t"), self.stg32,
                      writes=[self.dt_("fm_gr", s)])
            elif g in (5, 6):
                stg = self.stage()
                for cc in range(2):
                    a, b, _ = self.tmpf
                    S.tt(a, bk[cc + 2], self.rsin, ALU.mult)
                    S.tt(b, bk[cc], self.rcos, ALU.mult)
                    S.tt(stg[:, cc, :], a, b, ALU.add)
                nm = "fm_rq" if g == 5 else "fm_rk"
                S.dma("pool", self.dr[nm][:, :, g0:g0 + TT].rearrange("k p t -> p k t"), stg[:, 0:2, :],
                      writes=[self.dt_(nm, s)])
            else:
                gi = g - 7
                stg = self.stage()
                for cc in range(4):
                    a = self.tmpf[cc % 3]
                    S.act(a, bk[cc], AF.Tanh, scale=0.5)
                    S.ts(stg[:, cc, :], a, 0.5, ALU.mult, 0.5, ALU.add)
                S.dma("pool", self.dr["fm_mix"][gi * 4:gi * 4 + 4, :, g0:g0 + TT].rearrange("k p t -> p k t"), stg,
                      writes=[self.dt_("fm_mix", s)])
        for g, nm in enumerate(("z", "rv", "rg", "mv", "mo")):
            W = self.wload(l, OFF["inT"] + g * GE, GE)
            Wv = W.v().rearrange("p (k j) -> p k j", k=8)
            bk = self.banks4()
            S.mm_multi([(bk[tc], [(self.Xb[:, kc, tc * 128:(tc + 1) * 128], Wv[:, kc, :]) for kc in range(8)])
                        for tc in range(4)])
            stg = self.stage()
            for tc in range(4):
                if nm in ("z", "rg"):
                    S.act(stg[:, tc, :], bk[tc], AF.Silu)
                elif nm == "mo":
                    a = self.tmpf[tc % 3]
                    S.act(a, bk[tc], AF.Tanh, scale=0.5)
                    S.ts(stg[:, tc, :], a, 0.5, ALU.mult, 0.5, ALU.add)
                else:
                    S.copy(stg[:, tc, :], bk[tc], eng=("act" if tc % 2 else "dve"))
            S.dma("pool", self.dr["tm_" + nm][g0:g0 + TT, :].rearrange("(c p) j -> p c j", p=128), stg,
                  writes=[self.dt_("tm_" + nm, s)])

    def P2(self, l, s, t0):
        S = self.S
        g0 = self.base[s] + t0
        final = (l == self.depth - 1)
        S.dma("sp", self.Y, self.dr["fm_y"][:, :, g0:g0 + TT].rearrange("k p t -> p k t"),
              reads=[self.dt_("fm_y", s)])
        S.dma("sp", self.X32, self.dr["xT"][:, :, g0:g0 + TT].rearrange("k p t -> p k t"),
              reads=[self.dt_("xT", g0)])
        for i in range(4):
            W = self.wload(l, OFF["br"] + i * GE, GE)
            Wv = W.v().rearrange("p (k j) -> p k j", k=4)
            GT = self.GT[i % 2]
            S.dma("pool", GT, self.dr["fm_mix"][i * 8:i * 8 + 8, :, g0:g0 + TT].rearrange("k p t -> p k t"),
                  reads=[self.dt_("fm_mix", s)])
            for half in range(2):
                bk = self.banks4()
                S.mm_multi([(bk[q], [(Wv[:, kc, (half * 4 + q) * 128:(half * 4 + q + 1) * 128],
                                      self.Y[:, i * 4 + kc, :]) for kc in range(4)]) for q in range(4)])
                for q in range(4):
                    dc = half * 4 + q
                    if i == 0:
                        S.tt(self.M32[:, dc, :], bk[q], GT[:, dc, :], ALU.mult)
                    else:
                        a = self.tmpf[q % 2]
                        S.tt(a, bk[q], GT[:, dc, :], ALU.mult)
                        S.tt(self.M32[:, dc, :], self.M32[:, dc, :], a, ALU.add)
        S.act(self.SQ, self.M32, AF.Copy)
        for g in range(2):
            W = self.wload(l, OFF["out"] + g * GE, GE)
            Wv = W.v().rearrange("p (k j) -> p k j", k=8)
            bk = self.banks4()
            S.mm_multi([(bk[cc], [(Wv[:, kc, cc * 128:(cc + 1) * 128], self.SQ[:, kc, :]) for kc in range(8)])
                        for cc in range(4)])
            for cc in range(4):
                dc = g * 4 + cc
                S.stt(self.X32[:, dc, :], bk[cc], 1.0 / DN_ALPHA, self.X32[:, dc, :], ALU.mult, ALU.add)
        self.layernorm(1)
        self.ffn(l, 1)
        self.layernorm(2)
        self.store_x(l, g0, final)


def _LRU(self, l, s):
    S = self.S
    self.new_phase()
    A = self.arena
    T = self.Ts[s]
    b0 = self.base[s]
    XP = A.alloc([T + 4], BF16, "XP")
    XC = A.alloc([T], F32, "XC")
    XCb = A.alloc([T], BF16, "XCb")
    A1 = A.alloc([T], F32, "A1")
    A2 = A.alloc([T], F32, "A2")
    B1 = A.alloc([T], F32, "B1")
    HF = A.alloc([T], F32, "HF")
    HB = A.alloc([T], F32, "HB")
    GA = A.alloc([T], BF16, "GA")
    YA = A.alloc([T], BF16, "YA")
    LW = A.alloc([2048], BF16, "LW")
    S.dma("sp", LW, self.wview(l)[:, OFF["lru"]:OFF["lru"] + 2048], reads=[self.dt_("w", l, i) for i in range(7)])
    LWv = LW.v().rearrange("p (m j) -> p m j", m=16)
    S.memset(XP[:, 0:2], 0.0, small=True)
    S.memset(XP[:, T + 2:T + 4], 0.0, small=True)
    nb = 0
    for ct in range(4):
        S.dma("sp", XP[:, 2:2 + T], self.dr["fm_xa"][ct, :, b0:b0 + T], reads=[self.dt_("fm_xa", s)])
        S.dma("pool", GA, self.dr["fm_ga"][ct, :, b0:b0 + T], reads=[self.dt_("fm_ga", s)])
        S.ts(XC, XP[:, 0:T], self.pcol("xa_w", ct * 4), ALU.mult, self.pcol("xa_b", ct), ALU.add)
        for k in range(1, 4):
            S.stt(XC, XP[:, k:k + T], self.pcol("xa_w", ct * 4 + k), XC, ALU.mult, ALU.add)
        S.act(XCb, XC, AF.Copy)
        for d_ in range(2):
            for gate, dst in ((0, A1), (1, B1)):
                m = (d_ * 2 + gate) * 4 + ct
                hb = self.dpv[:, 16 + m:17 + m]
                for c0 in range(0, T, 512):
                    bk = self.bank[nb % 8]; nb += 1
                    S.mm(bk, [(LWv[:, m, :], XCb[:, c0:c0 + 512])])
                    S.act(dst[:, c0:c0 + 512], bk, AF.Tanh, scale=0.5, bias=hb)
            ch = self.dpv[:, d_ * 4 + ct:d_ * 4 + ct + 1]
            cf = self.dpv[:, 8 + d_ * 4 + ct:8 + d_ * 4 + ct + 1]
            S.act(A2, A1, AF.Exp, scale=cf, bias=cf)
            S.act(A1, A1, AF.Exp, scale=ch, bias=ch)
            S.act(A2, A2, AF.Sqrt, scale=-0.25, bias=0.25)
            S.stt(B1, B1, 1.0, XC, ALU.add, ALU.mult)
            S.tt(B1, B1, A2, ALU.mult)
            if d_ == 0:
                S.scan(HF, A1, B1, 0.0)
            else:
                S.scan(HB[:, ::-1], A1[:, ::-1], B1[:, ::-1], 0.0)
        S.tt(HF, HF, HB, ALU.add)
        S.tt(YA, HF, GA, ALU.mult)
        S.dma("sp", self.dr["fm_y"][ct, :, b0:b0 + T], YA, writes=[self.dt_("fm_y", s)])


def _GATE(self, l, s):
    S = self.S
    self.new_phase()
    A = self.arena
    T = self.Ts[s]
    b0 = self.base[s]
    nch = T // CH
    GA = A.alloc([T], F32, "gGA")
    GB = A.alloc([T], F32, "gGB")
    U = [A.alloc([T], F32, f"gU{i}") for i in range(5)]
    SM = A.alloc([8, 32], F32, "gSM")
    R1 = A.alloc([nch * 24], F32, "gR1")
    STG = [A.alloc([512], F32, f"gST{i}") for i in range(2)]
    ones = self.cst[:, CST["ones"]:CST["ones"] + 128]
    S.dma("sp", GA, self.dr["fm_gr"][0, :, b0:b0 + T], reads=[self.dt_("fm_gr", s)])
    S.dma("sp", GB, self.dr["fm_gr"][1, :, b0:b0 + T], reads=[self.dt_("fm_gr", s)])
    DT, LNDT, CUM, TA, TB = U
    lo = slice(0, 64)
    hi = slice(64, 128)
    S.act(DT[lo], GA[lo], AF.Exp, bias=self.pv[lo, PV["rbA"]:PV["rbA"] + 1])
    S.act(DT[lo], DT[lo], AF.Ln, bias=1.0)
    S.act(LNDT[lo], DT[lo], AF.Ln)
    S.ts(DT[lo], DT[lo], self.dpv[lo, 40:41], ALU.mult)
    for c in range(nch):
        cs = slice(c * CH, (c + 1) * CH)
        S.scan(CUM[0:32, cs], ones[0:32, :], DT[0:32, cs], 0.0)
        S.scan(CUM[32:64, cs][:, ::-1], ones[32:64, :], DT[32:64, cs][:, ::-1], 0.0)
    S.tt(LNDT[lo], LNDT[lo], CUM[lo], ALU.subtract)
    S.act(TA[lo], CUM[lo], AF.Exp)
    c3 = CUM.v().rearrange("p (c t) -> p c t", t=CH)
    S.copy(SM[0:32, 0, 0:nch], c3[0:32, :, CH - 1], small=True)
    S.copy(SM[32:64, 0, 0:nch], c3[32:64, :, 0], small=True)
    S.act(SM[lo, 1, 0:nch], SM[lo, 0, 0:nch], AF.Exp, small=True)
    S.tt(TB.v().rearrange("p (c t) -> p c t", t=CH)[lo], LNDT.v().rearrange("p (c t) -> p c t", t=CH)[lo],
         SM[lo, 0, 0:nch].bcast(2, CH), ALU.add)
    S.act(TB[lo], TB[lo], AF.Exp)
    S.dma("sp", self.dr["gq"][0, 0:64, b0:b0 + T], CUM[lo], writes=[self.dt_("gq", s)])
    S.dma("sp", self.dr["gq"][1, 0:64, b0:b0 + T], LNDT[lo], writes=[self.dt_("gq", s)])
    S.act(DT[hi], GB[hi], AF.Exp, scale=-1.0, bias=self.dpv[hi, 41:42])
    S.act(DT[hi], DT[hi], AF.Ln, bias=1.0)
    NB = GB
    o1f = self.cst[64:96, CST["ones"]:CST["ones"] + 1]
    o1b = self.cst[96:128, CST["ones"]:CST["ones"] + 1]
    S.scan(NB[64:96, :], V(o1f.ap.broadcast_to([32, T]), o1f.tl), DT[64:96, :], 0.0)
    S.scan(NB[96:128, :][:, ::-1], V(o1b.ap.broadcast_to([32, T]), o1b.tl), DT[96:128, :][:, ::-1], 0.0)
    UG = LNDT
    S.ts(UG[hi], GA[hi], self.pv[hi, PV["rbA"]:PV["rbA"] + 1], ALU.add)
    S.tt(UG[hi], UG[hi], NB[hi], ALU.add)
    S.reduce(SM[hi, 2, 0:nch], UG.v().rearrange("p (c t) -> p c t", t=CH)[hi], ALU.max, small=True)
    S.scan(SM[64:96, 3, 0:nch], SM[64:96, 2, 0:nch], SM[64:96, 2, 0:nch], 0.0, op0=ALU.max, op1=ALU.max)
    S.scan(SM[96:128, 3, 0:nch][:, ::-1], SM[96:128, 2, 0:nch][:, ::-1], SM[96:128, 2, 0:nch][:, ::-1], 0.0,
           op0=ALU.max, op1=ALU.max)
    S.memset(SM[hi, 4, :], 0.0, small=True)
    if nch > 1:
        S.copy(SM[64:96, 4, 1:nch], SM[64:96, 3, 0:nch - 1], small=True)
        S.copy(SM[96:128, 4, 0:nch - 1], SM[96:128, 3, 1:nch], small=True)
    S.tt(SM[hi, 5, 0:nch], SM[hi, 4, 0:nch], SM[hi, 3, 0:nch], ALU.subtract, small=True)
    S.act(SM[hi, 5, 0:nch], SM[hi, 5, 0:nch], AF.Exp, small=True)
    mgb = SM[hi, 3, 0:nch].bcast(2, CH)
    S.tt(TA.v().rearrange("p (c t) -> p c t", t=CH)[hi], UG.v().rearrange("p (c t) -> p c t", t=CH)[hi], mgb,
         ALU.subtract)
    S.act(TA[hi], TA[hi], AF.Exp)
    S.tt(TB.v().rearrange("p (c t) -> p c t", t=CH)[hi], NB.v().rearrange("p (c t) -> p c t", t=CH)[hi], mgb,
         ALU.subtract)
    S.act(TB[hi], TB[hi], AF.Exp, bias=self.lnb8[hi])
    selT = self.cst[:, CST["selT"]:CST["selT"] + 24]
    r1 = R1.v().rearrange("p (c n) -> p c n", n=24)
    S.tt(r1[lo, :, 0:16], SM[lo, 1, 0:nch].bcast(2, 16), selT[lo, 0:16].bcast(1, nch), ALU.mult)
    S.tt(r1[hi, :, 16:24], SM[hi, 5, 0:nch].bcast(2, 8), selT[hi, 16:24].bcast(1, nch), ALU.mult)
    bk = self.bank[0]
    o128 = self.cst[:, CST["ones"]:CST["ones"] + 128]
    S.memset(r1[lo, :, 16:24], 0.0)
    S.memset(r1[hi, :, 0:16], 0.0)
    for c0 in range(0, nch, 16):
        c1 = min(nch, c0 + 16)
        bkx = self.bank[(c0 // 16) % 2 * 5]
        S.mm(bkx[:, 0:(c1 - c0) * 24], [(o128, R1[:, c0 * 24:c1 * 24])])
        S.copy(self.SC[:, c0:c1, :], bkx.v()[:, 0:(c1 - c0) * 24].rearrange("p (c n) -> p c n", n=24))
    for c in range(nch):
        bk = self.bank[1 + c % 4]
        cs = slice(c * CH, (c + 1) * CH)
        S.transposes([(bk[:, 0:128], TA[:, cs], self.ident), (bk[:, 128:256], TB[:, cs], self.ident)])
        st = STG[c % 2]
        S.copy(st[:, 0:256], bk[:, 0:256], eng=("act" if c % 2 else "dve"))
        S.dma("sp", self.dr["tmg"][b0 + c * CH:b0 + (c + 1) * CH, :], st[:, 0:256], writes=[self.dt_("tmg", s)])


Prog.LRU = _LRU
Prog.GATE = _GATE


SEG = 512


def _nbk(self):
    b = self.bank[self._bki % 8]
    self._bki += 1
    return b


def _conv_silu(self, dst, XP, XH, TH, wcol0, bcol, n):
    S = self.S
    d2 = self.dpv2
    S.ts(XH, XP[:, 0:n], d2[:, wcol0:wcol0 + 1], ALU.mult, d2[:, bcol:bcol + 1], ALU.add)
    for k in range(1, 4):
        S.stt(XH, XP[:, k:k + n], d2[:, wcol0 + k:wcol0 + k + 1], XH, ALU.mult, ALU.add)
    S.act(TH, XH, AF.Tanh)
    S.stt(dst, TH, 1.0, XH, ALU.add, ALU.mult)


def _load_halo(self, XP, name, ct, s, t0, T):
    S = self.S
    b0 = self.base[s]
    lo = max(t0 - 2, 0)
    hi = min(t0 + SEG + 1, T)
    if t0 == 0:
        S.memset(XP[:, 0:2], 0.0, small=True)
    if t0 + SEG == T:
        S.memset(XP[:, SEG + 2:SEG + 4], 0.0, small=True)
    S.dma("sp", XP[:, lo - (t0 - 2):hi - (t0 - 2)], self.dr[name][ct, :, b0 + lo:b0 + hi], reads=[self.dt_(name, s)])


def _headnorm(self, Y, STATS, MV2, TMP4, nwi, gate, OUTb):
    S = self.S
    for h in range(4):
        S.op("dve", lambda e, h=h: e.bn_stats(out=_ap(STATS[:, h, :]), in_=_ap(Y[:, h, :])),
             reads=[Y.tl], writes=[STATS.tl], small=True)
    for h in range(4):
        S.op("dve", lambda e, h=h: e.bn_aggr(out=_ap(MV2[:, h, :]), in_=_ap(STATS[:, h, :])),
             reads=[STATS.tl], writes=[MV2.tl], small=True)
    S.act(TMP4, MV2[:, :, 1], AF.Ln, bias=self.epsb, small=True)
    S.act(TMP4, TMP4, AF.Exp, scale=-0.5, small=True)
    for h in range(4):
        S.ts(Y[:, h, :], Y[:, h, :], MV2[:, h, 0:1], ALU.subtract, TMP4[:, h:h + 1], ALU.mult)
    Yf = Y.rearrange("p h v -> p (h v)")
    S.tt(Yf, Yf, self.bv[:, nwi, :], ALU.mult)
    S.tt(OUTb, Yf, gate, ALU.mult)


def _SWEEP(self, l, s, sw):
    S = self.S
    self.new_phase()
    A = self.arena
    T = self.Ts[s]
    b0 = self.base[s]
    nseg = T // SEG
    F = (sw == "F")
    dn = 0 if F else 1
    self._bki = 0
    HS = A.alloc([256], F32, "HS"); HR = A.alloc([2, 128], F32, "HR"); HM = A.alloc([2, 132], F32, "HM")
    for t_ in (HS, HR, HM):
        S.memset(t_, 0.0)
    if F:
        HSx = [A.alloc([2, 256], BF16, f"HSx{i}") for i in range(2)]
        HRx = [A.alloc([2, 2, 128], BF16, f"HRx{i}") for i in range(2)]
        HMx = [A.alloc([2, 2, 132], BF16, f"HMx{i}") for i in range(2)]
        HSxb = [A.alloc([2, 256], BF16, f"HSxb{i}") for i in range(2)]
        HRxb = [A.alloc([2, 2, 128], BF16, f"HRxb{i}") for i in range(2)]
        HMxb = [A.alloc([2, 2, 132], BF16, f"HMxb{i}") for i in range(2)]
        for t_ in HSx + HRx + HMx + HSxb + HRxb + HMxb:
            S.memset(t_, 0.0)
    else:
        HSb = [A.alloc([256], BF16, f"HSb{i}") for i in range(2)]
        HRb = [A.alloc([2, 128], BF16, f"HRb{i}") for i in range(2)]
        HMb = [A.alloc([2, 132], BF16, f"HMb{i}") for i in range(2)]
        for t_ in HSb + HRb + HMb:
            S.memset(t_, 0.0)
    XP = [A.alloc([SEG + 4], BF16, f"sXP{i}") for i in range(2)]
    XH = A.alloc([SEG], F32, "sXH"); TH = A.alloc([SEG], F32, "sTH")
    XBC = A.alloc([6, SEG], BF16, "sXBC"); MQK = A.alloc([4, SEG], BF16, "sMQK")
    RK = A.alloc([2, SEG], BF16, "sRK")
    TMG = A.alloc([4, 256], F32, "sTMG")
    RV = A.alloc([4, 512], BF16, "sRV"); MV = A.alloc([4, 512], BF16, "sMV")
    TMA = A.alloc([896], BF16, "sTMA"); TMB = A.alloc([256], BF16, "sTMB")
    WV = A.alloc([512], BF16, "sWV"); WVR = A.alloc([512], BF16, "sWVR"); EV = A.alloc([4, 132], BF16, "sEV")
    if F:
        RQ = A.alloc([2, SEG], BF16, "sRQ")
        CMX = A.alloc([2, SEG], BF16, "sCMX"); MQX = A.alloc([2, 2, SEG], BF16, "sMQX"); RQX = A.alloc([2, 2, SEG], BF16, "sRQX")
        for t_ in (CMX, MQX, RQX):
            S.memset(t_, 0.0)
        CUM = A.alloc([SEG], F32, "sCUM"); CB = A.alloc([SEG], F32, "sCB")
        S.memset(CUM, 0.0); S.memset(CB, 0.0)
        ZG = A.alloc([4, 512], BF16, "sZ"); RG = A.alloc([4, 512], BF16, "sRG"); MO = A.alloc([4, 512], BF16, "sMO")
        WE = [A.alloc([512], F32, f"sWE{i}") for i in range(2)]
        PS_ = [A.alloc([512], BF16, f"sPS{i}") for i in range(4)]
        PR = A.alloc([512], BF16, "sPR"); PM = A.alloc([8, 128], BF16, "sPM")
        Y1 = A.alloc([512], F32, "sY1"); Y2 = A.alloc([512], F32, "sY2"); Y3 = A.alloc([512], F32, "sY3")
        OB = [A.alloc([512], BF16, f"sOB{i}") for i in range(3)]
        YT = [A.alloc([4, SEG], BF16, f"sYT{i}") for i in range(3)]
        STATS = A.alloc([4, 6], F32, "sSTATS"); MV2 = A.alloc([4, 2], F32, "sMV2"); TMP4 = A.alloc([8], F32, "sTMP4")
        DEN = A.alloc([8], F32, "sDEN")
    cst = self.cst
    ident = self.ident
    identb = self.identb
    onesb = self.onesb
    lo = slice(0, 64); hi = slice(64, 128)
    segs = range(nseg) if F else range(nseg - 1, -1, -1)
    xpi = 0
    tmv = lambda nm: self.dr["tm_" + nm][g0:g0 + SEG, :].rearrange("(c p) j -> p c j", p=128)
    for sg in segs:
        t0 = sg * SEG
        g0 = b0 + t0
        for ct in (range(6) if F else range(5)):
            xp = XP[xpi % 2]; xpi += 1
            _load_halo(self, xp, "fm_xbc", ct, s, t0, T)
            _conv_silu(self, XBC[:, ct, :], xp, XH, TH, ct * 4, 24 + ct, SEG)
        for ct in (range(4) if F else (2, 3)):
            xp = XP[xpi % 2]; xpi += 1
            _load_halo(self, xp, "fm_mqk", ct, s, t0, T)
            _conv_silu(self, MQK[:, ct, :], xp, XH, TH, 30 + ct * 4, 46 + ct, SEG)
        S.dma("pool", RK, self.dr["fm_rk"][:, :, g0:g0 + SEG].rearrange("k p t -> p k t"), reads=[self.dt_("fm_rk", s)])
        S.dma("pool", TMG, self.dr["tmg"][g0:g0 + SEG, :].rearrange("(c p) j -> p c j", p=128), reads=[self.dt_("tmg", s)])
        S.dma("pool", RV, tmv("rv"), reads=[self.dt_("tm_rv", s)])
        S.dma("pool", MV, tmv("mv"), reads=[self.dt_("tm_mv", s)])
        if F:
            S.dma("pool", RQ, self.dr["fm_rq"][:, :, g0:g0 + SEG].rearrange("k p t -> p k t"), reads=[self.dt_("fm_rq", s)])
            S.dma("sp", RQX[lo, :, 0, :], self.dr["fm_rq"][:, 0:64, g0:g0 + SEG].rearrange("k p t -> p k t"), reads=[self.dt_("fm_rq", s)])
            S.dma("sp", RQX[hi, :, 1, :], self.dr["fm_rq"][:, 64:128, g0:g0 + SEG].rearrange("k p t -> p k t"), reads=[self.dt_("fm_rq", s)])
            S.copy(CMX[lo, 0, :], XBC[lo, 5, :], eng="act")
            S.copy(CMX[hi, 1, :], XBC[hi, 5, :], eng="act")
            S.copy(MQX[lo, :, 0, :], MQK[lo, 0:2, :], eng="dve")
            S.copy(MQX[hi, :, 1, :], MQK[hi, 0:2, :], eng="dve")
            S.dma("sp", CUM[lo], self.dr["gq"][0, 0:64, g0:g0 + SEG], reads=[self.dt_("gq", s)])
            S.dma("sp", CB[lo], self.dr["gq"][1, 0:64, g0:g0 + SEG], reads=[self.dt_("gq", s)])
            for tl_, nm in ((ZG, "z"), (RG, "rg"), (MO, "mo")):
                S.dma("pool", tl_, tmv(nm), reads=[self.dt_("tm_" + nm, s)])
        cls = range(SEG // CH) if F else range(SEG // CH - 1, -1, -1)
        for cl in cls:
            cs = slice(cl * CH, (cl + 1) * CH)
            lc = t0 // CH + cl
            gc = g0 // CH + cl
            par = lc % 2
            tmg = TMG[:, cl, :]
            bT = _nbk(self); bT2 = _nbk(self)
            bTb = bT.v().bitcast(BF16); bT2b = bT2.v().bitcast(BF16)
            items = [(bTb[:, ct * 128:(ct + 1) * 128], XBC[:, ct, cs], identb) for ct in range(5)]
            items += [(bTb[:, 640:768], MQK[:, 2, cs], identb), (bTb[:, 768:896], MQK[:, 3, cs], identb)]
            S.transposes(items)
            S.transposes([(bT2b[:, 0:128], RK[:, 0, cs], identb), (bT2b[:, 128:256], RK[:, 1, cs], identb)])
            S.copy(TMA, bTb[:, 0:896], eng="act")
            S.copy(TMB, bT2b[:, 0:256], eng="dve")
            Vs = TMA[:, 0:512]; Bm = TMA[:, 512:640]; MKt = TMA[:, 640:896]; RKt = TMB
            S.tt(WV.v().rearrange("p (h v) -> p h v", h=8), Vs.rearrange("p (h v) -> p h v", h=8),
                 tmg[:, 128 + 32 * dn:128 + 32 * dn + 8].bcast(2, 64), ALU.mult)
            S.tt(WVR.v().rearrange("p (h v) -> p h v", h=4), RV[:, cl, :].rearrange("p (h v) -> p h v", h=4),
                 cst[:, CST["retS"] + 8 + 4 * dn:CST["retS"] + 12 + 4 * dn].bcast(2, 128), ALU.mult)
            S.tt(EV[:, :, 0:128], MV[:, cl, :].rearrange("p (h v) -> p h v", h=4),
                 tmg[:, 64 + 32 * dn:64 + 32 * dn + 4].bcast(2, 128), ALU.mult)
            S.copy(EV[:, :, 128], tmg[:, 64 + 32 * dn:64 + 32 * dn + 4], small=True)
            for h in range(4):
                r = slice((h % 2) * 64, (h % 2) * 64 + 64)
                S.ts(HM[r, h // 2, :], HM[r, h // 2, :], self.SC[r, lc, 16 + dn * 4 + h:16 + dn * 4 + h + 1], ALU.mult)
            if not F:
                S.copy(HMb[par], HM, eng="act")
                S.dma("sp", self.dr["st_ssd"][gc], HSb[1 - par], writes=[self.dt_("st_ssd", s)])
                S.dma("sp", self.dr["st_ret"][gc], HRb[1 - par].v().rearrange("p a b -> p (a b)"), writes=[self.dt_("st_ret", s)])
                S.dma("sp", self.dr["st_ml"][gc], HMb[par].v().rearrange("p a b -> p (a b)"), writes=[self.dt_("st_ml", s)])
            else:
                hsf = HSx[par]; hrf = HRx[par]; hmf = HMx[par]
                hsb = HSxb[par]; hrb = HRxb[par]; hmb = HMxb[par]
                S.copy(hsf[lo, 0, :], HS[lo], eng="act"); S.copy(hsf[hi, 1, :], HS[hi], eng="act")
                S.copy(hrf[lo, :, 0, :], HR[lo], eng="dve"); S.copy(hrf[hi, :, 1, :], HR[hi], eng="dve")
                S.copy(hmf[lo, :, 0, :], HM[lo], eng="act"); S.copy(hmf[hi, :, 1, :], HM[hi], eng="act")
                S.dma("pool", hsb[lo, 0, :], self.dr["st_ssd"][gc, 0:64, :], reads=[self.dt_("st_ssd", s)])
                S.dma("pool", hsb[hi, 1, :], self.dr["st_ssd"][gc, 64:128, :], reads=[self.dt_("st_ssd", s)])
                S.dma("pool", hrb[lo, :, 0, :], self.dr["st_ret"][gc, 0:64, :].rearrange("p (c v) -> p c v", c=2), reads=[self.dt_("st_ret", s)])
                S.dma("pool", hrb[hi, :, 1, :], self.dr["st_ret"][gc, 64:128, :].rearrange("p (c v) -> p c v", c=2), reads=[self.dt_("st_ret", s)])
                S.dma("pool", hmb[lo, :, 0, :], self.dr["st_ml"][gc, 0:64, :].rearrange("p (c v) -> p c v", c=2), reads=[self.dt_("st_ml", s)])
                S.dma("pool", hmb[hi, :, 1, :], self.dr["st_ml"][gc, 64:128, :].rearrange("p (c v) -> p c v", c=2), reads=[self.dt_("st_ml", s)])
                bG = _nbk(self)
                S.mm(bG[:, 0:256], [(XBC[:, 4, cs], CMX[:, :, cs])])
                pidx = 0
                Pm = {}
                for d_ in range(2):
                    neg = cst[:, CST["negF"]:CST["negF"] + 128] if d_ == 0 else cst[:, CST["negB"]:CST["negB"] + 128]
                    for g in range(2):
                        bE = _nbk(self)
                        grp = []
                        for q in range(4):
                            m = d_ * 8 + g * 4 + q
                            sel = cst[:, CST["sel"] + m * 128:CST["sel"] + (m + 1) * 128]
                            grp.append((bE[:, q * 128:(q + 1) * 128],
                                        [(sel, CUM[:, cs]), (CB[:, cs], sel), (ident, neg)]))
                        S.mm_multi(grp)
                        we = WE[pidx % 2]
                        S.act(we, bE, AF.Exp)
                        pt = PS_[pidx % 4]; pidx += 1
                        S.tt(pt.v().rearrange("p (q i) -> p q i", q=4), we.v().rearrange("p (q i) -> p q i", q=4),
                             bG[:, g * 128:(g + 1) * 128].bcast(1, 4), ALU.mult)
                        Pm[(d_, g)] = pt
                bY = _nbk(self)
                S.mm_multi([(bY[:, h * 64:(h + 1) * 64],
                             [(Pm[(0, h // 4)][:, (h % 4) * 128:(h % 4 + 1) * 128], Vs[:, h * 64:(h + 1) * 64]),
                              (Pm[(1, h // 4)][:, (h % 4) * 128:(h % 4 + 1) * 128], Vs[:, h * 64:(h + 1) * 64])])
                            for h in range(8)])
                bIf = _nbk(self); bIb = _nbk(self)
                S.mm(bIf, [(XBC[:, 5, cs], hsf.v().rearrange("p a b -> p (a b)"))])
                S.mm(bIb, [(XBC[:, 5, cs], hsb.v().rearrange("p a b -> p (a b)"))])
                y3 = lambda t_: t_.v().rearrange("p (h v) -> p h v", h=8)
                S.tt(y3(Y1), bIf.v().rearrange("p (h v) -> p h v", h=8), tmg[:, 0:8].bcast(2, 64), ALU.mult)
                S.tt(y3(Y2), bIb.v().rearrange("p (h v) -> p h v", h=8), tmg[:, 32:40].bcast(2, 64), ALU.mult)
                S.tt(Y1, Y1, Y2, ALU.add)
                S.tt(Y1, Y1, bY, ALU.add)
                S.tt(Y2, Vs, self.bv[:, 3, :], ALU.mult)
                S.tt(Y1, Y1, Y2, ALU.add)
                S.tt(Y1, Y1, ZG[:, cl, :], ALU.mult)
                S.op("dve", lambda e: e.bn_stats(out=_ap(STATS[:, 0, :]), in_=_ap(Y1)), reads=[Y1], writes=[STATS], small=True)
                S.op("dve", lambda e: e.bn_aggr(out=_ap(MV2[:, 0, :]), in_=_ap(STATS[:, 0, :])), reads=[STATS], writes=[MV2], small=True)
                S.tt(TMP4[:, 0:1], MV2[:, 0, 0:1], MV2[:, 0, 0:1], ALU.mult, small=True)
                S.tt(TMP4[:, 0:1], TMP4[:, 0:1], MV2[:, 0, 1:2], ALU.add, small=True)
                S.act(TMP4[:, 0:1], TMP4[:, 0:1], AF.Ln, bias=self.epsb, small=True)
                S.act(TMP4[:, 0:1], TMP4[:, 0:1], AF.Exp, scale=-0.5, small=True)
                S.stt(OB[0], Y1, TMP4[:, 0:1], self.bv[:, 0, :], ALU.mult, ALU.mult)
                bG2 = _nbk(self)
                S.mm_multi([(bG2[:, ct * 256:(ct + 1) * 256], [(RK[:, ct, cs], RQX[:, ct, :, cs])]) for ct in range(2)])
                S.tt(PR, bG2, cst[:, CST["retW"]:CST["retW"] + 512], ALU.mult)
                bY2 = _nbk(self)
                S.mm_multi([(bY2[:, h * 128:(h + 1) * 128], [(PR[:, h * 128:(h + 1) * 128], RV[:, cl, h * 128:(h + 1) * 128])])
                            for h in range(4)])
                bI2f = _nbk(self); bI2b = _nbk(self)
                for bI, hh in ((bI2f, hrf), (bI2b, hrb)):
                    S.mm_multi([(bI[:, ct * 256:(ct + 1) * 256], [(RQ[:, ct, cs], hh[:, ct, :, :])]) for ct in range(2)])
                y4 = lambda t_: t_.v().rearrange("p (h v) -> p h v", h=4)
                S.tt(y4(Y2), bI2f.v().rearrange("p (h v) -> p h v", h=4), cst[:, CST["retS"]:CST["retS"] + 4].bcast(2, 128), ALU.mult)
                S.tt(y4(Y3), bI2b.v().rearrange("p (h v) -> p h v", h=4), cst[:, CST["retS"] + 4:CST["retS"] + 8].bcast(2, 128), ALU.mult)
                S.tt(Y2, Y2, Y3, ALU.add)
                S.tt(Y2, Y2, bY2, ALU.add)
                _headnorm(self, y4(Y2), STATS, MV2, TMP4[:, 0:4], 1, RG[:, cl, :], OB[1])
                bG3 = _nbk(self)
                S.mm_multi([(bG3[:, ct * 256:(ct + 1) * 256], [(MQK[:, 2 + ct, cs], MQX[:, ct, :, cs])]) for ct in range(2)])
                for d_ in range(2):
                    msk = cst[:, CST["triU"]:CST["triU"] + 128] if d_ == 0 else cst[:, CST["triL"]:CST["triL"] + 128]
                    for h in range(4):
                        S.stt(PM[:, d_ * 4 + h, :], bG3[:, h * 128:(h + 1) * 128], tmg[:, 64 + 32 * d_ + h:64 + 32 * d_ + h + 1],
                              msk, ALU.mult, ALU.mult)
                bN = [_nbk(self), _nbk(self)]
                bD = _nbk(self)
                for d_, hm in ((0, hmf), (1, hmb)):
                    raw = []
                    for ct in range(2):
                        raw.append((bN[d_][:, ct * 256:(ct + 1) * 256], MQK[:, ct, cs], hm[:, ct, :, 0:128], True, False))
                        for hl in range(2):
                            h = ct * 2 + hl
                            raw.append((bN[d_][:, h * 128:(h + 1) * 128], PM[:, d_ * 4 + h, :], MV[:, cl, h * 128:(h + 1) * 128],
                                        False, hl == 1))
                    S.mm_raw(raw)
                raw = []
                for d_, hm in ((0, hmf), (1, hmb)):
                    for ct in range(2):
                        raw.append((bD[:, d_ * 4 + ct * 2:d_ * 4 + ct * 2 + 2], MQK[:, ct, cs], hm[:, ct, :, 128], True, False))
                        for hl in range(2):
                            h = ct * 2 + hl
                            raw.append((bD[:, d_ * 4 + h:d_ * 4 + h + 1], PM[:, d_ * 4 + h, :], onesb[:, 0:1], False, hl == 1))
                S.mm_raw(raw)
                S.act(DEN, bD[:, 0:8], AF.Abs, small=True)
                clampv = tmg[:, 192:256].rearrange("p (d x) -> p d x", d=2)[:, :, 0:4]
                S.tt(DEN.v().rearrange("p (d x) -> p d x", d=2), DEN.v().rearrange("p (d x) -> p d x", d=2), clampv, ALU.max, small=True)
                S.recip(DEN, DEN, small=True)
                S.tt(y4(Y3), bN[0].v().rearrange("p (h v) -> p h v", h=4), DEN[:, 0:4].bcast(2, 128), ALU.mult)
                S.tt(y4(Y1), bN[1].v().rearrange("p (h v) -> p h v", h=4), DEN[:, 4:8].bcast(2, 128), ALU.mult)
                S.tt(Y3, Y3, Y1, ALU.add)
                _headnorm(self, y4(Y3), STATS, MV2, TMP4[:, 0:4], 2, MO[:, cl, :], OB[2])
                for bi in range(3):
                    bO = _nbk(self)
                    bOb = bO.v().bitcast(BF16)
                    S.transposes([(bOb[:, k * 128:(k + 1) * 128], OB[bi][:, k * 128:(k + 1) * 128], identb) for k in range(4)])
                    S.copy(YT[bi][:, :, cs], bOb[:, 0:512].rearrange("p (k t) -> p k t", k=4), eng=("act" if bi % 2 else "dve"))
            bS = _nbk(self)
            S.mm(bS, [(Bm, WV)])
            for g in range(2):
                r = slice(g * 64, (g + 1) * 64)
                S.tt(HS[r].rearrange("p (h v) -> p h v", h=4), HS[r].rearrange("p (h v) -> p h v", h=4),
                     self.SC[r, lc, dn * 8 + g * 4:dn * 8 + g * 4 + 4].bcast(2, 64), ALU.mult)
                S.tt(HS[r], HS[r], bS[r, g * 256:(g + 1) * 256], ALU.add)
            bS2 = _nbk(self)
            S.mm_multi([(bS2[:, ct * 256:(ct + 1) * 256], [(RKt[:, ct * 128:(ct + 1) * 128], WVR[:, ct * 256:(ct + 1) * 256])])
                        for ct in range(2)])
            S.tt(HR, HR, cst[:, CST["retG"]:CST["retG"] + 256].rearrange("p (c v) -> p c v", c=2), ALU.mult)
            b2v = bS2.v().rearrange("p (c a v) -> p c a v", c=2, a=2)
            S.tt(HR[lo], HR[lo], b2v[lo, :, 0, :], ALU.add)
            S.tt(HR[hi], HR[hi], b2v[hi, :, 1, :], ALU.add)
            for ct in range(2):
                bS3 = _nbk(self)
                S.mm(bS3[:, 0:264], [(MKt[:, ct * 128:(ct + 1) * 128], EV[:, 2 * ct:2 * ct + 2, :])])
                S.tt(HM[lo, ct, 0:129], HM[lo, ct, 0:129], bS3[lo, 0:129], ALU.add)
                S.tt(HM[hi, ct, 0:129], HM[hi, ct, 0:129], bS3[hi, 132:261], ALU.add)
            if not F:
                S.copy(HSb[par], HS, eng="act")
                S.copy(HRb[par], HR, eng="act")
        if F:
            for bi in range(3):
                S.dma("sp", self.dr["fm_y"][4 + bi * 4:8 + bi * 4, :, g0:g0 + SEG].rearrange("k p t -> p k t"), YT[bi],
                      writes=[self.dt_("fm_y", s)])


Prog.SWEEP = _SWEEP


_SEQS = [2048, 4096, 4096]


def kernel(**inputs):
    inp = {k: np.asarray(v) for k, v in inputs.items()}
    prog = Prog(_SEQS, DEPTH)
    nc = prog.build()
    wbig = [pack_layer_weights(inp, l).reshape(-1, 2048) for l in range(DEPTH)]
    pvec = np.stack([pack_pvec(inp, l) for l in range(DEPTH)])
    bvec = np.stack([pack_bvec(inp, l) for l in range(DEPTH)])
    cst, rc, rs = build_consts(max(_SEQS))
    xp, xs = inp["x_prompt"], inp["x_sample"]
    in_maps = []
    for c in range(NCORES):
        x_in = np.concatenate([xp[c], xs[2 * c], xs[2 * c + 1]], axis=0)
        m = {"x_in": np.ascontiguousarray(x_in), "pvec": pvec, "bvec": bvec, "cst": cst, "rcos": rc, "rsin": rs}
        for l in range(DEPTH):
            m[f"wbig{l}"] = wbig[l]
        in_maps.append(m)
    res = run_bass_kernel_spmd(nc, in_maps, core_ids=list(range(NCORES)))
    yp = np.empty_like(xp)
    ys = np.empty_like(xs)
    for c in range(NCORES):
        y = np.asarray(res.results[c]["y_out"])
        yp[c] = y[0:2048]
        ys[2 * c] = y[2048:6144]
        ys[2 * c + 1] = y[6144:10240]
    return yp, ys
```

```python
import contextlib
import math
import numpy as np
import concourse.bass as bass
import concourse.mybir as mybir
from concourse.bass_utils import run_bass_kernel_spmd

F32 = mybir.dt.float32
BF16 = mybir.dt.bfloat16
AF = mybir.ActivationFunctionType
ALU = mybir.AluOpType
AX = mybir.AxisListType

D = 1024
DFF = 2816
DEPTH = 4
NKC = 8
TT = 512
CH = 128
DN_ALPHA = (2.0 * DEPTH) ** 0.25
EPS = 1e-5
NCORES = 8

GE = 4096
OFF = {}
_o = 0
for f in range(2):
    OFF[("up", f)] = _o; _o += 11 * GE
    OFF[("dn", f)] = _o; _o += 8 * 2816
OFF["inF"] = _o; _o += 15 * GE
OFF["inT"] = _o; _o += 5 * GE
OFF["br"] = _o; _o += 4 * GE
OFF["out"] = _o; _o += 2 * GE
OFF["lru"] = _o; _o += 2048
WPP = _o
assert WPP % 2048 == 0

_sp = [512, 512, 512, 768, 16, 256, 256, 512, 512, 512, 512, 512, 16, 4096]
_names = ["xa", "ga", "z", "xbc", "dt", "rq", "rk", "rv", "rg", "mqk", "mv", "mo", "mg", "mix"]
COL = {}
_a = 0
for n_, s_ in zip(_names, _sp):
    COL[n_] = _a; _a += s_
assert _a == 9504


def _rot_cols(base):
    idx = []
    for h in range(4):
        for d in range(64):
            idx.append(base + h * 64 + (d + 32) % 64)
    return idx


def _inF_colmap():
    groups = []
    groups.append(list(range(COL["xa"], COL["xa"] + 512)))
    groups.append(list(range(COL["ga"], COL["ga"] + 512)))
    groups.append(list(range(COL["xbc"], COL["xbc"] + 512)))
    g3 = list(range(COL["xbc"] + 512, COL["xbc"] + 768))
    ga = [-1] * 128
    gb = [-1] * 128
    for h in range(8):
        ga[h] = COL["dt"] + h
        ga[32 + h] = COL["dt"] + 8 + h
    for h in range(4):
        ga[64 + h] = COL["mg"] + 0 + h
        ga[96 + h] = COL["mg"] + 8 + h
        gb[64 + h] = COL["mg"] + 4 + h
        gb[96 + h] = COL["mg"] + 12 + h
    groups.append(g3 + ga + gb)
    groups.append(list(range(COL["mqk"], COL["mqk"] + 512)))
    groups.append(list(range(COL["rq"], COL["rq"] + 256)) + _rot_cols(COL["rq"]))
    groups.append(list(range(COL["rk"], COL["rk"] + 256)) + _rot_cols(COL["rk"]))
    for g in range(8):
        groups.append(list(range(COL["mix"] + g * 512, COL["mix"] + (g + 1) * 512)))
    return groups


def _inT_colmap():
    return [list(range(COL[n], COL[n] + 512)) for n in ("z", "rv", "rg", "mv", "mo")]


def _grp_kc(w, cols):
    cols = np.asarray(cols)
    safe = np.where(cols < 0, 0, cols)
    g = w[:, safe]
    if (cols < 0).any():
        g = g.copy()
        g[:, cols < 0] = 0.0
    k = w.shape[0] // 128
    return g.reshape(k, 128, len(cols)).transpose(1, 0, 2).reshape(128, k * len(cols))


def pack_layer_weights(inp, l):
    out = np.zeros((128, WPP), np.float32)
    for f in range(2):
        wu = inp["w_ffn_in"][l, f]
        o = OFF[("up", f)]
        for g in range(11):
            cols = list(range(256 * g, 256 * g + 256)) + list(range(DFF + 256 * g, DFF + 256 * g + 256))
            out[:, o + g * GE:o + (g + 1) * GE] = _grp_kc(wu, cols)
        wd = inp["w_ffn_out"][l, f]
        o = OFF[("dn", f)]
        for dc in range(8):
            out[:, o + dc * 2816:o + (dc + 1) * 2816] = _grp_kc(wd, list(range(dc * 128, dc * 128 + 128)))
    wi = inp["w_in"][l]
    for g, cols in enumerate(_inF_colmap()):
        out[:, OFF["inF"] + g * GE:OFF["inF"] + (g + 1) * GE] = _grp_kc(wi, cols)
    for g, cols in enumerate(_inT_colmap()):
        out[:, OFF["inT"] + g * GE:OFF["inT"] + (g + 1) * GE] = _grp_kc(wi, cols)
    for i in range(4):
        out[:, OFF["br"] + i * GE:OFF["br"] + (i + 1) * GE] = _grp_kc(inp["w_branch"][l, i], list(range(1024)))
    for g in range(2):
        out[:, OFF["out"] + g * GE:OFF["out"] + (g + 1) * GE] = _grp_kc(inp["w_out"][l], list(range(g * 512, g * 512 + 512)))
    gw = inp["lru_gate_w"][l]
    blk = np.zeros((128, 16, 128), np.float32)
    for d_ in range(2):
        for g_ in range(2):
            for ct in range(4):
                m = (d_ * 2 + g_) * 4 + ct
                for hl in range(2):
                    blk[hl * 64:(hl + 1) * 64, m, hl * 64:(hl + 1) * 64] = gw[d_, g_, ct * 2 + hl]
    out[:, OFF["lru"]:OFF["lru"] + 2048] = blk.reshape(128, 2048)
    return out


PV = {}
_p = 0
for nm, n in [("ln_g", 24), ("ln_b", 24), ("xa_w", 16), ("xa_b", 4), ("xbc_w", 24), ("xbc_b", 6),
              ("mqk_w", 16), ("mqk_b", 4), ("lru_gb", 16), ("lru_lam", 8), ("rbA", 1), ("rbB", 1), ("ralog", 1)]:
    PV[nm] = _p; _p += n
NPV = _p


def pack_pvec(inp, l):
    pv = np.zeros((128, NPV), np.float32)
    for i in range(3):
        pv[:, PV["ln_g"] + i * 8:PV["ln_g"] + i * 8 + 8] = inp["ln_g"][l, i].reshape(8, 128).T
        pv[:, PV["ln_b"] + i * 8:PV["ln_b"] + i * 8 + 8] = inp["ln_b"][l, i].reshape(8, 128).T
    for nm, key, nct in (("xa", "lru_conv", 4), ("xbc", "ssd_conv", 6), ("mqk", "mlstm_conv", 4)):
        w = inp[key + "_w"][l]
        b = inp[key + "_b"][l]
        for ct in range(nct):
            for k in range(4):
                pv[:, PV[nm + "_w"] + ct * 4 + k] = w[k, ct * 128:(ct + 1) * 128]
            pv[:, PV[nm + "_b"] + ct] = b[ct * 128:(ct + 1) * 128]
    for d_ in range(2):
        for g_ in range(2):
            for ct in range(4):
                pv[:, PV["lru_gb"] + (d_ * 2 + g_) * 4 + ct] = inp["lru_gate_b"][l, d_, g_, ct * 128:(ct + 1) * 128]
        for ct in range(4):
            pv[:, PV["lru_lam"] + d_ * 4 + ct] = inp["lru_lambda"][l, d_, ct * 128:(ct + 1) * 128]
    pv[0:8, PV["rbA"]] = inp["ssd_dt_bias"][l, 0]
    pv[32:40, PV["rbA"]] = inp["ssd_dt_bias"][l, 1]
    pv[64:68, PV["rbA"]] = inp["mlstm_gate_b"][l, 0, 0]
    pv[96:100, PV["rbA"]] = inp["mlstm_gate_b"][l, 1, 0]
    pv[64:68, PV["rbB"]] = inp["mlstm_gate_b"][l, 0, 1]
    pv[96:100, PV["rbB"]] = inp["mlstm_gate_b"][l, 1, 1]
    pv[0:8, PV["ralog"]] = inp["ssd_a_log"][l, 0]
    pv[32:40, PV["ralog"]] = inp["ssd_a_log"][l, 1]
    return pv


def pack_bvec(inp, l):
    bv = np.zeros((4, 512), np.float32)
    bv[0] = inp["ssd_norm_w"][l]
    bv[1] = inp["ret_norm_w"][l]
    bv[2] = inp["mlstm_norm_w"][l]
    bv[3] = np.repeat(inp["ssd_d"][l], 64)
    return bv


CST = {}
_c = 0
for nm, n in [("ident", 128), ("triU", 128), ("triL", 128), ("negF", 128), ("negB", 128), ("sel", 16 * 128),
              ("retW", 4 * 128), ("retS", 20), ("ones", 512), ("selT", 24), ("retG", 256)]:
    CST[nm] = _c; _c += n
NCST = _c
NEGV = -30000.0


def build_consts(tmax):
    c = np.zeros((128, NCST), np.float32)
    j = np.arange(128)[:, None]
    i = np.arange(128)[None, :]
    c[:, CST["ident"]:CST["ident"] + 128] = (j == i)
    c[:, CST["triU"]:CST["triU"] + 128] = (j <= i)
    c[:, CST["triL"]:CST["triL"] + 128] = (j >= i)
    c[:, CST["negF"]:CST["negF"] + 128] = np.where(j <= i, 0.0, NEGV)
    c[:, CST["negB"]:CST["negB"] + 128] = np.where(j >= i, 0.0, NEGV)
    for m in range(16):
        r = (m % 8) + (32 if m >= 8 else 0)
        c[r, CST["sel"] + m * 128:CST["sel"] + (m + 1) * 128] = 1.0
    lg = np.log1p(-np.exp2(-5.0 - np.arange(4, dtype=np.float64)))
    for h in range(4):
        c[:, CST["retW"] + h * 128:CST["retW"] + (h + 1) * 128] = 0.125 * np.exp(lg[h] * np.abs(i - j))
        t = np.arange(128, dtype=np.float64)
        c[:, CST["retS"] + 0 + h] = 0.125 * np.exp(lg[h] * (t + 1))
        c[:, CST["retS"] + 4 + h] = 0.125 * np.exp(lg[h] * (128 - t))
        c[:, CST["retS"] + 8 + h] = np.exp(lg[h] * (127 - t))
        c[:, CST["retS"] + 12 + h] = np.exp(lg[h] * t)
        c[:, CST["retS"] + 16 + h] = np.exp(lg[h] * 128)
    c[:, CST["ones"]:CST["ones"] + 512] = 1.0
    for n in range(16):
        c[(n % 8) + (32 if n >= 8 else 0), CST["selT"] + n] = 1.0
    for n2 in range(8):
        c[64 + 32 * (n2 // 4) + (n2 % 4), CST["selT"] + 16 + n2] = 1.0
    for h in range(4):
        r0 = (h % 2) * 64
        c0 = CST["retG"] + (h // 2) * 128
        c[r0:r0 + 64, c0:c0 + 128] = np.exp(lg[h] * 128)
    p = np.arange(128)
    d = p % 64
    inv = 10000.0 ** (-(2.0 * (d % 32)) / 64.0)
    ang = inv[:, None].astype(np.float32) * np.arange(tmax, dtype=np.float32)[None, :]
    cos = np.cos(ang).astype(np.float32)
    sin = np.sin(ang).astype(np.float32)
    sin_signed = np.where((d < 32)[:, None], -sin, sin).astype(np.float32)
    return c, cos, sin_signed


class Ev:
    __slots__ = ("sem", "val", "eng", "small")

    def __init__(self, sem, val, eng, small=False):
        self.sem, self.val, self.eng, self.small = sem, val, eng, small


class Tl:
    def __init__(self, ap, name):
        self.ap, self.name = ap, name
        self.w = None
        self.r = {}

    def __getitem__(self, idx):
        return V(self.ap[idx], self)

    def v(self):
        return V(self.ap, self)

    @property
    def tl(self):
        return self

    def rearrange(self, pat, **kw):
        return self.v().rearrange(pat, **kw)

    def bitcast(self, dt):
        return self.v().bitcast(dt)

    def bcast(self, axis, n):
        return self.v().bcast(axis, n)


class V:
    def __init__(self, ap, tl):
        self.ap, self.tl = ap, tl

    def __getitem__(self, idx):
        return V(self.ap[idx], self.tl)

    def bitcast(self, dt):
        return V(self.ap.bitcast(dt), self.tl)

    def rearrange(self, pat, **kw):
        return V(self.ap.rearrange(pat, **kw), self.tl)

    def bcast(self, axis, n):
        ap = self.ap.unsqueeze(axis)
        shp = list(ap.shape)
        shp[axis] = n
        return V(ap.broadcast_to(shp), self.tl)


def _ap(x):
    return x.ap if isinstance(x, (V, Tl)) else x


def _tl(x):
    return x.tl if isinstance(x, V) else (x if isinstance(x, Tl) else None)


EPOCH = 30000
RELAX_SAME_ENGINE = False
NDSEM = 40


class Sched:
    def __init__(self, nc, stack):
        self.nc = nc
        self.stack = stack
        self.h = {"pe": nc.tensor, "act": nc.scalar, "dve": nc.vector, "pool": nc.gpsimd, "sp": nc.sync}
        self.cnt = {e: 0 for e in ("pe", "act", "dve", "pool")}
        self.sems = {e: [] for e in ("pe", "act", "dve", "pool")}
        self.waited = {e: {} for e in self.h}
        self.dsem = {q: [[stack.enter_context(nc.semaphore(f"dq{q}{i}")), 0] for i in range(n)]
                     for q, n in (("sp", NDSEM), ("pool", 16), ("act", 8), ("cvt", 8))}
        self.drr = {"sp": 0, "pool": 0, "act": 0, "cvt": 0}
        self.last = {}
        self.ninstr = 0

    def _sem(self, eng, n):
        k = (n - 1) // EPOCH
        while len(self.sems[eng]) <= k:
            self.sems[eng].append(self.stack.enter_context(self.nc.semaphore(f"s_{eng}{len(self.sems[eng])}")))
        return self.sems[eng][k], (n - 1) % EPOCH + 1

    def _wait(self, eng, ev):
        if ev is None:
            return
        key = id(ev.sem)
        if self.waited[eng].get(key, 0) >= ev.val:
            return
        self.h[eng].wait_ge(ev.sem, ev.val)
        self.waited[eng][key] = ev.val

    def _deps(self, eng, reads, writes):
        evs = []
        for t in reads:
            if t.w is not None:
                evs.append(t.w)
        for t in writes:
            if t.w is not None:
                evs.append(t.w)
            evs.extend(t.r.values())
        for ev in evs:
            if ev.eng == eng and eng == "pe":
                continue
            if ev.eng == eng and RELAX_SAME_ENGINE and not ev.small and ev.eng in self.cnt:
                continue
            self._wait(eng, ev)

    def _update(self, ev, reads, writes, key):
        for t in writes:
            t.w = ev
            t.r = {}
        for t in reads:
            if t in writes:
                continue
            t.r[key] = ev

    def op(self, eng, fn, reads=(), writes=(), small=False):
        reads = [t for t in reads if t is not None]
        writes = [t for t in writes if t is not None]
        self._deps(eng, reads, writes)
        ins = fn(self.h[eng])
        self.cnt[eng] += 1
        sem, val = self._sem(eng, self.cnt[eng])
        ins.then_inc(sem, 1)
        ev = Ev(sem, val, eng, small)
        self.last[eng] = ev
        self._update(ev, reads, writes, eng)
        self.ninstr += 1
        return ev

    def dma(self, q, out, in_, reads=(), writes=(), key=None):
        reads = [t for t in list(reads) + [_tl(in_)] if t is not None]
        writes = [t for t in list(writes) + [_tl(out)] if t is not None]
        key = key or q
        pool_ = self.dsem[key]
        slot = pool_[self.drr[key] % len(pool_)]
        self.drr[key] += 1
        if slot[1] > 0:
            self._wait(q, Ev(slot[0], slot[1], "dma"))
        evs = []
        for t in reads:
            if t.w is not None:
                evs.append(t.w)
        for t in writes:
            if t.w is not None:
                evs.append(t.w)
            evs.extend(t.r.values())
        for ev in evs:
            self._wait(q, ev)
        ins = self.h[q].dma_start(out=_ap(out), in_=_ap(in_))
        slot[1] += 16
        ins.then_inc(slot[0], 16)
        ev = Ev(slot[0], slot[1], "dma")
        self._update(ev, reads, writes, ("dma", q, id(slot[0])))
        self.ninstr += 1
        return ev

    def barrier(self):
        evs = [ev for ev in self.last.values()]
        for k_, pool_ in self.dsem.items():
            if k_ == "cvt":
                continue
            for s in pool_:
                if s[1] > 0:
                    evs.append(Ev(s[0], s[1], "dma"))
        for e in self.h:
            for ev in evs:
                if ev.eng == e and e == "pe":
                    continue
                self._wait(e, ev)

    def act(self, out, in_, func, bias=None, scale=None, small=False, eng="act"):
        kw = {}
        if bias is not None:
            kw["bias"] = _ap(bias)
        if scale is not None:
            kw["scale"] = _ap(scale)
        return self.op(eng, lambda h: h.activation(out=_ap(out), in_=_ap(in_), func=func, **kw),
                       reads=[_tl(in_), _tl(bias), _tl(scale)], writes=[_tl(out)], small=small)

    def tt(self, out, in0, in1, op, small=False, eng="dve"):
        return self.op(eng, lambda h: h.tensor_tensor(out=_ap(out), in0=_ap(in0), in1=_ap(in1), op=op),
                       reads=[_tl(in0), _tl(in1)], writes=[_tl(out)], small=small)

    def ts(self, out, in0, s1, op0, s2=None, op1=None, small=False, eng="dve"):
        kw = {}
        if op1 is not None:
            kw["op1"] = op1
        return self.op(eng, lambda h: h.tensor_scalar(out=_ap(out), in0=_ap(in0), scalar1=_ap(s1), scalar2=_ap(s2),
                                                      op0=op0, **kw),
                       reads=[_tl(in0), _tl(s1), _tl(s2)], writes=[_tl(out)], small=small)

    def stt(self, out, in0, scalar, in1, op0, op1, small=False):
        return self.op("dve", lambda h: h.scalar_tensor_tensor(out=_ap(out), in0=_ap(in0), scalar=_ap(scalar),
                                                               in1=_ap(in1), op0=op0, op1=op1),
                       reads=[_tl(in0), _tl(scalar), _tl(in1)], writes=[_tl(out)], small=small)

    def copy(self, out, in_, eng="dve", small=False):
        if eng == "act":
            return self.act(out, in_, AF.Copy, small=small)
        return self.op(eng, lambda h: h.tensor_copy(out=_ap(out), in_=_ap(in_)),
                       reads=[_tl(in_)], writes=[_tl(out)], small=small)

    def memset(self, out, val, eng="dve", small=False):
        return self.op(eng, lambda h: h.memset(_ap(out), val), writes=[_tl(out)], small=small)

    def recip(self, out, in_, small=False):
        return self.op("dve", lambda h: h.reciprocal(out=_ap(out), in_=_ap(in_)),
                       reads=[_tl(in_)], writes=[_tl(out)], small=small)

    def scan(self, out, d0, d1, init, op0=ALU.mult, op1=ALU.add):
        return self.op("dve", lambda h: h.tensor_tensor_scan(out=_ap(out), data0=_ap(d0), data1=_ap(d1),
                                                             initial=_ap(init), op0=op0, op1=op1),
                       reads=[_tl(d0), _tl(d1), _tl(init)], writes=[_tl(out)])

    def reduce(self, out, in_, op, axis=AX.X, small=False):
        return self.op("dve", lambda h: h.tensor_reduce(out=_ap(out), in_=_ap(in_), axis=axis, op=op),
                       reads=[_tl(in_)], writes=[_tl(out)], small=small)

    def mm(self, out, pairs, extra_reads=()):
        reads = [_tl(a) for a, b in pairs] + [_tl(b) for a, b in pairs] + list(extra_reads)
        n = len(pairs)

        def fn(h):
            ins = None
            for k, (a, b) in enumerate(pairs):
                ins = h.matmul(_ap(out), _ap(a), _ap(b), start=(k == 0), stop=(k == n - 1))
            return ins
        self.ninstr += n - 1
        return self.op("pe", fn, reads=reads, writes=[_tl(out)])

    def mm_multi(self, groups, extra_reads=()):
        reads, writes = list(extra_reads), []
        for out, pairs in groups:
            writes.append(_tl(out))
            for a, b in pairs:
                reads += [_tl(a), _tl(b)]

        def fn(h):
            ins = None
            for out, pairs in groups:
                n = len(pairs)
                for k, (a, b) in enumerate(pairs):
                    ins = h.matmul(_ap(out), _ap(a), _ap(b), start=(k == 0), stop=(k == n - 1))
            return ins
        self.ninstr += sum(len(p) for _, p in groups) - 1
        return self.op("pe", fn, reads=reads, writes=writes)

    def mm_raw(self, items):
        reads, writes = [], []
        for o, a, b, st_, sp_ in items:
            writes.append(_tl(o)); reads += [_tl(a), _tl(b)]

        def fn(h):
            ins = None
            for o, a, b, st_, sp_ in items:
                ins = h.matmul(_ap(o), _ap(a), _ap(b), start=st_, stop=sp_)
            return ins
        self.ninstr += len(items) - 1
        return self.op("pe", fn, reads=reads, writes=writes)

    def transposes(self, items):
        reads, writes = [], []
        for o, i, idn in items:
            writes.append(_tl(o)); reads += [_tl(i), _tl(idn)]

        def fn(h):
            ins = None
            for o, i, idn in items:
                ins = h.transpose(_ap(o), _ap(i), _ap(idn))
            return ins
        self.ninstr += len(items) - 1
        return self.op("pe", fn, reads=reads, writes=writes)


class Arena:
    def __init__(self, nc, stack, nbytes):
        self.t = stack.enter_context(nc.sbuf_tensor("arena", [128, nbytes // 4], F32))
        self.n32 = nbytes // 4
        self.off = 0

    def alloc(self, shape, dt, name):
        n = int(np.prod(shape))
        isz = 4 if dt == F32 else 2
        n32 = (n * isz + 3) // 4
        n32 = (n32 + 15) // 16 * 16
        assert self.off + n32 <= self.n32, f"arena overflow at {name}: {self.off + n32} > {self.n32}"
        ap = self.t[:, self.off:self.off + n32]
        self.off += n32
        if dt != F32:
            ap = ap.bitcast(dt)
        ap = ap[:, 0:n]
        if len(shape) == 2:
            ap = ap.rearrange("p (a b) -> p a b", a=shape[0])
        elif len(shape) == 3:
            ap = ap.rearrange("p (a b c) -> p a b c", a=shape[0], b=shape[1])
        return Tl(ap, name)


class Prog:
    def __init__(self, Ts, depth, debug=False, phases=("P1", "LRU", "GATE", "SWB", "SWF", "P2")):
        self.Ts = list(Ts)
        self.depth = depth
        self.debug = debug
        self.phases = phases
        self.base = [0]
        for t in self.Ts:
            self.base.append(self.base[-1] + t)
        self.Ttot = self.base[-1]
        self.Tmax = max(self.Ts)
        self.nc = bass.Bass("TRN2", target_bir_lowering=False)
        self.stack = contextlib.ExitStack()
        nc = self.nc
        Tt = self.Ttot
        self.dr = {}

        def dram(name, shape, dt, kind):
            self.dr[name] = nc.dram_tensor(name, list(shape), dt, kind=kind).ap()
            return self.dr[name]
        dram("x_in", [Tt, D], F32, "ExternalInput")
        dram("y_out", [Tt, D], F32, "ExternalOutput")
        for l in range(depth):
            dram(f"wbig{l}", [128 * WPP // 2048, 2048], F32, "ExternalInput")
            dram(f"wb16_{l}", [128 * WPP // 2048, 2048], BF16, "Internal")
        dram("pvec", [depth, 128, NPV], F32, "ExternalInput")
        dram("bvec", [depth, 4, 512], F32, "ExternalInput")
        dram("cst", [128, NCST], F32, "ExternalInput")
        dram("rcos", [128, self.Tmax], F32, "ExternalInput")
        dram("rsin", [128, self.Tmax], F32, "ExternalInput")
        sk = "ExternalOutput" if debug else "Internal"
        dram("xT", [8, 128, Tt], F32, sk)
        dram("fm_xa", [4, 128, Tt], BF16, sk)
        dram("fm_ga", [4, 128, Tt], BF16, sk)
        dram("fm_xbc", [6, 128, Tt], BF16, sk)
        dram("fm_mqk", [4, 128, Tt], BF16, sk)
        dram("fm_rq", [2, 128, Tt], BF16, sk)
        dram("fm_rk", [2, 128, Tt], BF16, sk)
        dram("fm_mix", [32, 128, Tt], BF16, sk)
        dram("fm_gr", [2, 128, Tt], F32, sk)
        for n_ in ("z", "rv", "rg", "mv", "mo"):
            dram("tm_" + n_, [Tt, 512], BF16, sk)
        dram("fm_y", [16, 128, Tt], BF16, sk)
        dram("gq", [4, 128, Tt], F32, sk)
        dram("tmg", [Tt, 256], F32, sk)
        nch = Tt // CH
        dram("st_ssd", [nch, 128, 256], BF16, sk)
        dram("st_ret", [nch, 128, 256], BF16, sk)
        dram("st_ml", [nch, 128, 264], BF16, sk)
        dram("st_sc", [nch, 128, 8], F32, sk)
        self.drt = {}

    def dt_(self, *key):
        if key not in self.drt:
            self.drt[key] = Tl(None, str(key))
        return self.drt[key]

    def wview(self, l):
        return self.dr[f"wb16_{l}"].rearrange("(p a) b -> p (a b)", p=128)

    def build(self):
        nc = self.nc
        with self.stack:
            self.S = S = Sched(nc, self.stack)
            st = self.stack
            self.cst = Tl(st.enter_context(nc.sbuf_tensor("cst_sb", [128, NCST], F32))[:], "cst")
            self.cstb = Tl(st.enter_context(nc.sbuf_tensor("cstb_sb", [128, 512], BF16))[:], "cstb")
            self.pv = Tl(st.enter_context(nc.sbuf_tensor("pv_sb", [128, NPV], F32))[:], "pv")
            self.dpv = Tl(st.enter_context(nc.sbuf_tensor("dpv_sb", [128, 64], F32))[:], "dpv")
            self.bv = Tl(st.enter_context(nc.sbuf_tensor("bv_sb", [128, 4, 512], F32))[:], "bv")
            self.SC = Tl(st.enter_context(nc.sbuf_tensor("sc_sb", [128, 32, 24], F32))[:], "SC")
            self.lnb8 = Tl(st.enter_context(nc.sbuf_tensor("lnb8_sb", [128, 1], F32))[:], "lnb8")
            psum = st.enter_context(nc.psum_tensor("ps_all", [128, 8 * 512], F32))
            self.bank = [Tl(psum[:, k * 512:(k + 1) * 512], f"bank{k}") for k in range(8)]
            self.arena = Arena(nc, st, 166 * 1024)
            rows = 128 * WPP // 2048
            npc = 7
            self.cvt_rows = rows
            for i in range(npc):
                self.cvt_piece(0, i)
            S.dma("sp", self.cst, self.dr["cst"])
            S.copy(self.cstb[:, 0:128], self.cst[:, CST["ident"]:CST["ident"] + 128])
            S.copy(self.cstb[:, 128:256], self.cst[:, CST["ones"]:CST["ones"] + 128])
            S.copy(self.cstb[:, 256:384], self.cst[:, CST["triU"]:CST["triU"] + 128])
            S.copy(self.cstb[:, 384:512], self.cst[:, CST["triL"]:CST["triL"] + 128])
            S.memset(self.lnb8, math.log(8.0), small=True)
            self.epsb = Tl(st.enter_context(nc.sbuf_tensor("epsb_sb", [128, 1], F32))[:], "epsb")
            S.memset(self.epsb, EPS, small=True)
            self.identb = self.cstb[:, 0:128]
            self.onesb = self.cstb[:, 128:256]
            self.ident = self.cst[:, CST["ident"]:CST["ident"] + 128]
            self.wring_i = 0
            for l in range(self.depth):
                self.layer(l)
            S.barrier()
        return nc

    def cvt_piece(self, l, i):
        rows, npc = self.cvt_rows, 7
        step = rows // npc
        src = self.dr[f"wbig{l}"]
        dst = self.dr[f"wb16_{l}"]
        self.S.dma("pool", dst[i * step:(i + 1) * step, :], src[i * step:(i + 1) * step, :],
                   writes=[self.dt_("w", l, i)], key="cvt")

    def pcol(self, name, i=0):
        return self.pv[:, PV[name] + i:PV[name] + i + 1]

    def layer(self, l):
        S = self.S
        S.dma("sp", self.pv, self.dr["pvec"][l])
        S.dma("sp", self.bv, self.dr["bvec"][l].partition_broadcast(128))
        self.derive_params(l)
        nseq = len(self.Ts)
        if "P1" in self.phases:
            self.setup_P1()
            tiles = [(s, t0) for s in range(nseq) for t0 in range(0, self.Ts[s], TT)]
            gens = [self.P1(l, s, t0) for (s, t0) in tiles]
            next(gens[0])
            for k_ in range(len(gens)):
                if k_ + 1 < len(gens):
                    next(gens[k_ + 1])
                for _ in gens[k_]:
                    pass
        for s in range(nseq):
            if "LRU" in self.phases:
                self.LRU(l, s)
            if "GATE" in self.phases:
                self.GATE(l, s)
            if "SWB" in self.phases:
                self.SWEEP(l, s, "B")
            if "SWF" in self.phases:
                self.SWEEP(l, s, "F")
        if "P2" in self.phases:
            self.setup_P2()
            tiles = [(s, t0) for s in range(nseq) for t0 in range(0, self.Ts[s], TT)]
            gens = [self.P2(l, s, t0) for (s, t0) in tiles]
            ti = len(tiles)

            def start(k_):
                if l + 1 < self.depth and k_ < 7:
                    self.cvt_piece(l + 1, k_)
                next(gens[k_])
            start(0)
            for k_ in range(len(gens)):
                if k_ + 1 < len(gens):
                    start(k_ + 1)
                for _ in gens[k_]:
                    pass
            if l + 1 < self.depth:
                for i in range(min(ti, 7), 7):
                    self.cvt_piece(l + 1, i)

    def new_phase(self):
        self.S.barrier()
        self.arena.off = 0
        for b in self.bank:
            b.w = None
            b.r = {}

    def derive_params(self, l):
        S = self.S
        dp = self.dpv
        lam = self.pv[:, PV["lru_lam"]:PV["lru_lam"] + 8]
        S.act(dp[:, 32:40], lam, AF.Exp, scale=-1.0, small=True)
        S.act(dp[:, 32:40], dp[:, 32:40], AF.Ln, bias=1.0, small=True)
        S.ts(dp[:, 0:8], dp[:, 32:40], -4.0, ALU.mult, small=True)
        S.ts(dp[:, 8:16], dp[:, 32:40], -8.0, ALU.mult, small=True)
        S.ts(dp[:, 16:32], self.pv[:, PV["lru_gb"]:PV["lru_gb"] + 16], 0.5, ALU.mult, small=True)
        S.act(dp[:, 40:41], self.pcol("ralog"), AF.Exp, small=True)
        S.ts(dp[:, 40:41], dp[:, 40:41], -1.0, ALU.mult, small=True)
        S.ts(dp[:, 41:42], self.pcol("rbB"), -1.0, ALU.mult, small=True)
        if not hasattr(self, "dpv2"):
            self.dpv2 = Tl(self.stack.enter_context(self.nc.sbuf_tensor("dpv2_sb", [128, 64], F32))[:], "dpv2")
        d2 = self.dpv2
        S.ts(d2[:, 0:24], self.pv[:, PV["xbc_w"]:PV["xbc_w"] + 24], 0.5, ALU.mult, small=True)
        S.ts(d2[:, 24:30], self.pv[:, PV["xbc_b"]:PV["xbc_b"] + 6], 0.5, ALU.mult, small=True)
        S.ts(d2[:, 30:46], self.pv[:, PV["mqk_w"]:PV["mqk_w"] + 16], 0.5, ALU.mult, small=True)
        S.ts(d2[:, 46:50], self.pv[:, PV["mqk_b"]:PV["mqk_b"] + 4], 0.5, ALU.mult, small=True)

    def wload(self, l, off, n, q="sp"):
        w = self.wring[self.wring_i % len(self.wring)]
        self.wring_i += 1
        self.S.dma(q, w[:, 0:n], self.wview(l)[:, off:off + n], reads=[self.dt_("w", l, i) for i in range(7)])
        return w

    def alloc_common(self, nw):
        A = self.arena
        self.X32s = [A.alloc([8, TT], F32, f"X32_{i}") for i in range(2)]
        self.Xbs = [A.alloc([8, TT], BF16, f"Xb_{i}") for i in range(2)]
        self.X32 = self.X32s[0]
        self.Xb = self.Xbs[0]
        self.tile_i = 0
        self.SQ = A.alloc([8, TT], BF16, "SQ")
        self.G = A.alloc([22, TT], BF16, "G")
        self.wring = [A.alloc([GE], BF16, f"W{i}") for i in range(nw)]
        self.lnt = [A.alloc([TT], F32, f"lnt{i}") for i in range(3)]
        self.sgt = [A.alloc([TT], BF16, f"sg{i}") for i in range(2)]
        self.pset = 0

    def banks4(self):
        b = self.bank[4 * self.pset:4 * self.pset + 4]
        self.pset ^= 1
        return b

    def setup_P1(self):
        self.new_phase()
        A = self.arena
        self.alloc_common(3)
        self.stg = [A.alloc([4, TT], BF16, f"stg{i}") for i in range(2)]
        self.stg32 = A.alloc([2, TT], F32, "stg32")
        self.rcos = A.alloc([TT], F32, "rcos")
        self.rsin = A.alloc([TT], F32, "rsin")
        self.tmpf = [A.alloc([TT], F32, f"tmpf{i}") for i in range(3)]
        self.xin = [A.alloc([D], F32, f"xin{i}") for i in range(2)]
        self.stg_i = 0

    def setup_P2(self):
        self.new_phase()
        A = self.arena
        self.alloc_common(2)
        self.Y = A.alloc([16, TT], BF16, "Y")
        self.M32 = A.alloc([8, TT], F32, "M32")
        self.GT = [A.alloc([8, TT], BF16, f"GT{i}") for i in range(2)]
        self.tmpf = [A.alloc([TT], F32, f"tmpf{i}") for i in range(2)]
        self.outt = [A.alloc([D], F32, f"outt{i}") for i in range(2)]

    def ffn(self, l, f):
        S = self.S
        c = 0.5 / DN_ALPHA
        for g in range(11):
            W = self.wload(l, OFF[("up", f)] + g * GE, GE)
            Wv = W.v().rearrange("p (k j) -> p k j", k=8)
            bk = self.banks4()
            S.mm_multi([(bk[cc], [(Wv[:, kc, cc * 128:(cc + 1) * 128], self.Xb[:, kc, :]) for kc in range(8)])
                        for cc in range(4)])
            for p in range(2):
                sg = self.sgt[p]
                S.act(sg, bk[p], AF.Silu)
                S.tt(self.G[:, 2 * g + p, :], sg, bk[2 + p], ALU.mult)
        for half in range(2):
            bk = self.banks4()
            for q in range(4):
                dc = half * 4 + q
                W = self.wload(l, OFF[("dn", f)] + dc * 2816, 2816)
                Wv = W[:, 0:2816].rearrange("p (k j) -> p k j", k=22)
                S.mm(bk[q], [(Wv[:, fc, :], self.G[:, fc, :]) for fc in range(22)])
                S.stt(self.X32[:, dc, :], bk[q], c, self.X32[:, dc, :], ALU.mult, ALU.add)

    def layernorm(self, i, refresh=True):
        S = self.S
        epsp = EPS / (DN_ALPHA ** 2)
        S.act(self.Xb, self.X32, AF.Copy)
        S.act(self.SQ, self.X32, AF.Square)
        bk = self.banks4()
        S.mm(bk[0], [(self.onesb, self.Xb[:, kc, :]) for kc in range(8)])
        S.mm(bk[1], [(self.onesb, self.SQ[:, kc, :]) for kc in range(8)])
        mean, t1, rstd = self.lnt
        S.act(mean, bk[0], AF.Identity, scale=1.0 / D)
        S.act(t1, bk[0], AF.Square, scale=1.0 / D)
        S.stt(t1, bk[1], 1.0 / D, t1, ALU.mult, ALU.subtract)
        S.ts(t1, t1, epsp, ALU.add)
        S.act(t1, t1, AF.Ln)
        S.act(rstd, t1, AF.Exp, scale=-0.5)
        S.tt(self.X32, self.X32, mean.v().bcast(1, 8), ALU.subtract)
        S.tt(self.X32, self.X32, rstd.v().bcast(1, 8), ALU.mult)
        for kc in range(8):
            S.act(self.X32[:, kc, :], self.X32[:, kc, :], AF.Identity,
                  scale=self.pcol("ln_g", i * 8 + kc), bias=self.pcol("ln_b", i * 8 + kc))
        if refresh:
            S.copy(self.Xb, self.X32, eng="dve")

    def load_x(self, l, g0):
        S = self.S
        if l == 0:
            for tc in range(TT // 128):
                xi = self.xin[tc % 2]
                S.dma("sp", xi, self.dr["x_in"][g0 + tc * 128:g0 + (tc + 1) * 128, :])
                bk = self.banks4()
                for hb in range(2):
                    S.transposes([(bk[hb][:, q * 128:(q + 1) * 128], xi[:, (hb * 4 + q) * 128:(hb * 4 + q + 1) * 128],
                                   self.ident) for q in range(4)])
                    S.copy(self.X32[:, hb * 4:hb * 4 + 4, tc * 128:(tc + 1) * 128],
                           bk[hb].v().rearrange("p (q t) -> p q t", q=4), eng=("act" if hb else "dve"))
        else:
            S.dma("sp", self.X32, self.dr["xT"][:, :, g0:g0 + TT].rearrange("k p t -> p k t"),
                  reads=[self.dt_("xT", g0)])

    def store_x(self, l, g0, final):
        S = self.S
        if final:
            for tc in range(TT // 128):
                ot = self.outt[tc % 2]
                bk = self.banks4()
                for hb in range(2):
                    S.transposes([(bk[hb][:, q * 128:(q + 1) * 128], self.X32[:, hb * 4 + q, tc * 128:(tc + 1) * 128],
                                   self.ident) for q in range(4)])
                    S.copy(ot[:, hb * 512:(hb + 1) * 512], bk[hb], eng=("act" if hb else "dve"))
                S.dma("pool", self.dr["y_out"][g0 + tc * 128:g0 + (tc + 1) * 128, :], ot)
        else:
            S.dma("pool", self.dr["xT"][:, :, g0:g0 + TT].rearrange("k p t -> p k t"), self.X32,
                  writes=[self.dt_("xT", g0)])

    def stage(self):
        s = self.stg[self.stg_i % 2]
        self.stg_i += 1
        return s

    def next_tile(self):
        self.X32 = self.X32s[self.tile_i % 2]
        self.Xb = self.Xbs[self.tile_i % 2]
        self.tile_i += 1

    def P1(self, l, s, t0):
        S = self.S
        g0 = self.base[s] + t0
        self.next_tile()
        X32_, Xb_ = self.X32, self.Xb
        self.load_x(l, g0)
        S.copy(self.Xb, self.X32, eng="act")
        self.ffn(l, 0)
        self.layernorm(0)
        S.dma("pool", self.dr["xT"][:, :, g0:g0 + TT].rearrange("k p t -> p k t"), self.X32,
              writes=[self.dt_("xT", g0)])
        yield
        self.X32, self.Xb = X32_, Xb_
        S.dma("pool", self.rcos, self.dr["rcos"][:, t0:t0 + TT])
        S.dma("pool", self.rsin, self.dr["rsin"][:, t0:t0 + TT])
        fmdst = {0: ("fm_xa", 0), 1: ("fm_ga", 0), 2: ("fm_xbc", 0), 4: ("fm_mqk", 0)}
        for g in range(15):
            W = self.wload(l, OFF["inF"] + g * GE, GE)
            Wv = W.v().rearrange("p (k j) -> p k j", k=8)
            bk = self.banks4()
            S.mm_multi([(bk[cc], [(Wv[:, kc, cc * 128:(cc + 1) * 128], self.Xb[:, kc, :]) for kc in range(8)])
                        for cc in range(4)])
            if g in (0, 2, 4):
                stg = self.stage()
                for cc in range(4):
                    S.copy(stg[:, cc, :], bk[cc], eng=("act" if cc % 2 else "dve"))
                nm, c0 = fmdst[g]
                S.dma("pool", self.dr[nm][c0:c0 + 4, :, g0:g0 + TT].rearrange("k p t -> p k t"), stg,
                      writes=[self.dt_(nm, s)])
            elif g == 1:
                stg = self.stage()
                for cc in range(4):
                    a, b, c_ = self.tmpf
                    S.act(a, bk[cc], AF.Square)
                    S.ts(a, a, 0.044715, ALU.mult, 1.0, ALU.add)
                    S.tt(a, a, bk[cc], ALU.mult)
                    S.act(b, a, AF.Tanh, scale=0.7978845608028654)
                    S.act(c_, bk[cc], AF.Copy, scale=0.5)
                    S.stt(stg[:, cc, :], b, 1.0, c_, ALU.add, ALU.mult)
                S.dma("pool", self.dr["fm_ga"][:, :, g0:g0 + TT].rearrange("k p t -> p k t"), stg,
                      writes=[self.dt_("fm_ga", s)])
            elif g == 3:
                stg = self.stage()
                S.copy(stg[:, 0, :], bk[0], eng="act")
                S.copy(stg[:, 1, :], bk[1], eng="dve")
                S.dma("pool", self.dr["fm_xbc"][4:6, :, g0:g0 + TT].rearrange("k p t -> p k t"), stg[:, 0:2, :],
                      writes=[self.dt_("fm_xbc", s)])
                S.copy(self.stg32[:, 0, :], bk[2], eng="act")
                S.copy(self.stg32[:, 1, :], bk[3], eng="dve")
                S.dma("pool", self.dr["fm_gr"][:, :, g0:g0 + TT].rearrange("k p t -> p k t"), self.stg32,
                      writes=[self.dt_("fm_gr", s)])
            elif g in (5, 6):
                stg = self.stage()
                for cc in range(2):
                    a, b, _ = self.tmpf
                    S.tt(a, bk[cc + 2], self.rsin, ALU.mult)
                    S.tt(b, bk[cc], self.rcos, ALU.mult)
                    S.tt(stg[:, cc, :], a, b, ALU.add)
                nm = "fm_rq" if g == 5 else "fm_rk"
                S.dma("pool", self.dr[nm][:, :, g0:g0 + TT].rearrange("k p t -> p k t"), stg[:, 0:2, :],
                      writes=[self.dt_(nm, s)])
            else:
                gi = g - 7
                stg = self.stage()
                for cc in range(4):
                    a = self.tmpf[cc % 3]
                    S.act(a, bk[cc], AF.Tanh, scale=0.5)
                    S.ts(stg[:, cc, :], a, 0.5, ALU.mult, 0.5, ALU.add)
                S.dma("pool", self.dr["fm_mix"][gi * 4:gi * 4 + 4, :, g0:g0 + TT].rearrange("k p t -> p k t"), stg,
                      writes=[self.dt_("fm_mix", s)])
        for g, nm in enumerate(("z", "rv", "rg", "mv", "mo")):
            W = self.wload(l, OFF["inT"] + g * GE, GE)
            Wv = W.v().rearrange("p (k j) -> p k j", k=8)
            bk = self.banks4()
            S.mm_multi([(bk[tc], [(self.Xb[:, kc, tc * 128:(tc + 1) * 128], Wv[:, kc, :]) for kc in range(8)])
                        for tc in range(4)])
            stg = self.stage()
            for tc in range(4):
                if nm in ("z", "rg"):
                    S.act(stg[:, tc, :], bk[tc], AF.Silu)
                elif nm == "mo":
                    a = self.tmpf[tc % 3]
                    S.act(a, bk[tc], AF.Tanh, scale=0.5)
                    S.ts(stg[:, tc, :], a, 0.5, ALU.mult, 0.5, ALU.add)
                else:
                    S.copy(stg[:, tc, :], bk[tc], eng=("act" if tc % 2 else "dve"))
            S.dma("pool", self.dr["tm_" + nm][g0:g0 + TT, :].rearrange("(c p) j -> p c j", p=128), stg,
                  writes=[self.dt_("tm_" + nm, s)])

    def P2(self, l, s, t0):
        S = self.S
        g0 = self.base[s] + t0
        self.next_tile()
        X32_, Xb_ = self.X32, self.Xb
        final = (l == self.depth - 1)
        S.dma("sp", self.Y, self.dr["fm_y"][:, :, g0:g0 + TT].rearrange("k p t -> p k t"),
              reads=[self.dt_("fm_y", s)])
        S.dma("sp", self.X32, self.dr["xT"][:, :, g0:g0 + TT].rearrange("k p t -> p k t"),
              reads=[self.dt_("xT", g0)])
        for i in range(4):
            W = self.wload(l, OFF["br"] + i * GE, GE)
            Wv = W.v().rearrange("p (k j) -> p k j", k=4)
            GT = self.GT[i % 2]
            S.dma("pool", GT, self.dr["fm_mix"][i * 8:i * 8 + 8, :, g0:g0 + TT].rearrange("k p t -> p k t"),
                  reads=[self.dt_("fm_mix", s)])
            for half in range(2):
                bk = self.banks4()
                S.mm_multi([(bk[q], [(Wv[:, kc, (half * 4 + q) * 128:(half * 4 + q + 1) * 128],
                                      self.Y[:, i * 4 + kc, :]) for kc in range(4)]) for q in range(4)])
                for q in range(4):
                    dc = half * 4 + q
                    if i == 0:
                        S.tt(self.M32[:, dc, :], bk[q], GT[:, dc, :], ALU.mult)
                    else:
                        a = self.tmpf[q % 2]
                        S.tt(a, bk[q], GT[:, dc, :], ALU.mult)
                        S.tt(self.M32[:, dc, :], self.M32[:, dc, :], a, ALU.add)
        S.act(self.SQ, self.M32, AF.Copy)
        for g in range(2):
            W = self.wload(l, OFF["out"] + g * GE, GE)
            Wv = W.v().rearrange("p (k j) -> p k j", k=8)
            bk = self.banks4()
            S.mm_multi([(bk[cc], [(Wv[:, kc, cc * 128:(cc + 1) * 128], self.SQ[:, kc, :]) for kc in range(8)])
                        for cc in range(4)])
            for cc in range(4):
                dc = g * 4 + cc
                S.stt(self.X32[:, dc, :], bk[cc], 1.0 / DN_ALPHA, self.X32[:, dc, :], ALU.mult, ALU.add)
        self.layernorm(1)
        yield
        self.X32, self.Xb = X32_, Xb_
        self.ffn(l, 1)
        self.layernorm(2, refresh=False)
        self.store_x(l, g0, final)


def _LRU(self, l, s):
    S = self.S
    self.new_phase()
    A = self.arena
    T = self.Ts[s]
    b0 = self.base[s]
    XP = A.alloc([T + 4], BF16, "XP")
    XC = A.alloc([T], F32, "XC")
    XCb = A.alloc([T], BF16, "XCb")
    A1 = A.alloc([T], F32, "A1")
    A2 = A.alloc([T], F32, "A2")
    B1 = A.alloc([T], F32, "B1")
    HF = A.alloc([T], F32, "HF")
    HB = A.alloc([T], F32, "HB")
    GA = A.alloc([T], BF16, "GA")
    YA = A.alloc([T], BF16, "YA")
    LW = A.alloc([2048], BF16, "LW")
    S.dma("sp", LW, self.wview(l)[:, OFF["lru"]:OFF["lru"] + 2048], reads=[self.dt_("w", l, i) for i in range(7)])
    LWv = LW.v().rearrange("p (m j) -> p m j", m=16)
    S.memset(XP[:, 0:2], 0.0, small=True)
    S.memset(XP[:, T + 2:T + 4], 0.0, small=True)
    nb = 0
    for ct in range(4):
        S.dma("sp", XP[:, 2:2 + T], self.dr["fm_xa"][ct, :, b0:b0 + T], reads=[self.dt_("fm_xa", s)])
        S.dma("pool", GA, self.dr["fm_ga"][ct, :, b0:b0 + T], reads=[self.dt_("fm_ga", s)])
        S.act(XC, XP[:, 0:T], AF.Identity, scale=self.pcol("xa_w", ct * 4), bias=self.pcol("xa_b", ct))
        for k in range(1, 4):
            S.stt(XC, XP[:, k:k + T], self.pcol("xa_w", ct * 4 + k), XC, ALU.mult, ALU.add)
        S.act(XCb, XC, AF.Copy)
        for d_ in range(2):
            for gate, dst in ((0, A1), (1, B1)):
                m = (d_ * 2 + gate) * 4 + ct
                hb = self.dpv[:, 16 + m:17 + m]
                for c0 in range(0, T, 512):
                    bk = self.bank[nb % 8]; nb += 1
                    S.mm(bk, [(LWv[:, m, :], XCb[:, c0:c0 + 512])])
                    S.act(dst[:, c0:c0 + 512], bk, AF.Tanh, scale=0.5, bias=hb)
            ch = self.dpv[:, d_ * 4 + ct:d_ * 4 + ct + 1]
            cf = self.dpv[:, 8 + d_ * 4 + ct:8 + d_ * 4 + ct + 1]
            S.act(A2, A1, AF.Exp, scale=cf, bias=cf)
            S.act(A1, A1, AF.Exp, scale=ch, bias=ch)
            S.act(A2, A2, AF.Sqrt, scale=-0.25, bias=0.25)
            S.stt(B1, B1, 1.0, XC, ALU.add, ALU.mult)
            S.tt(B1, B1, A2, ALU.mult)
            if d_ == 0:
                S.scan(HF, A1, B1, 0.0)
            else:
                S.scan(HB[:, ::-1], A1[:, ::-1], B1[:, ::-1], 0.0)
        S.tt(HF, HF, HB, ALU.add)
        S.tt(YA, HF, GA, ALU.mult)
        S.dma("sp", self.dr["fm_y"][ct, :, b0:b0 + T], YA, writes=[self.dt_("fm_y", s)])


def _GATE(self, l, s):
    S = self.S
    self.new_phase()
    A = self.arena
    T = self.Ts[s]
    b0 = self.base[s]
    nch = T // CH
    GA = A.alloc([T], F32, "gGA")
    GB = A.alloc([T], F32, "gGB")
    U = [A.alloc([T], F32, f"gU{i}") for i in range(5)]
    SM = A.alloc([8, 32], F32, "gSM")
    R1 = A.alloc([nch * 24], F32, "gR1")
    STG = [A.alloc([512], F32, f"gST{i}") for i in range(2)]
    ones = self.cst[:, CST["ones"]:CST["ones"] + 128]
    S.dma("sp", GA, self.dr["fm_gr"][0, :, b0:b0 + T], reads=[self.dt_("fm_gr", s)])
    S.dma("sp", GB, self.dr["fm_gr"][1, :, b0:b0 + T], reads=[self.dt_("fm_gr", s)])
    DT, LNDT, CUM, TA, TB = U
    lo = slice(0, 64)
    hi = slice(64, 128)
    S.act(DT[lo], GA[lo], AF.Exp, bias=self.pv[lo, PV["rbA"]:PV["rbA"] + 1])
    S.act(DT[lo], DT[lo], AF.Ln, bias=1.0)
    S.act(LNDT[lo], DT[lo], AF.Ln)
    S.ts(DT[lo], DT[lo], self.dpv[lo, 40:41], ALU.mult)
    for c in range(nch):
        cs = slice(c * CH, (c + 1) * CH)
        S.scan(CUM[0:32, cs], ones[0:32, :], DT[0:32, cs], 0.0)
        S.scan(CUM[32:64, cs][:, ::-1], ones[32:64, :], DT[32:64, cs][:, ::-1], 0.0)
    S.tt(LNDT[lo], LNDT[lo], CUM[lo], ALU.subtract)
    S.act(TA[lo], CUM[lo], AF.Exp)
    c3 = CUM.v().rearrange("p (c t) -> p c t", t=CH)
    S.copy(SM[0:32, 0, 0:nch], c3[0:32, :, CH - 1], small=True)
    S.copy(SM[32:64, 0, 0:nch], c3[32:64, :, 0], small=True)
    S.act(SM[lo, 1, 0:nch], SM[lo, 0, 0:nch], AF.Exp, small=True)
    S.tt(TB.v().rearrange("p (c t) -> p c t", t=CH)[lo], LNDT.v().rearrange("p (c t) -> p c t", t=CH)[lo],
         SM[lo, 0, 0:nch].bcast(2, CH), ALU.add)
    S.act(TB[lo], TB[lo], AF.Exp)
    S.dma("sp", self.dr["gq"][0, 0:64, b0:b0 + T], CUM[lo], writes=[self.dt_("gq", s)])
    S.dma("sp", self.dr["gq"][1, 0:64, b0:b0 + T], LNDT[lo], writes=[self.dt_("gq", s)])
    S.act(DT[hi], GB[hi], AF.Exp, scale=-1.0, bias=self.dpv[hi, 41:42])
    S.act(DT[hi], DT[hi], AF.Ln, bias=1.0)
    NB = GB
    o1f = self.cst[64:96, CST["ones"]:CST["ones"] + 1]
    o1b = self.cst[96:128, CST["ones"]:CST["ones"] + 1]
    S.scan(NB[64:96, :], V(o1f.ap.broadcast_to([32, T]), o1f.tl), DT[64:96, :], 0.0)
    S.scan(NB[96:128, :][:, ::-1], V(o1b.ap.broadcast_to([32, T]), o1b.tl), DT[96:128, :][:, ::-1], 0.0)
    UG = LNDT
    S.ts(UG[hi], GA[hi], self.pv[hi, PV["rbA"]:PV["rbA"] + 1], ALU.add)
    S.tt(UG[hi], UG[hi], NB[hi], ALU.add)
    S.reduce(SM[hi, 2, 0:nch], UG.v().rearrange("p (c t) -> p c t", t=CH)[hi], ALU.max, small=True)
    S.scan(SM[64:96, 3, 0:nch], SM[64:96, 2, 0:nch], SM[64:96, 2, 0:nch], 0.0, op0=ALU.max, op1=ALU.max)
    S.scan(SM[96:128, 3, 0:nch][:, ::-1], SM[96:128, 2, 0:nch][:, ::-1], SM[96:128, 2, 0:nch][:, ::-1], 0.0,
           op0=ALU.max, op1=ALU.max)
    S.memset(SM[hi, 4, :], 0.0, small=True)
    if nch > 1:
        S.copy(SM[64:96, 4, 1:nch], SM[64:96, 3, 0:nch - 1], small=True)
        S.copy(SM[96:128, 4, 0:nch - 1], SM[96:128, 3, 1:nch], small=True)
    S.tt(SM[hi, 5, 0:nch], SM[hi, 4, 0:nch], SM[hi, 3, 0:nch], ALU.subtract, small=True)
    S.act(SM[hi, 5, 0:nch], SM[hi, 5, 0:nch], AF.Exp, small=True)
    mgb = SM[hi, 3, 0:nch].bcast(2, CH)
    S.tt(TA.v().rearrange("p (c t) -> p c t", t=CH)[hi], UG.v().rearrange("p (c t) -> p c t", t=CH)[hi], mgb,
         ALU.subtract)
    S.act(TA[hi], TA[hi], AF.Exp)
    S.tt(TB.v().rearrange("p (c t) -> p c t", t=CH)[hi], NB.v().rearrange("p (c t) -> p c t", t=CH)[hi], mgb,
         ALU.subtract)
    S.act(TB[hi], TB[hi], AF.Exp, bias=self.lnb8[hi])
    selT = self.cst[:, CST["selT"]:CST["selT"] + 24]
    r1 = R1.v().rearrange("p (c n) -> p c n", n=24)
    S.tt(r1[lo, :, 0:16], SM[lo, 1, 0:nch].bcast(2, 16), selT[lo, 0:16].bcast(1, nch), ALU.mult)
    S.tt(r1[hi, :, 16:24], SM[hi, 5, 0:nch].bcast(2, 8), selT[hi, 16:24].bcast(1, nch), ALU.mult)
    bk = self.bank[0]
    o128 = self.cst[:, CST["ones"]:CST["ones"] + 128]
    S.memset(r1[lo, :, 16:24], 0.0)
    S.memset(r1[hi, :, 0:16], 0.0)
    for c0 in range(0, nch, 16):
        c1 = min(nch, c0 + 16)
        bkx = self.bank[(c0 // 16) % 2 * 5]
        S.mm(bkx[:, 0:(c1 - c0) * 24], [(o128, R1[:, c0 * 24:c1 * 24])])
        S.copy(self.SC[:, c0:c1, :], bkx.v()[:, 0:(c1 - c0) * 24].rearrange("p (c n) -> p c n", n=24))
    for c in range(nch):
        bk = self.bank[1 + c % 4]
        cs = slice(c * CH, (c + 1) * CH)
        S.transposes([(bk[:, 0:128], TA[:, cs], self.ident), (bk[:, 128:256], TB[:, cs], self.ident)])
        st = STG[c % 2]
        S.copy(st[:, 0:256], bk[:, 0:256], eng=("act" if c % 2 else "dve"))
        S.dma("sp", self.dr["tmg"][b0 + c * CH:b0 + (c + 1) * CH, :], st[:, 0:256], writes=[self.dt_("tmg", s)])


Prog.LRU = _LRU
Prog.GATE = _GATE


SEG = 512


def _nbk(self):
    b = self.bank[self._bki % 8]
    self._bki += 1
    return b


def _conv_silu(self, dst, XP, XH, TH, wcol0, bcol, n):
    S = self.S
    d2 = self.dpv2
    S.act(XH, XP[:, 0:n], AF.Identity, scale=d2[:, wcol0:wcol0 + 1], bias=d2[:, bcol:bcol + 1])
    for k in range(1, 4):
        S.stt(XH, XP[:, k:k + n], d2[:, wcol0 + k:wcol0 + k + 1], XH, ALU.mult, ALU.add)
    S.act(TH, XH, AF.Tanh)
    S.stt(dst, TH, 1.0, XH, ALU.add, ALU.mult)


def _load_halo(self, XP, name, ct, s, t0, T):
    S = self.S
    b0 = self.base[s]
    lo = max(t0 - 2, 0)
    hi = min(t0 + SEG + 1, T)
    if t0 == 0:
        S.memset(XP[:, 0:2], 0.0, small=True)
    if t0 + SEG == T:
        S.memset(XP[:, SEG + 2:SEG + 4], 0.0, small=True)
    S.dma("sp", XP[:, lo - (t0 - 2):hi - (t0 - 2)], self.dr[name][ct, :, b0 + lo:b0 + hi], reads=[self.dt_(name, s)])


def _headnorm(self, Y, STATS, MV2, TMP4, nwi, gate, OUTb):
    S = self.S
    for h in range(4):
        S.op("dve", lambda e, h=h: e.bn_stats(out=_ap(STATS[:, h, :]), in_=_ap(Y[:, h, :])),
             reads=[Y.tl], writes=[STATS.tl], small=True)
    for h in range(4):
        S.op("dve", lambda e, h=h: e.bn_aggr(out=_ap(MV2[:, h, :]), in_=_ap(STATS[:, h, :])),
             reads=[STATS.tl], writes=[MV2.tl], small=True)
    S.act(TMP4, MV2[:, :, 1], AF.Ln, bias=self.epsb, small=True)
    S.act(TMP4, TMP4, AF.Exp, scale=-0.5, small=True)
    S.stt(MV2[:, :, 1], MV2[:, :, 0], -1.0, TMP4, ALU.mult, ALU.mult, small=True)
    for h in range(4):
        S.act(Y[:, h, :], Y[:, h, :], AF.Identity, scale=TMP4[:, h:h + 1], bias=MV2[:, h, 1:2])
    Yf = Y.rearrange("p h v -> p (h v)")
    S.tt(Yf, Yf, self.bv[:, nwi, :], ALU.mult)
    S.tt(OUTb, Yf, gate, ALU.mult)


def _SWEEP(self, l, s, sw):
    S = self.S
    self.new_phase()
    A = self.arena
    T = self.Ts[s]
    b0 = self.base[s]
    nseg = T // SEG
    F = (sw == "F")
    dn = 0 if F else 1
    self._bki = 0
    HS = A.alloc([256], F32, "HS"); HR = A.alloc([2, 128], F32, "HR"); HM = A.alloc([2, 132], F32, "HM")
    for t_ in (HS, HR, HM):
        S.memset(t_, 0.0)
    if F:
        HSx = [A.alloc([2, 256], BF16, f"HSx{i}") for i in range(2)]
        HRx = [A.alloc([2, 2, 128], BF16, f"HRx{i}") for i in range(2)]
        HMx = [A.alloc([2, 2, 132], BF16, f"HMx{i}") for i in range(2)]
        HSxb = [A.alloc([2, 256], BF16, f"HSxb{i}") for i in range(2)]
        HRxb = [A.alloc([2, 2, 128], BF16, f"HRxb{i}") for i in range(2)]
        HMxb = [A.alloc([2, 2, 132], BF16, f"HMxb{i}") for i in range(2)]
        for t_ in HSx + HRx + HMx + HSxb + HRxb + HMxb:
            S.memset(t_, 0.0)
    else:
        HSb = [A.alloc([256], BF16, f"HSb{i}") for i in range(2)]
        HRb = [A.alloc([2, 128], BF16, f"HRb{i}") for i in range(2)]
        HMb = [A.alloc([2, 132], BF16, f"HMb{i}") for i in range(2)]
        for t_ in HSb + HRb + HMb:
            S.memset(t_, 0.0)
    XP = [A.alloc([SEG + 4], BF16, f"sXP{i}") for i in range(2)]
    XH = A.alloc([SEG], F32, "sXH"); TH = A.alloc([SEG], F32, "sTH")
    XBC = A.alloc([6, SEG], BF16, "sXBC"); MQK = A.alloc([4, SEG], BF16, "sMQK")
    RK = A.alloc([2, SEG], BF16, "sRK")
    TMG = A.alloc([4, 256], F32, "sTMG")
    RV = A.alloc([4, 512], BF16, "sRV"); MV = A.alloc([4, 512], BF16, "sMV")
    TMA_ = [A.alloc([896], BF16, f"sTMA{i}") for i in range(2)]; TMB_ = [A.alloc([256], BF16, f"sTMB{i}") for i in range(2)]
    WV_ = [A.alloc([512], BF16, f"sWV{i}") for i in range(2)]; WVR_ = [A.alloc([512], BF16, f"sWVR{i}") for i in range(2)]
    EV_ = [A.alloc([4, 132], BF16, f"sEV{i}") for i in range(2)]
    if F:
        RQ = A.alloc([2, SEG], BF16, "sRQ")
        CMX = A.alloc([2, SEG], BF16, "sCMX"); MQX = A.alloc([2, 2, SEG], BF16, "sMQX"); RQX = A.alloc([2, 2, SEG], BF16, "sRQX")
        for t_ in (CMX, MQX, RQX):
            S.memset(t_, 0.0)
        CUM = A.alloc([SEG], F32, "sCUM"); CB = A.alloc([SEG], F32, "sCB")
        S.memset(CUM, 0.0); S.memset(CB, 0.0)
        ZG = A.alloc([4, 512], BF16, "sZ"); RG = A.alloc([4, 512], BF16, "sRG"); MO = A.alloc([4, 512], BF16, "sMO")
        WE = [A.alloc([512], F32, f"sWE{i}") for i in range(2)]
        PS8 = [A.alloc([512], BF16, f"sPS{i}") for i in range(8)]
        PR_ = [A.alloc([512], BF16, f"sPR{i}") for i in range(2)]; PM_ = [A.alloc([8, 128], BF16, f"sPM{i}") for i in range(2)]
        Y1_ = [A.alloc([512], F32, f"sY1{i}") for i in range(2)]; Y2_ = [A.alloc([512], F32, f"sY2{i}") for i in range(2)]
        Y3_ = [A.alloc([512], F32, f"sY3{i}") for i in range(2)]; Y4_ = [A.alloc([512], F32, f"sY4{i}") for i in range(2)]
        Y5_ = [A.alloc([512], F32, f"sY5{i}") for i in range(2)]
        OB_ = [[A.alloc([512], BF16, f"sOB{i}{k}") for i in range(3)] for k in range(2)]
        YT_ = [[A.alloc([4, SEG], BF16, f"sYT{i}") for i in range(3)]] * 2
        ST_ = [[(A.alloc([4, 6], F32, f"sSTATS{m}{k}"), A.alloc([4, 2], F32, f"sMV2{m}{k}"), A.alloc([8], F32, f"sTMP4{m}{k}"))
                for m in range(3)] for k in range(2)]
        DEN_ = [A.alloc([8], F32, f"sDEN{i}") for i in range(2)]
    cst = self.cst
    ident = self.ident
    identb = self.identb
    onesb = self.onesb
    lo = slice(0, 64); hi = slice(64, 128)
    segs = range(nseg) if F else range(nseg - 1, -1, -1)
    xpi = 0
    tmv = lambda nm: self.dr["tm_" + nm][g0:g0 + SEG, :].rearrange("(c p) j -> p c j", p=128)
    for sg in segs:
        t0 = sg * SEG
        g0 = b0 + t0
        for ct in (range(6) if F else range(5)):
            xp = XP[xpi % 2]; xpi += 1
            _load_halo(self, xp, "fm_xbc", ct, s, t0, T)
            _conv_silu(self, XBC[:, ct, :], xp, XH, TH, ct * 4, 24 + ct, SEG)
        for ct in (range(4) if F else (2, 3)):
            xp = XP[xpi % 2]; xpi += 1
            _load_halo(self, xp, "fm_mqk", ct, s, t0, T)
            _conv_silu(self, MQK[:, ct, :], xp, XH, TH, 30 + ct * 4, 46 + ct, SEG)
        S.dma("pool", RK, self.dr["fm_rk"][:, :, g0:g0 + SEG].rearrange("k p t -> p k t"), reads=[self.dt_("fm_rk", s)])
        S.dma("pool", TMG, self.dr["tmg"][g0:g0 + SEG, :].rearrange("(c p) j -> p c j", p=128), reads=[self.dt_("tmg", s)])
        S.dma("pool", RV, tmv("rv"), reads=[self.dt_("tm_rv", s)])
        S.dma("pool", MV, tmv("mv"), reads=[self.dt_("tm_mv", s)])
        if F:
            S.dma("pool", RQ, self.dr["fm_rq"][:, :, g0:g0 + SEG].rearrange("k p t -> p k t"), reads=[self.dt_("fm_rq", s)])
            S.dma("sp", RQX[lo, :, 0, :], self.dr["fm_rq"][:, 0:64, g0:g0 + SEG].rearrange("k p t -> p k t"), reads=[self.dt_("fm_rq", s)])
            S.dma("sp", RQX[hi, :, 1, :], self.dr["fm_rq"][:, 64:128, g0:g0 + SEG].rearrange("k p t -> p k t"), reads=[self.dt_("fm_rq", s)])
            S.copy(CMX[lo, 0, :], XBC[lo, 5, :], eng="act")
            S.copy(CMX[hi, 1, :], XBC[hi, 5, :], eng="act")
            S.copy(MQX[lo, :, 0, :], MQK[lo, 0:2, :], eng="act")
            S.copy(MQX[hi, :, 1, :], MQK[hi, 0:2, :], eng="act")
            S.dma("sp", CUM[lo], self.dr["gq"][0, 0:64, g0:g0 + SEG], reads=[self.dt_("gq", s)])
            S.dma("sp", CB[lo], self.dr["gq"][1, 0:64, g0:g0 + SEG], reads=[self.dt_("gq", s)])
            for tl_, nm in ((ZG, "z"), (RG, "rg"), (MO, "mo")):
                S.dma("pool", tl_, tmv(nm), reads=[self.dt_("tm_" + nm, s)])
        cls = range(SEG // CH) if F else range(SEG // CH - 1, -1, -1)
        def chunk(cl):
            cs = slice(cl * CH, (cl + 1) * CH)
            lc = t0 // CH + cl
            gc = g0 // CH + cl
            par = lc % 2
            tmg = TMG[:, cl, :]
            TMA = TMA_[par]; TMB = TMB_[par]; WV = WV_[par]; WVR = WVR_[par]; EV = EV_[par]
            if F:
                PS_ = PS8[4 * par:4 * par + 4]; PR = PR_[par]; PM = PM_[par]
                Y1 = Y1_[par]; Y2 = Y2_[par]; Y3 = Y3_[par]; Y4 = Y4_[par]; Y5 = Y5_[par]
                OB = OB_[par]; YT = YT_[sg % 2]; DEN = DEN_[par]
            bT = _nbk(self); bT2 = _nbk(self)
            bTb = bT.v().bitcast(BF16); bT2b = bT2.v().bitcast(BF16)
            items = [(bTb[:, ct * 128:(ct + 1) * 128], XBC[:, ct, cs], identb) for ct in range(5)]
            items += [(bTb[:, 640:768], MQK[:, 2, cs], identb), (bTb[:, 768:896], MQK[:, 3, cs], identb)]
            S.transposes(items)
            S.transposes([(bT2b[:, 0:128], RK[:, 0, cs], identb), (bT2b[:, 128:256], RK[:, 1, cs], identb)])
            S.copy(TMA, bTb[:, 0:896], eng="act")
            S.copy(TMB, bT2b[:, 0:256], eng="act")
            Vs = TMA[:, 0:512]; Bm = TMA[:, 512:640]; MKt = TMA[:, 640:896]; RKt = TMB
            S.tt(WV.v().rearrange("p (h v) -> p h v", h=8), Vs.rearrange("p (h v) -> p h v", h=8),
                 tmg[:, 128 + 32 * dn:128 + 32 * dn + 8].bcast(2, 64), ALU.mult)
            S.tt(WVR.v().rearrange("p (h v) -> p h v", h=4), RV[:, cl, :].rearrange("p (h v) -> p h v", h=4),
                 cst[:, CST["retS"] + 8 + 4 * dn:CST["retS"] + 12 + 4 * dn].bcast(2, 128), ALU.mult)
            S.tt(EV[:, :, 0:128], MV[:, cl, :].rearrange("p (h v) -> p h v", h=4),
                 tmg[:, 64 + 32 * dn:64 + 32 * dn + 4].bcast(2, 128), ALU.mult)
            S.copy(EV[:, :, 128], tmg[:, 64 + 32 * dn:64 + 32 * dn + 4], small=True)
            if F:
                bG = _nbk(self)
                S.mm(bG[:, 0:256], [(XBC[:, 4, cs], CMX[:, :, cs])])
                pidx = 0
                Pm = {}
                for d_ in range(2):
                    neg = cst[:, CST["negF"]:CST["negF"] + 128] if d_ == 0 else cst[:, CST["negB"]:CST["negB"] + 128]
                    for g in range(2):
                        bE = _nbk(self)
                        grp = []
                        for q in range(4):
                            m = d_ * 8 + g * 4 + q
                            sel = cst[:, CST["sel"] + m * 128:CST["sel"] + (m + 1) * 128]
                            grp.append((bE[:, q * 128:(q + 1) * 128],
                                        [(sel, CUM[:, cs]), (CB[:, cs], sel), (ident, neg)]))
                        S.mm_multi(grp)
                        we = WE[pidx % 2]
                        S.act(we, bE, AF.Exp)
                        pt = PS_[pidx % 4]; pidx += 1
                        S.tt(pt.v().rearrange("p (q i) -> p q i", q=4), we.v().rearrange("p (q i) -> p q i", q=4),
                             bG[:, g * 128:(g + 1) * 128].bcast(1, 4), ALU.mult)
                        Pm[(d_, g)] = pt
                bG2 = _nbk(self)
                S.mm_multi([(bG2[:, ct * 256:(ct + 1) * 256], [(RK[:, ct, cs], RQX[:, ct, :, cs])]) for ct in range(2)])
                S.tt(PR, bG2, cst[:, CST["retW"]:CST["retW"] + 512], ALU.mult)
                bG3 = _nbk(self)
                S.mm_multi([(bG3[:, ct * 256:(ct + 1) * 256], [(MQK[:, 2 + ct, cs], MQX[:, ct, :, cs])]) for ct in range(2)])
                for d_ in range(2):
                    msk = cst[:, CST["triU"]:CST["triU"] + 128] if d_ == 0 else cst[:, CST["triL"]:CST["triL"] + 128]
                    for h in range(4):
                        S.stt(PM[:, d_ * 4 + h, :], bG3[:, h * 128:(h + 1) * 128], tmg[:, 64 + 32 * d_ + h:64 + 32 * d_ + h + 1],
                              msk, ALU.mult, ALU.mult)
            yield
            for h in range(4):
                r = slice((h % 2) * 64, (h % 2) * 64 + 64)
                S.act(HM[r, h // 2, :], HM[r, h // 2, :], AF.Copy if False else AF.Identity,
                      scale=self.SC[r, lc, 16 + dn * 4 + h:16 + dn * 4 + h + 1])
            if not F:
                S.copy(HMb[par], HM, eng="act")
                S.dma("sp", self.dr["st_ssd"][gc], HSb[1 - par], writes=[self.dt_("st_ssd", s)])
                S.dma("sp", self.dr["st_ret"][gc], HRb[1 - par].v().rearrange("p a b -> p (a b)"), writes=[self.dt_("st_ret", s)])
                S.dma("sp", self.dr["st_ml"][gc], HMb[par].v().rearrange("p a b -> p (a b)"), writes=[self.dt_("st_ml", s)])
            else:
                hsf = HSx[par]; hrf = HRx[par]; hmf = HMx[par]
                hsb = HSxb[par]; hrb = HRxb[par]; hmb = HMxb[par]
                S.copy(hsf[lo, 0, :], HS[lo], eng="act"); S.copy(hsf[hi, 1, :], HS[hi], eng="act")
                S.copy(hrf[lo, :, 0, :], HR[lo], eng="act"); S.copy(hrf[hi, :, 1, :], HR[hi], eng="act")
                S.copy(hmf[lo, :, 0, :], HM[lo], eng="act"); S.copy(hmf[hi, :, 1, :], HM[hi], eng="act")
                S.dma("pool", hsb[lo, 0, :], self.dr["st_ssd"][gc, 0:64, :], reads=[self.dt_("st_ssd", s)])
                S.dma("pool", hsb[hi, 1, :], self.dr["st_ssd"][gc, 64:128, :], reads=[self.dt_("st_ssd", s)])
                S.dma("pool", hrb[lo, :, 0, :], self.dr["st_ret"][gc, 0:64, :].rearrange("p (c v) -> p c v", c=2), reads=[self.dt_("st_ret", s)])
                S.dma("pool", hrb[hi, :, 1, :], self.dr["st_ret"][gc, 64:128, :].rearrange("p (c v) -> p c v", c=2), reads=[self.dt_("st_ret", s)])
                S.dma("pool", hmb[lo, :, 0, :], self.dr["st_ml"][gc, 0:64, :].rearrange("p (c v) -> p c v", c=2), reads=[self.dt_("st_ml", s)])
                S.dma("pool", hmb[hi, :, 1, :], self.dr["st_ml"][gc, 64:128, :].rearrange("p (c v) -> p c v", c=2), reads=[self.dt_("st_ml", s)])
                bY = _nbk(self)
                S.mm_multi([(bY[:, h * 64:(h + 1) * 64],
                             [(Pm[(0, h // 4)][:, (h % 4) * 128:(h % 4 + 1) * 128], Vs[:, h * 64:(h + 1) * 64]),
                              (Pm[(1, h // 4)][:, (h % 4) * 128:(h % 4 + 1) * 128], Vs[:, h * 64:(h + 1) * 64])])
                            for h in range(8)])
                bIf = _nbk(self); bIb = _nbk(self)
                S.mm(bIf, [(XBC[:, 5, cs], hsf.v().rearrange("p a b -> p (a b)"))])
                S.mm(bIb, [(XBC[:, 5, cs], hsb.v().rearrange("p a b -> p (a b)"))])
                y3 = lambda t_: t_.v().rearrange("p (h v) -> p h v", h=8)
                S.tt(y3(Y1), bIf.v().rearrange("p (h v) -> p h v", h=8), tmg[:, 0:8].bcast(2, 64), ALU.mult)
                S.tt(y3(Y2), bIb.v().rearrange("p (h v) -> p h v", h=8), tmg[:, 32:40].bcast(2, 64), ALU.mult)
                S.tt(Y1, Y1, Y2, ALU.add)
                S.tt(Y1, Y1, bY, ALU.add)
                S.tt(Y2, Vs, self.bv[:, 3, :], ALU.mult)
                S.tt(Y1, Y1, Y2, ALU.add)
                S.tt(Y1, Y1, ZG[:, cl, :], ALU.mult)
                STATS, MV2, TMP4 = ST_[par][0]
                S.op("dve", lambda e: e.bn_stats(out=_ap(STATS[:, 0, :]), in_=_ap(Y1)), reads=[Y1], writes=[STATS], small=True)
                S.op("dve", lambda e: e.bn_aggr(out=_ap(MV2[:, 0, :]), in_=_ap(STATS[:, 0, :])), reads=[STATS], writes=[MV2], small=True)
                S.tt(TMP4[:, 0:1], MV2[:, 0, 0:1], MV2[:, 0, 0:1], ALU.mult, small=True)
                S.tt(TMP4[:, 0:1], TMP4[:, 0:1], MV2[:, 0, 1:2], ALU.add, small=True)
                S.act(TMP4[:, 0:1], TMP4[:, 0:1], AF.Ln, bias=self.epsb, small=True)
                S.act(TMP4[:, 0:1], TMP4[:, 0:1], AF.Exp, scale=-0.5, small=True)
                S.stt(OB[0], Y1, TMP4[:, 0:1], self.bv[:, 0, :], ALU.mult, ALU.mult)
                bY2 = _nbk(self)
                S.mm_multi([(bY2[:, h * 128:(h + 1) * 128], [(PR[:, h * 128:(h + 1) * 128], RV[:, cl, h * 128:(h + 1) * 128])])
                            for h in range(4)])
                bI2f = _nbk(self); bI2b = _nbk(self)
                for bI, hh in ((bI2f, hrf), (bI2b, hrb)):
                    S.mm_multi([(bI[:, ct * 256:(ct + 1) * 256], [(RQ[:, ct, cs], hh[:, ct, :, :])]) for ct in range(2)])
                y4 = lambda t_: t_.v().rearrange("p (h v) -> p h v", h=4)
                S.tt(y4(Y2), bI2f.v().rearrange("p (h v) -> p h v", h=4), cst[:, CST["retS"]:CST["retS"] + 4].bcast(2, 128), ALU.mult)
                S.tt(y4(Y3), bI2b.v().rearrange("p (h v) -> p h v", h=4), cst[:, CST["retS"] + 4:CST["retS"] + 8].bcast(2, 128), ALU.mult)
                S.tt(Y2, Y2, Y3, ALU.add)
                S.tt(Y2, Y2, bY2, ALU.add)
                STATS, MV2, TMP4 = ST_[par][1]
                _headnorm(self, y4(Y2), STATS, MV2, TMP4[:, 0:4], 1, RG[:, cl, :], OB[1])
                bN = [_nbk(self), _nbk(self)]
                bD = _nbk(self)
                for d_, hm in ((0, hmf), (1, hmb)):
                    raw = []
                    for ct in range(2):
                        raw.append((bN[d_][:, ct * 256:(ct + 1) * 256], MQK[:, ct, cs], hm[:, ct, :, 0:128], True, False))
                        for hl in range(2):
                            h = ct * 2 + hl
                            raw.append((bN[d_][:, h * 128:(h + 1) * 128], PM[:, d_ * 4 + h, :], MV[:, cl, h * 128:(h + 1) * 128],
                                        False, hl == 1))
                    S.mm_raw(raw)
                raw = []
                for d_, hm in ((0, hmf), (1, hmb)):
                    for ct in range(2):
                        raw.append((bD[:, d_ * 4 + ct * 2:d_ * 4 + ct * 2 + 2], MQK[:, ct, cs], hm[:, ct, :, 128], True, False))
                        for hl in range(2):
                            h = ct * 2 + hl
                            raw.append((bD[:, d_ * 4 + h:d_ * 4 + h + 1], PM[:, d_ * 4 + h, :], onesb[:, 0:1], False, hl == 1))
                S.mm_raw(raw)
                S.act(DEN, bD[:, 0:8], AF.Abs, small=True)
                clampv = tmg[:, 192:256].rearrange("p (d x) -> p d x", d=2)[:, :, 0:4]
                S.tt(DEN.v().rearrange("p (d x) -> p d x", d=2), DEN.v().rearrange("p (d x) -> p d x", d=2), clampv, ALU.max, small=True)
                S.recip(DEN, DEN, small=True)
                S.tt(y4(Y4), bN[0].v().rearrange("p (h v) -> p h v", h=4), DEN[:, 0:4].bcast(2, 128), ALU.mult)
                S.tt(y4(Y5), bN[1].v().rearrange("p (h v) -> p h v", h=4), DEN[:, 4:8].bcast(2, 128), ALU.mult)
                S.tt(Y4, Y4, Y5, ALU.add)
                STATS, MV2, TMP4 = ST_[par][2]
                _headnorm(self, y4(Y4), STATS, MV2, TMP4[:, 0:4], 2, MO[:, cl, :], OB[2])
                for bi in range(3):
                    bO = _nbk(self)
                    bOb = bO.v().bitcast(BF16)
                    S.transposes([(bOb[:, k * 128:(k + 1) * 128], OB[bi][:, k * 128:(k + 1) * 128], identb) for k in range(4)])
                    S.copy(YT[bi][:, :, cs], bOb[:, 0:512].rearrange("p (k t) -> p k t", k=4), eng="act")
            bS = _nbk(self)
            S.mm(bS, [(Bm, WV)])
            for g in range(2):
                r = slice(g * 64, (g + 1) * 64)
                S.tt(HS[r].rearrange("p (h v) -> p h v", h=4), HS[r].rearrange("p (h v) -> p h v", h=4),
                     self.SC[r, lc, dn * 8 + g * 4:dn * 8 + g * 4 + 4].bcast(2, 64), ALU.mult)
                S.tt(HS[r], HS[r], bS[r, g * 256:(g + 1) * 256], ALU.add)
            bS2 = _nbk(self)
            S.mm_multi([(bS2[:, ct * 256:(ct + 1) * 256], [(RKt[:, ct * 128:(ct + 1) * 128], WVR[:, ct * 256:(ct + 1) * 256])])
                        for ct in range(2)])
            S.tt(HR, HR, cst[:, CST["retG"]:CST["retG"] + 256].rearrange("p (c v) -> p c v", c=2), ALU.mult)
            b2v = bS2.v().rearrange("p (c a v) -> p c a v", c=2, a=2)
            S.tt(HR[lo], HR[lo], b2v[lo, :, 0, :], ALU.add)
            S.tt(HR[hi], HR[hi], b2v[hi, :, 1, :], ALU.add)
            for ct in range(2):
                bS3 = _nbk(self)
                S.mm(bS3[:, 0:264], [(MKt[:, ct * 128:(ct + 1) * 128], EV[:, 2 * ct:2 * ct + 2, :])])
                S.tt(HM[lo, ct, 0:129], HM[lo, ct, 0:129], bS3[lo, 0:129], ALU.add)
                S.tt(HM[hi, ct, 0:129], HM[hi, ct, 0:129], bS3[hi, 132:261], ALU.add)
            if not F:
                S.copy(HSb[par], HS, eng="act")
                S.copy(HRb[par], HR, eng="act")
        order = list(cls)
        gens = [chunk(c_) for c_ in order]
        next(gens[0])
        for k_ in range(len(order)):
            if k_ + 1 < len(order):
                next(gens[k_ + 1])
            for _ in gens[k_]:
                pass
        if F:
            YT = YT_[sg % 2]
            for bi in range(3):
                S.dma("pool", self.dr["fm_y"][4 + bi * 4:8 + bi * 4, :, g0:g0 + SEG].rearrange("k p t -> p k t"), YT[bi],
                      writes=[self.dt_("fm_y", s)])


Prog.SWEEP = _SWEEP


_SEQS = [2048, 4096, 4096]


def kernel(**inputs):
    inp = {k: np.asarray(v) for k, v in inputs.items()}
    prog = Prog(_SEQS, DEPTH)
    nc = prog.build()
    wbig = [pack_layer_weights(inp, l).reshape(-1, 2048) for l in range(DEPTH)]
    pvec = np.stack([pack_pvec(inp, l) for l in range(DEPTH)])
    bvec = np.stack([pack_bvec(inp, l) for l in range(DEPTH)])
    cst, rc, rs = build_consts(max(_SEQS))
    xp, xs = inp["x_prompt"], inp["x_sample"]
    in_maps = []
    for c in range(NCORES):
        x_in = np.concatenate([xp[c], xs[2 * c], xs[2 * c + 1]], axis=0)
        m = {"x_in": np.ascontiguousarray(x_in), "pvec": pvec, "bvec": bvec, "cst": cst, "rcos": rc, "rsin": rs}
        for l in range(DEPTH):
            m[f"wbig{l}"] = wbig[l]
        in_maps.append(m)
    res = run_bass_kernel_spmd(nc, in_maps, core_ids=list(range(NCORES)))
    yp = np.empty_like(xp)
    ys = np.empty_like(xs)
    for c in range(NCORES):
        y = np.asarray(res.results[c]["y_out"])
        yp[c] = y[0:2048]
        ys[2 * c] = y[2048:6144]
        ys[2 * c + 1] = y[6144:10240]
    return yp, ys
```
